# Optimizing a Trainium2 kernel written in Bass

```python
import jax, jax.numpy as jnp
from jax import lax
import numpy as np

D_MODEL = 1024
BATCH = 4
SEQ = 4096
DEPTH = 1
DEC_BATCH = 16
DEC_SEQ = 4096
PAST_LEN = 128

CHUNK = 128
A_WIDTH = D_MODEL // 2
A_HEADS = 4
A_HEAD_DIM = A_WIDTH // A_HEADS
B_WIDTH = D_MODEL - A_WIDTH
B_GROUPS = 4
B_GROUP_DIM = B_WIDTH // B_GROUPS
MIX_WIDTH = A_WIDTH + B_WIDTH
CONV_WIDTH = 31
CONV_PAD = CONV_WIDTH // 2
D_FF = -(-8 * D_MODEL // (3 * 256)) * 256
EPS = 1e-6

kernel_name = "hybrid_gmlp_conformer_encoder"


def rms_norm(x, g):
    xf = x.astype(jnp.float32)
    y = xf * lax.rsqrt(jnp.mean(xf * xf, axis=-1, keepdims=True) + EPS)
    return (y * g.astype(jnp.float32)).astype(x.dtype)


def group_layer_norm(x, g, b, n_groups):
    shp = x.shape
    xf = x.astype(jnp.float32).reshape(shp[:-1] + (n_groups, shp[-1] // n_groups))
    mu = jnp.mean(xf, axis=-1, keepdims=True)
    var = jnp.mean(jnp.square(xf - mu), axis=-1, keepdims=True)
    y = ((xf - mu) * lax.rsqrt(var + EPS)).reshape(shp)
    return (y * g.astype(jnp.float32) + b.astype(jnp.float32)).astype(x.dtype)


def mixer_a(u, v, ln_g, ln_b, sp_w, sp_b):
    bsz, s, _ = v.shape
    u = jax.nn.gelu(u)
    v = group_layer_norm(jax.nn.gelu(v), ln_g, ln_b, A_HEADS)
    vc = v.reshape(bsz, s // CHUNK, CHUNK, A_HEADS, A_HEAD_DIM)
    sv = jnp.einsum('hqp,bcphe->bcqhe', sp_w.astype(v.dtype), vc)
    sv = sv + jnp.transpose(sp_b).astype(v.dtype)[None, None, :, :, None]
    return u * sv.reshape(bsz, s, A_WIDTH)


def mixer_b(val, gate, conv_w, conv_b, ln_g, ln_b):
    z = val * jax.nn.sigmoid(gate)
    z = lax.conv_general_dilated(
        z, conv_w.astype(z.dtype)[:, None, :], window_strides=(1,),
        padding=[(CONV_PAD, CONV_PAD)], dimension_numbers=('NWC', 'WIO', 'NWC'),
        feature_group_count=B_WIDTH) + conv_b.astype(z.dtype)
    z = group_layer_norm(z, ln_g, ln_b, B_GROUPS)
    return jax.nn.silu(z)


def encoder_layer(x, mix_pre_g, w_in, a_ln_g, a_ln_b, a_sp_w, a_sp_b,
                  b_conv_w, b_conv_b, b_ln_g, b_ln_b, grp_g, w_out, mix_post_g,
                  ffn_pre_g, w_gate_up, w_down, ffn_post_g):
    dt = x.dtype
    hn = rms_norm(x, mix_pre_g)
    proj = jnp.einsum('bsd,df->bsf', hn, w_in.astype(dt))
    u_a = proj[..., :A_WIDTH]
    v_a = proj[..., A_WIDTH:2 * A_WIDTH]
    val_b = proj[..., 2 * A_WIDTH:2 * A_WIDTH + B_WIDTH]
    gate_b = proj[..., 2 * A_WIDTH + B_WIDTH:]
    out_a = mixer_a(u_a, v_a, a_ln_g, a_ln_b, a_sp_w, a_sp_b)
    out_b = mixer_b(val_b, gate_b, b_conv_w, b_conv_b, b_ln_g, b_ln_b)
    mixed = jnp.concatenate([rms_norm(out_a, grp_g[:A_WIDTH]),
                             rms_norm(out_b, grp_g[A_WIDTH:])], axis=-1)
    mix_out = jnp.einsum('bsf,fd->bsd', mixed, w_out.astype(dt))
    h = x + rms_norm(mix_out, mix_post_g)
    hn2 = rms_norm(h, ffn_pre_g)
    gu = jnp.einsum('bsd,df->bsf', hn2, w_gate_up.astype(dt))
    act = jax.nn.silu(gu[..., :D_FF]) * gu[..., D_FF:]
    ffn_out = jnp.einsum('bsf,fd->bsd', act, w_down.astype(dt))
    return h + rms_norm(ffn_out, ffn_post_g)


def setup_inputs(seed: int = 0) -> dict:
    key = jax.random.key(seed)
    ks = jax.random.split(key, 20)
    f32 = jnp.float32
    def nrm(k, shape, scale):
        return jax.random.normal(k, shape, f32) * scale
    def gain(k, shape):
        return 1.0 + 0.02 * jax.random.normal(k, shape, f32)
    return {
        "x_prompt": jax.random.normal(ks[0], (BATCH, SEQ, D_MODEL), f32),
        "x_sample": jax.random.normal(ks[1], (DEC_BATCH, DEC_SEQ, D_MODEL), f32),
        "mix_pre_g": gain(ks[2], (DEPTH, D_MODEL)),
        "w_in": nrm(ks[3], (DEPTH, D_MODEL, 2 * MIX_WIDTH), D_MODEL ** -0.5),
        "a_ln_g": gain(ks[4], (DEPTH, A_WIDTH)),
        "a_ln_b": nrm(ks[5], (DEPTH, A_WIDTH), 0.02),
        "a_sp_w": nrm(ks[6], (DEPTH, A_HEADS, CHUNK, CHUNK), CHUNK ** -0.5),
        "a_sp_b": nrm(ks[7], (DEPTH, A_HEADS, CHUNK), 0.02),
        "b_conv_w": nrm(ks[8], (DEPTH, CONV_WIDTH, B_WIDTH), CONV_WIDTH ** -0.5),
        "b_conv_b": nrm(ks[9], (DEPTH, B_WIDTH), 0.02),
        "b_ln_g": gain(ks[10], (DEPTH, B_WIDTH)),
        "b_ln_b": nrm(ks[11], (DEPTH, B_WIDTH), 0.02),
        "grp_g": gain(ks[12], (DEPTH, MIX_WIDTH)),
        "w_out": nrm(ks[13], (DEPTH, MIX_WIDTH, D_MODEL), MIX_WIDTH ** -0.5),
        "mix_post_g": gain(ks[14], (DEPTH, D_MODEL)),
        "ffn_pre_g": gain(ks[15], (DEPTH, D_MODEL)),
        "w_gate_up": nrm(ks[16], (DEPTH, D_MODEL, 2 * D_FF), D_MODEL ** -0.5),
        "w_down": nrm(ks[17], (DEPTH, D_FF, D_MODEL), D_FF ** -0.5),
        "ffn_post_g": gain(ks[18], (DEPTH, D_MODEL)),
    }


def reference(x_prompt, x_sample, mix_pre_g, w_in, a_ln_g, a_ln_b, a_sp_w, a_sp_b,
              b_conv_w, b_conv_b, b_ln_g, b_ln_b, grp_g, w_out, mix_post_g,
              ffn_pre_g, w_gate_up, w_down, ffn_post_g):
    y_prompt = x_prompt
    y_sample = x_sample
    for l in range(DEPTH):
        params = (mix_pre_g[l], w_in[l], a_ln_g[l], a_ln_b[l], a_sp_w[l], a_sp_b[l],
                  b_conv_w[l], b_conv_b[l], b_ln_g[l], b_ln_b[l], grp_g[l], w_out[l],
                  mix_post_g[l], ffn_pre_g[l], w_gate_up[l], w_down[l], ffn_post_g[l])
        y_prompt = encoder_layer(y_prompt, *params)
        y_sample = encoder_layer(y_sample, *params)
    return (y_prompt, y_sample)
```

```python
import numpy as np
from contextlib import ExitStack
import concourse.bass as bass
import concourse.mybir as mybir
from concourse.bass_utils import run_bass_kernel_spmd

F32 = mybir.dt.float32
BF16 = mybir.dt.bfloat16
I32 = mybir.dt.int32
AF = mybir.ActivationFunctionType
ALU = mybir.AluOpType

D = 1024
DFF = 2816
NFB = DFF // 128
T = 512
NCORES = 8
TOK_PER_CORE = 10240
NT_FULL = TOK_PER_CORE // T
SEQ = 4096
EPS = 1e-6
NW = 4
MAGIC = 1597463007.0

VB_GPOST, VB_GPOST2 = 0, 1024
VB_ALG, VB_ALB, VB_BLG, VB_BLB = 2048, 2560, 3072, 3584
VB_N = 4096
YC_ALIAS = False
VP_CB, VP_SPB, VP_CW = 0, 4, 8
VP_GPRE, VP_GPRE2, VP_GRP = 136, 144, 152
VP_N = 160


class Trk:
    def __init__(self, nc, es):
        self.nc = nc
        self.es = es
        self.engs = ("pe", "act", "dve", "pool", "sp")
        self.streams = {e: [] for e in self.engs}
        self.psem = {e: es.enter_context(nc.semaphore("prog_" + e)) for e in ("pe", "act", "dve", "pool")}
        self.cnt = {e: 0 for e in self.psem}
        self.dcnt = {}
        self.waited = {e: {} for e in self.engs}
        self.res = {}

    def new_dma_sem(self, name):
        s = self.es.enter_context(self.nc.semaphore(name))
        self.dcnt[s.num] = 0
        return s

    def _deps(self, eng, reads, writes):
        evs = {}

        def add(ev):
            if ev is None:
                return
            sem, val = ev
            if evs.get(sem.num, (None, 0))[1] < val:
                evs[sem.num] = (sem, val)

        for r in reads:
            st = self.res.get(r)
            if st:
                add(st[0])
        for w in writes:
            st = self.res.get(w)
            if st:
                if st[1]:
                    for ev in st[1].values():
                        add(ev)
                else:
                    add(st[0])
        out = []
        for key, (sem, val) in evs.items():
            if eng == "pe" and sem.num == self.psem["pe"].num:
                continue
            if self.waited[eng].get(key, 0) >= val:
                continue
            self.waited[eng][key] = val
            out.append((sem, val))
        return out

    def _update(self, ev, reads, writes):
        for r in reads:
            st = self.res.setdefault(r, [None, {}])
            st[1][ev[0].num] = ev
        for w in writes:
            self.res[w] = [ev, {}]

    def op(self, eng, fn, reads=(), writes=()):
        waits = self._deps(eng, reads, writes)
        self.cnt[eng] += 1
        ev = (self.psem[eng], self.cnt[eng])
        self.streams[eng].append((waits, fn, (self.psem[eng], 1)))
        self._update(ev, reads, writes)

    def dma(self, q, fn, sem, reads=(), writes=()):
        waits = self._deps(q, reads, writes)
        self.dcnt[sem.num] += 16
        ev = (sem, self.dcnt[sem.num])
        self.streams[q].append((waits, fn, (sem, 16)))
        self._update(ev, reads, writes)

    def final_wait(self, q, sems):
        waits = []
        for s in sems:
            waits.append((s, self.dcnt[s.num]))
        self.streams[q].append((waits, None, None))

    def replay(self, eng_name, eng):
        for waits, fn, inc in self.streams[eng_name]:
            for sem, val in waits:
                eng.wait_ge(sem, val)
            if fn is None:
                continue
            ins = fn(eng)
            if inc is not None:
                ins.then_inc(inc[0], inc[1])


def build(NT, debug=False):
    nc = bass.Bass("TRN2", target_bir_lowering=False)
    ntok = NT * T
    x_d = nc.dram_tensor("x", [ntok, D], F32, kind="ExternalInput").ap()
    xh_d = nc.dram_tensor("xh", [NT * 32, D], F32, kind="ExternalInput").ap()
    w_in_d = nc.dram_tensor("w_in", [D, 2 * D], F32, kind="ExternalInput").ap()
    w_out_d = nc.dram_tensor("w_out", [D, D], F32, kind="ExternalInput").ap()
    w_gu_d = nc.dram_tensor("w_gu", [D, 2 * DFF], F32, kind="ExternalInput").ap()
    w_dn_d = nc.dram_tensor("w_dn", [DFF, D], F32, kind="ExternalInput").ap()
    vecP_d = nc.dram_tensor("vecP", [128, VP_N], F32, kind="ExternalInput").ap()
    vecB_d = nc.dram_tensor("vecB", [128, VB_N], F32, kind="ExternalInput").ap()
    wsT_d = nc.dram_tensor("wsT", [128, 4 * 128], F32, kind="ExternalInput").ap()
    ident_d = nc.dram_tensor("ident", [128, 128], F32, kind="ExternalInput").ap()
    mask_d = nc.dram_tensor("mask", [128, 32], F32, kind="ExternalInput").ap()
    y_d = nc.dram_tensor("y", [ntok, D], F32, kind="ExternalOutput").ap()
    win_b = nc.dram_tensor("win_b", [D, 2 * D], BF16, kind="Internal").ap()
    wout_b = nc.dram_tensor("wout_b", [D, D], BF16, kind="Internal").ap()
    wgu_b = nc.dram_tensor("wgu_b", [D, 11, 512], BF16, kind="Internal").ap()
    wdn_b = nc.dram_tensor("wdn_b", [DFF, D], BF16, kind="Internal").ap()

    es = ExitStack()
    with es:
        def sb(name, shape, dt):
            return es.enter_context(nc.sbuf_tensor(name, shape, dt))

        xin = sb("xin", [128, 2, D], F32)
        xt = sb("xt", [128, 4, D], F32)
        tb = sb("tb", [128, 4, D], BF16)
        aT = sb("aT", [128, 2, 8, T + 32], BF16)
        u_sb = sb("u_sb", [128, 4, 512], F32)
        vn = sb("vn", [128, 4, 512], BF16)
        sig = sb("sig", [128, 2, 512], F32)
        sil = sig
        zbuf = sb("zbuf", [128, 4, T + 32], BF16)
        oa = sb("oa", [128, 4, 512], F32)
        v_sb = oa
        ob = sb("ob", [128, 4, 512], F32)
        actT = sb("actT", [128, NFB, T], BF16)
        mixT = sb("mixT", [128, 8, T], BF16)
        yc = sb("yc", [128, 4, 512], F32) if not YC_ALIAS else None
        st = sb("st", [128, 8, 192], F32)
        identf = sb("identf", [128, 128], F32)
        identb = sb("identb", [128, 128], BF16)
        wsT_b = sb("wsT_b", [128, 4, 128], BF16)
        diagK = sb("diagK", [128, 128, 32], BF16)
        ZS = sb("ZS", [128, 4, 4, 520], BF16)
        maskf = sb("maskf", [128, 32], F32)
        vecP = sb("vecP_s", [128, VP_N], F32)
        vecB = sb("vecB_s", [128, VB_N], F32)
        wbuf = sb("wbuf", [128, NW, 8, 512], BF16)
        ps = [es.enter_context(nc.psum_tensor("ps%d" % i, [128, 512], F32)) for i in range(8)]
        if YC_ALIAS:
            actT_f = actT.bitcast(F32)

            def yc_blk(j):
                return actT_f[:, 8 + 2 * j:10 + 2 * j, :].rearrange("p a b -> p (a b)")

            def yc_tile(j, i):
                return actT_f[:, 8 + 2 * j + i // 2, (i % 2) * 128:(i % 2) * 128 + 128]
        else:
            def yc_blk(j):
                return yc[:, j, :]

            def yc_tile(j, i):
                return yc[:, j, i * 128:(i + 1) * 128]

        tk = Trk(nc, es)
        s_cast = {k: tk.new_dma_sem("s_cast_" + k) for k in "abcde"}
        s_const = [tk.new_dma_sem("s_const%d" % i) for i in range(5)]
        s_zs = tk.new_dma_sem("s_zs")
        s_xin = [tk.new_dma_sem("s_xin%d" % i) for i in range(2)]
        s_xt = [tk.new_dma_sem("s_xt%d" % i) for i in range(4)]
        s_y = [tk.new_dma_sem("s_y%d" % i) for i in range(4)]
        s_y2 = [tk.new_dma_sem("s_yb%d" % i) for i in range(4)]
        s_w = [tk.new_dma_sem("s_w%d" % i) for i in range(NW)]

        free_banks = list(range(8))

        def balloc():
            assert free_banks, "PSUM banks exhausted at emission"
            return free_banks.pop(0)

        def bfree(b):
            free_banks.append(b)

        def psb(b):
            return ps[b][:]

        ps16 = [p.bitcast(BF16) for p in ps]

        def psb16(b):
            return ps16[b][:]

        F_BN, F_MV, F_A, F_Y, F_T, F_SS = 0, 96, 128, 144, 160, 176

        def chain(slot, src_ap, scale, n, c0, reads, tag):
            res = "s%dch%s" % (slot, tag)
            a = st[:, slot, F_A + c0:F_A + c0 + n]
            y = st[:, slot, F_Y + c0:F_Y + c0 + n]
            t = st[:, slot, F_T + c0:F_T + c0 + n]
            tk.op("dve", lambda e: e.tensor_scalar(out=a, in0=src_ap, scalar1=scale, scalar2=EPS,
                                                   op0=ALU.mult, op1=ALU.add), reads=list(reads) + [res], writes=[res])
            tk.op("dve", lambda e: e.tensor_scalar(out=y.bitcast(I32), in0=a.bitcast(I32), scalar1=-0.5,
                                                   scalar2=MAGIC, op0=ALU.mult, op1=ALU.add),
                  reads=[res], writes=[res])
            for _ in range(3):
                tk.op("dve", lambda e: e.scalar_tensor_tensor(out=t, in0=y, scalar=-0.5, in1=y,
                                                              op0=ALU.mult, op1=ALU.mult),
                      reads=[res], writes=[res])
                tk.op("dve", lambda e: e.tensor_tensor(out=t, in0=t, in1=a, op=ALU.mult),
                      reads=[res], writes=[res])
                tk.op("dve", lambda e: e.scalar_tensor_tensor(out=y, in0=t, scalar=1.5, in1=y,
                                                              op0=ALU.add, op1=ALU.mult),
                      reads=[res], writes=[res])
            return y, res

        s_cast_win = [tk.new_dma_sem("s_cast_win%d" % c) for c in range(4)]
        for c in range(4):
            for hb in range(2):
                r0 = hb * 512
                tk.dma("pool", lambda e, r0=r0, c=c: e.dma_start(
                    out=win_b[r0:r0 + 512, c * 512:(c + 1) * 512], in_=w_in_d[r0:r0 + 512, c * 512:(c + 1) * 512]),
                    s_cast_win[c], writes=["wscr_a%d_%d" % (c, hb)])
        for rb in range(8):
            r0 = rb * 128
            tk.dma("pool", lambda e, r0=r0: e.dma_start(out=wout_b[r0:r0 + 128, :], in_=w_out_d[r0:r0 + 128, :]),
                   s_cast["b"], writes=["wscr%d_b" % rb])
        for rb in range(8):
            r0 = rb * 128
            tk.dma("pool", lambda e, r0=r0: e.dma_start(
                out=wgu_b[r0:r0 + 128, :, 0:256],
                in_=w_gu_d[r0:r0 + 128, 0:DFF].rearrange("p (c f) -> p c f", f=256)),
                s_cast["c"], writes=["wscr%d_c" % rb])
            tk.dma("pool", lambda e, r0=r0: e.dma_start(
                out=wgu_b[r0:r0 + 128, :, 256:512],
                in_=w_gu_d[r0:r0 + 128, DFF:2 * DFF].rearrange("p (c f) -> p c f", f=256)),
                s_cast["d"], writes=["wscr%d_d" % rb])
        for rb in range(NFB):
            r0 = rb * 128
            tk.dma("pool", lambda e, r0=r0: e.dma_start(out=wdn_b[r0:r0 + 128, :], in_=w_dn_d[r0:r0 + 128, :]),
                   s_cast["e"], writes=["wscr%d_e" % rb])
        WSCR = {"win0": ["wscr_a0_0", "wscr_a0_1"], "win1": ["wscr_a1_0", "wscr_a1_1"],
                "win2": ["wscr_a2_0", "wscr_a2_1"], "win3": ["wscr_a3_0", "wscr_a3_1"],
                "wout": ["wscr%d_b" % i for i in range(8)],
                "wgu": ["wscr%d_c" % i for i in range(8)] + ["wscr%d_d" % i for i in range(8)],
                "wdn": ["wscr%d_e" % i for i in range(NFB)]}

        tk.dma("sp", lambda e: e.dma_start(out=vecP[:], in_=vecP_d), s_const[0], writes=["vecP"])
        tk.dma("sp", lambda e: e.dma_start(out=vecB[:], in_=vecB_d), s_const[1], writes=["vecB"])
        tk.dma("sp", lambda e: e.dma_start(out=identf[:], in_=ident_d), s_const[2], writes=["identf"])
        tk.dma("sp", lambda e: e.dma_start(out=ob[:, 0, :], in_=wsT_d), s_const[3], writes=["ob0"])
        tk.op("dve", lambda e: e.tensor_copy(out=identb[:], in_=identf[:]), reads=["identf"], writes=["identb"])
        tk.op("dve", lambda e: e.tensor_copy(out=wsT_b[:].rearrange("p h q -> p (h q)"), in_=ob[:, 0, :]),
              reads=["ob0"], writes=["wsT_b"])
        tk.dma("sp", lambda e: e.dma_start(out=maskf[:], in_=mask_d), s_const[4], writes=["maskf"])
        tk.op("dve", lambda e: e.tensor_scalar(out=vecP[:, VP_CW:VP_CW + 128], in0=vecP[:, VP_CW:VP_CW + 128],
                                               scalar1=0.5, scalar2=None, op0=ALU.mult),
              reads=["vecP"], writes=["vecP"])
        for jk in range(128):
            def _f(e, jk=jk):
                return e.tensor_scalar(out=diagK[:, jk, :], in0=maskf[:], scalar1=vecP[:, VP_CW + jk:VP_CW + jk + 1],
                                       scalar2=None, op0=ALU.mult)
            tk.op("dve", _f, reads=["maskf", "vecP"], writes=["diag_jk%d" % jk])
        tk.op("dve", lambda e: e.memset(st[:], 0.0), reads=["diag_jk%d" % jk for jk in range(128)],
              writes=["diagall", "stall"])

        WIN4 = [("win", c) for c in range(4)]
        PLAN = list(WIN4) + [("wout", 0), ("wout", 1)]
        if NT > 1:
            PLAN += WIN4
        for m_ in range(NT):
            PLAN += [("wgu", c) for c in range(11)] + [("wdn", c) for c in range(6)]
            if m_ + 1 < NT:
                PLAN += [("wout", 0), ("wout", 1)]
            if m_ + 2 < NT:
                PLAN += WIN4
        wstate = {"next_issue": 0, "total": len(PLAN)}

        def issue_chunk():
            g = wstate["next_issue"]
            if g >= wstate["total"]:
                return
            wstate["next_issue"] += 1
            kind, c = PLAN[g]
            slot = g % NW
            res = "wslot%d" % slot
            if kind == "win":
                src = win_b.rearrange("(k p) f -> p k f", p=128)[:, :, c * 512:(c + 1) * 512]
                dst = wbuf[:, slot, :, :]
            elif kind == "wout":
                src = wout_b.rearrange("(k p) f -> p k f", p=128)[:, :, c * 512:(c + 1) * 512]
                dst = wbuf[:, slot, :, :]
            elif kind == "wgu":
                src = wgu_b.rearrange("(k p) c f -> p k c f", p=128)[:, :, c, :]
                dst = wbuf[:, slot, :, :]
            else:
                half, kg = c // 3, c % 3
                nk = 8 if kg < 2 else 6
                src = wdn_b.rearrange("(k p) f -> p k f", p=128)[:, kg * 8:kg * 8 + nk, half * 512:(half + 1) * 512]
                dst = wbuf[:, slot, 0:nk, :]
            wkey = ("win%d" % c) if kind == "win" else kind
            tk.dma("sp", lambda e: e.dma_start(out=dst, in_=src), s_w[slot], reads=WSCR[wkey], writes=[res])

        cons = {"n": 0}

        def next_chunk():
            g = cons["n"]
            cons["n"] += 1
            return g % NW, "wslot%d" % (g % NW)

        AT_P = ["aP0", "aP1", "aP2", "aP3"]
        AT_Q = ["aQ0", "aQ1", "aQ2", "aQ3"]

        def p1_s0(m, j):
            rows = 128 if j < 4 else 32
            xs = j % 2
            if j < 4:
                src = x_d[m * T + j * 128: m * T + (j + 1) * 128, :]
            else:
                src = xh_d[m * 32:(m + 1) * 32, :]
            tk.dma("sp", lambda e: e.dma_start(out=xin[0:rows, xs, :], in_=src), s_xin[xs], writes=["xin%d" % xs])

        def p1_s12(m, j):
            rows = 128 if j < 4 else 32
            xs = j % 2
            xres = "xin%d" % xs
            sres = "s0ss%d" % j
            tk.op("act", lambda e: e.activation(out=tb[0:rows, j % 4, :], in_=xin[0:rows, xs, :], func=AF.Square,
                                                accum_out=st[0:rows, 0, F_SS + j:F_SS + j + 1]),
                  reads=[xres, "stall"], writes=["tb%da" % (j % 4), "tb%db" % (j % 4), sres])
            y, cres = chain(0, st[:, 0, F_SS + j:F_SS + j + 1], 1.0 / D, 1, j, [sres], "j%d" % j)
            ts = j % 4
            tres = ["tb%da" % ts, "tb%db" % ts]
            tk.op("dve", lambda e: e.tensor_scalar(out=tb[0:rows, ts, :], in0=xin[0:rows, xs, :],
                                                   scalar1=y[0:rows, :], scalar2=None, op0=ALU.mult),
                  reads=[xres, cres], writes=tres)

        def phase1_nonpe(m, j):
            p1_s0(m, j)
            p1_s12(m, j)

        def phase1_pe(m, j):
            rows = 128 if j < 4 else 32
            ts = j % 4
            tres = ["tb%da" % ts, "tb%db" % ts]
            b = balloc()
            bres = "bank%d" % b

            def _tr(e):
                ins = None
                for k in range(8):
                    ins = e.transpose(psb16(b)[:, k * 128:k * 128 + rows],
                                      tb[0:rows, ts, k * 128:(k + 1) * 128], identb[0:rows, 0:rows])
                return ins
            tk.op("pe", _tr, reads=tres + ["identb"], writes=[bres])
            ares = "aP%d" % j

            def _ev(e):
                ins = None
                for k in range(8):
                    ins = e.activation(out=aT[:, 0, k, j * 128:j * 128 + rows],
                                       in_=psb16(b)[:, k * 128:k * 128 + rows], func=AF.Copy,
                                       scale=vecP[:, VP_GPRE + k:VP_GPRE + k + 1])
                return ins
            tk.op("act", _ev, reads=[bres, "vecP"], writes=[ares])
            bfree(b)

        def A_mm_all(m, wu, wv):
            for which, (ws, wres) in enumerate((wu, wv)):
                for i in range(4):
                    b = balloc()

                    def _mm(e, ws=ws, b=b, i=i):
                        ins = None
                        for k in range(8):
                            ins = e.matmul(psb(b), lhsT=aT[:, 0, k, i * 128:(i + 1) * 128], rhs=wbuf[:, ws, k, :],
                                           start=(k == 0), stop=(k == 7))
                        return ins
                    tk.op("pe", _mm, reads=["aP%d" % i, wres], writes=["bank%d" % b])
                    dst = u_sb if which == 0 else v_sb
                    dres = ("u%d" if which == 0 else "oa%d") % i
                    tk.op("act", lambda e, b=b, dst=dst, i=i: e.activation(out=dst[:, i, :], in_=psb(b), func=AF.Gelu),
                          reads=["bank%d" % b], writes=[dres])
                    bfree(b)
                if which == 0:
                    issue_chunk()
            issue_chunk()

        def A_ln(m):
            for i in range(4):
                def _bn(e, i=i):
                    ins = None
                    for h in range(4):
                        ins = e.bn_stats(out=st[:, 1, F_BN + (i * 4 + h) * 6:F_BN + (i * 4 + h + 1) * 6],
                                         in_=v_sb[:, i, h * 128:(h + 1) * 128])
                    return ins
                tk.op("dve", _bn, reads=["oa%d" % i, "stall"], writes=["s1bn%d" % i])

                def _ag(e, i=i):
                    ins = None
                    for h in range(4):
                        ins = e.bn_aggr(out=st[:, 1, F_MV + (i * 4 + h) * 2:F_MV + (i * 4 + h + 1) * 2],
                                        in_=st[:, 1, F_BN + (i * 4 + h) * 6:F_BN + (i * 4 + h + 1) * 6])
                    return ins
                tk.op("dve", _ag, reads=["s1bn%d" % i], writes=["s1mv%d" % i])

            var_ap = st[:, 1, F_MV:F_MV + 32].rearrange("p (n c) -> p n c", c=2)[:, :, 1]
            y, cres = chain(1, var_ap, 1.0, 16, 0, ["s1mv%d" % i for i in range(4)], "")
            for i in range(4):
                def _nrm(e, i=i):
                    ins = None
                    for h in range(4):
                        c = i * 4 + h
                        ins = e.tensor_scalar(out=v_sb[:, i, h * 128:(h + 1) * 128],
                                              in0=v_sb[:, i, h * 128:(h + 1) * 128],
                                              scalar1=st[:, 1, F_MV + 2 * c:F_MV + 2 * c + 1], scalar2=y[:, c:c + 1],
                                              op0=ALU.subtract, op1=ALU.mult)
                    return ins
                tk.op("dve", _nrm, reads=["oa%d" % i, cres, "s1mv%d" % i], writes=["oa%d" % i])
                tk.op("pool", lambda e, i=i: e.tensor_tensor(out=v_sb[:, i, :], in0=v_sb[:, i, :],
                                                             in1=vecB[:, VB_ALG:VB_ALG + 512], op=ALU.mult),
                      reads=["oa%d" % i, "vecB"], writes=["oa%d" % i])
                tk.op("pool", lambda e, i=i: e.tensor_tensor(out=vn[:, i, :], in0=v_sb[:, i, :],
                                                             in1=vecB[:, VB_ALB:VB_ALB + 512], op=ALU.add),
                      reads=["oa%d" % i, "vecB"], writes=["vn%d" % i])

        def A_sp_a(m):
            for i in range(4):
                b = balloc()

                def _mm(e, b=b, i=i):
                    ins = None
                    for h in range(4):
                        ins = e.matmul(psb(b)[:, h * 128:(h + 1) * 128], lhsT=wsT_b[:, h, :],
                                       rhs=vn[:, i, h * 128:(h + 1) * 128], start=True, stop=True)
                    return ins
                tk.op("pe", _mm, reads=["vn%d" % i, "wsT_b"], writes=["bank%d" % b])

                def _oa(e, b=b, i=i):
                    ins = None
                    for h in range(4):
                        ins = e.scalar_tensor_tensor(out=oa[:, i, h * 128:(h + 1) * 128],
                                                     in0=psb(b)[:, h * 128:(h + 1) * 128],
                                                     scalar=vecP[:, VP_SPB + h:VP_SPB + h + 1],
                                                     in1=u_sb[:, i, h * 128:(h + 1) * 128],
                                                     op0=ALU.add, op1=ALU.mult)
                    return ins
                tk.op("dve", _oa, reads=["bank%d" % b, "u%d" % i, "vecP"], writes=["oa%d" % i])
                bfree(b)
                tk.op("act", lambda e, i=i: e.activation(out=tb[:, i, 0:512], in_=oa[:, i, :], func=AF.Square,
                                                         accum_out=st[:, 2, F_SS + i:F_SS + i + 1]),
                      reads=["oa%d" % i, "stall"], writes=["tb%da" % i, "s2ss%d" % i])

        def A_sp_b(m):
            y, cres = chain(2, st[:, 2, F_SS:F_SS + 4], 1.0 / 512, 4, 0, ["s2ss%d" % i for i in range(4)], "")
            for i in range(4):
                tk.op("dve", lambda e, i=i: e.tensor_scalar(out=tb[:, i, 0:512], in0=oa[:, i, :],
                                                            scalar1=y[:, i:i + 1], scalar2=None, op0=ALU.mult),
                      reads=["oa%d" % i, cres], writes=["tb%da" % i])

        def T_half(i, half):
            b = balloc()
            c0 = half * 512
            tres = "tb%d%s" % (i, "ab"[half])

            def _tr(e):
                ins = None
                for k in range(4):
                    ins = e.transpose(psb16(b)[:, k * 128:(k + 1) * 128],
                                      tb[:, i, c0 + k * 128:c0 + (k + 1) * 128], identb[:])
                return ins
            tk.op("pe", _tr, reads=[tres, "identb"], writes=["bank%d" % b])

            def _ev(e):
                ins = None
                for k in range(4):
                    kk = half * 4 + k
                    ins = e.activation(out=mixT[:, kk, i * 128:(i + 1) * 128],
                                       in_=psb16(b)[:, k * 128:(k + 1) * 128], func=AF.Copy,
                                       scale=vecP[:, VP_GRP + kk:VP_GRP + kk + 1])
                return ins
            tk.op("act", _ev, reads=["bank%d" % b, "vecP"], writes=["mixT%d_%d" % (i, half)])
            bfree(b)

        def B_proj(m, wval, wgate, mid_hook=None):
            (sv_, rv), (sg_, rg) = wval, wgate
            bh = balloc()

            def _mh(e):
                ins = None
                for j in range(4):
                    for k in range(8):
                        ins = e.matmul(psb(bh)[:, j * 32:(j + 1) * 32], lhsT=wbuf[:, sv_, k, j * 128:(j + 1) * 128],
                                       rhs=aT[:, 0, k, T:T + 32], start=(k == 0), stop=(k == 7))
                for j in range(4):
                    for k in range(8):
                        ins = e.matmul(psb(bh)[:, 128 + j * 32:128 + (j + 1) * 32],
                                       lhsT=wbuf[:, sg_, k, j * 128:(j + 1) * 128],
                                       rhs=aT[:, 0, k, T:T + 32], start=(k == 0), stop=(k == 7))
                return ins
            tk.op("pe", _mh, reads=["aP4", rv, rg], writes=["bank%d" % bh])
            tk.op("act", lambda e: e.activation(out=sig[:, 0, 0:128], in_=psb(bh)[:, 128:256],
                                                func=AF.Tanh, scale=0.5),
                  reads=["bank%d" % bh], writes=["sig0"])
            for side in range(2):
                def _zh(e, side=side):
                    dst = zbuf[:, :, 0:16] if side == 0 else zbuf[:, :, T + 16:T + 32]
                    a = sig[:, 0, 0:128].rearrange("p (j t) -> p j t", j=4)[:, :, side * 16:side * 16 + 16]
                    bb = psb(bh)[:, 0:128].rearrange("p (j t) -> p j t", j=4)[:, :, side * 16:side * 16 + 16]
                    return e.scalar_tensor_tensor(out=dst, in0=a, scalar=1.0, in1=bb, op0=ALU.add, op1=ALU.mult)
                tk.op("dve", _zh, reads=["bank%d" % bh, "sig0"], writes=["zh%d" % side])
            bfree(bh)
            for j in range(4):
                bv = balloc()
                bg = balloc()

                def _mv(e, bv=bv, j=j):
                    ins = None
                    for k in range(8):
                        ins = e.matmul(psb(bv), lhsT=wbuf[:, sv_, k, j * 128:(j + 1) * 128], rhs=aT[:, 0, k, 0:T],
                                       start=(k == 0), stop=(k == 7))
                    return ins

                def _mg(e, bg=bg, j=j):
                    ins = None
                    for k in range(8):
                        ins = e.matmul(psb(bg), lhsT=wbuf[:, sg_, k, j * 128:(j + 1) * 128], rhs=aT[:, 0, k, 0:T],
                                       start=(k == 0), stop=(k == 7))
                    return ins
                tk.op("pe", _mg, reads=AT_P + [rg], writes=["bank%d" % bg])
                tk.op("pe", _mv, reads=AT_P + [rv], writes=["bank%d" % bv])
                ss = (j + 1) % 2
                tk.op("act", lambda e, bg=bg, ss=ss: e.activation(out=sig[:, ss, :], in_=psb(bg), func=AF.Tanh,
                                                                  scale=0.5),
                      reads=["bank%d" % bg], writes=["sig%d" % ss])
                bfree(bg)
                tk.op("dve", lambda e, bv=bv, ss=ss, j=j: e.scalar_tensor_tensor(
                    out=zbuf[:, j, 16:16 + T], in0=sig[:, ss, :], scalar=1.0, in1=psb(bv),
                    op0=ALU.add, op1=ALU.mult), reads=["bank%d" % bv, "sig%d" % ss], writes=["z%d" % j])
                bfree(bv)
                if j == 1 and mid_hook is not None:
                    mid_hook()

        ZS_RES = ["zs%d_%d" % (g, jj) for g in range(4) for jj in range(4)]

        def B_shift(m):
            for g in range(4):
                for jj in range(4):
                    tk.dma("pool", lambda e, g=g, jj=jj: e.dma_start(
                        out=ZS[32 * jj:32 * jj + 32, g, :, :], in_=zbuf[32 * g:32 * g + 32, :, 8 * jj:8 * jj + 520]),
                        s_zs, reads=["z0", "z1", "z2", "z3", "zh0", "zh1"], writes=["zs%d_%d" % (g, jj)])

        def B_conv(m):
            for j in range(4):
                b = balloc()

                def _cv(e, b=b, j=j):
                    ins = None
                    for mm in range(8):
                        for g in range(4):
                            ins = e.matmul(ps[b][32 * g:32 * g + 32, :], lhsT=diagK[:, j * 32 + mm * 4 + g, :],
                                           rhs=ZS[:, g, j, mm + 1:mm + 1 + T], start=(mm == 0), stop=(mm == 7),
                                           tile_position=(0, 32 * g))
                    return ins
                tk.op("pe", _cv, reads=ZS_RES + ["diagall"], writes=["bank%d" % b])
                tk.op("act", lambda e, b=b, j=j: e.activation(out=yc_blk(j), in_=psb(b), func=AF.Identity,
                                                              bias=vecP[:, VP_CB + j:VP_CB + j + 1]),
                      reads=["bank%d" % b, "vecP"], writes=["yc%d" % j])
                bfree(b)

        def B_tok_a(m):
            banks = []
            for i in range(4):
                b = balloc()
                banks.append(b)

                def _tr(e, b=b, i=i):
                    ins = None
                    for j in range(4):
                        ins = e.transpose(psb(b)[:, j * 128:(j + 1) * 128], yc_tile(j, i), identf[:])
                    return ins
                tk.op("pe", _tr, reads=["yc0", "yc1", "yc2", "yc3", "identf"], writes=["bank%d" % b])

                def _bn(e, b=b, i=i):
                    ins = None
                    for h in range(4):
                        ins = e.bn_stats(out=st[:, 3, F_BN + (i * 4 + h) * 6:F_BN + (i * 4 + h + 1) * 6],
                                         in_=psb(b)[:, h * 128:(h + 1) * 128])
                    return ins
                tk.op("dve", _bn, reads=["bank%d" % b, "stall"], writes=["s3bn%d" % i])

                def _ag(e, i=i):
                    ins = None
                    for h in range(4):
                        ins = e.bn_aggr(out=st[:, 3, F_MV + (i * 4 + h) * 2:F_MV + (i * 4 + h + 1) * 2],
                                        in_=st[:, 3, F_BN + (i * 4 + h) * 6:F_BN + (i * 4 + h + 1) * 6])
                    return ins
                tk.op("dve", _ag, reads=["s3bn%d" % i], writes=["s3mv%d" % i])
            var_ap = st[:, 3, F_MV:F_MV + 32].rearrange("p (n c) -> p n c", c=2)[:, :, 1]
            y, cres = chain(3, var_ap, 1.0, 16, 0, ["s3mv%d" % i for i in range(4)], "")
            for i in range(4):
                b = banks[i]

                def _nrm(e, b=b, i=i):
                    ins = None
                    for h in range(4):
                        c = i * 4 + h
                        ins = e.tensor_scalar(out=ob[:, i, h * 128:(h + 1) * 128], in0=psb(b)[:, h * 128:(h + 1) * 128],
                                              scalar1=st[:, 3, F_MV + 2 * c:F_MV + 2 * c + 1], scalar2=y[:, c:c + 1],
                                              op0=ALU.subtract, op1=ALU.mult)
                    return ins
                tk.op("dve", _nrm, reads=["bank%d" % b, cres, "s3mv%d" % i], writes=["ob%d" % i])
                bfree(b)
                geng = "pool"
                tk.op(geng, lambda e, i=i: e.tensor_tensor(out=ob[:, i, :], in0=ob[:, i, :],
                                                           in1=vecB[:, VB_BLG:VB_BLG + 512], op=ALU.mult),
                      reads=["ob%d" % i, "vecB"], writes=["ob%d" % i])
                tk.op(geng, lambda e, i=i: e.tensor_tensor(out=ob[:, i, :], in0=ob[:, i, :],
                                                           in1=vecB[:, VB_BLB:VB_BLB + 512], op=ALU.add),
                      reads=["ob%d" % i, "vecB"], writes=["ob%d" % i])

        def B_tok_b(m):
            for i in range(4):
                tk.op("act", lambda e, i=i: e.activation(out=ob[:, i, :], in_=ob[:, i, :], func=AF.Silu),
                      reads=["ob%d" % i], writes=["ob%d" % i])
                tk.op("act", lambda e, i=i: e.activation(out=tb[:, i, 512:1024], in_=ob[:, i, :], func=AF.Square,
                                                         accum_out=st[:, 4, F_SS + i:F_SS + i + 1]),
                      reads=["ob%d" % i, "stall"], writes=["tb%db" % i, "s4ss%d" % i])

        def B_tok_c(m):
            y2, c2 = chain(4, st[:, 4, F_SS:F_SS + 4], 1.0 / 512, 4, 0, ["s4ss%d" % i for i in range(4)], "")
            for i in range(4):
                tk.op("dve", lambda e, i=i: e.tensor_scalar(out=tb[:, i, 512:1024], in0=ob[:, i, :],
                                                            scalar1=y2[:, i:i + 1], scalar2=None, op0=ALU.mult),
                      reads=["ob%d" % i, c2], writes=["tb%db" % i])

        def x_reload(m):
            for i in range(4):
                tk.dma("act", lambda e, i=i: e.dma_start(out=xt[:, i, :],
                                                         in_=x_d[m * T + i * 128:m * T + (i + 1) * 128, :]),
                       s_xt[i], writes=["xt%d" % i])

        def wout_wave(m, tiles, w0, w1, wave):
            tmpbuf = ob if wave == 0 else yc
            tmpname = "ob%d" if wave == 0 else "yc%d"
            for i in tiles:
                for half, (ws, wres) in enumerate((w0, w1)):
                    b = balloc()
                    q = 2 * (i % 2) + half

                    def _mm(e, ws=ws, b=b, i=i):
                        ins = None
                        for k in range(8):
                            ins = e.matmul(psb(b), lhsT=mixT[:, k, i * 128:(i + 1) * 128], rhs=wbuf[:, ws, k, :],
                                           start=(k == 0), stop=(k == 7))
                        return ins
                    tk.op("pe", _mm, reads=["mixT%d_0" % i, "mixT%d_1" % i, wres], writes=["bank%d" % b])
                    tk.op("act", lambda e, b=b, q=q: e.activation(out=tmpbuf[:, q, :], in_=psb(b), func=AF.Copy),
                          reads=["bank%d" % b], writes=[tmpname % q])
                    bfree(b)
                    tk.op("act", lambda e, q=q, i=i, half=half: e.activation(
                        out=tb[:, i, half * 512:(half + 1) * 512], in_=tmpbuf[:, q, :], func=AF.Square,
                        accum_out=st[:, 5, F_SS + 2 * i + half:F_SS + 2 * i + half + 1]),
                        reads=[tmpname % q, "stall"], writes=["tb%d%s" % (i, "ab"[half]), "s5ss%d_%d" % (i, half)])
                    tk.op("pool", lambda e, q=q, half=half: e.tensor_tensor(
                        out=tmpbuf[:, q, :], in0=tmpbuf[:, q, :],
                        in1=vecB[:, VB_GPOST + half * 512:VB_GPOST + (half + 1) * 512], op=ALU.mult),
                        reads=[tmpname % q, "vecB"], writes=[tmpname % q])

        def wout_wave_b(m, tiles, wave):
            tmpbuf = ob if wave == 0 else yc
            tmpname = "ob%d" if wave == 0 else "yc%d"
            i0 = tiles[0]
            n = len(tiles)
            sres = ["s5ss%d_%d" % (i, h) for i in tiles for h in range(2)]
            sums = st[:, 5, F_SS + 8 + i0:F_SS + 8 + i0 + n]
            pair = st[:, 5, F_SS + 2 * i0:F_SS + 2 * i0 + 2 * n].rearrange("p (n c) -> p n c", c=2)
            tk.op("dve", lambda e: e.tensor_tensor(out=sums, in0=pair[:, :, 0], in1=pair[:, :, 1], op=ALU.add),
                  reads=sres, writes=["s5sum%d" % wave])
            y, cres = chain(5, sums, 1.0 / D, n, i0, ["s5sum%d" % wave], "w%d" % wave)
            for ii, i in enumerate(tiles):
                for half in range(2):
                    q = 2 * (i % 2) + half
                    tmp = tmpbuf[:, q, :]
                    tres = tmpname % q
                    tk.op("dve", lambda e, half=half, i=i, ii=ii, tmp=tmp: e.scalar_tensor_tensor(
                        out=xt[:, i, half * 512:(half + 1) * 512], in0=tmp, scalar=y[:, ii:ii + 1],
                        in1=xt[:, i, half * 512:(half + 1) * 512], op0=ALU.mult, op1=ALU.add),
                        reads=["xt%d" % i, tres, cres], writes=["xt%d" % i])
                tk.op("dve", lambda e, i=i: e.scalar_tensor_tensor(
                    out=tb[:, i, :], in0=xt[:, i, :], scalar=1.0, in1=xt[:, i, :], op0=ALU.mult, op1=ALU.mult,
                    accum_out=st[:, 6, F_SS + i:F_SS + i + 1]),
                    reads=["xt%d" % i, "stall"], writes=["s6ss%d" % i, "tb%da" % i, "tb%db" % i])
            y2, c2 = chain(6, st[:, 6, F_SS + i0:F_SS + i0 + n], 1.0 / D, n, i0, ["s6ss%d" % i for i in tiles],
                           "w%d" % wave)
            for ii, i in enumerate(tiles):
                tk.op("dve", lambda e, i=i, ii=ii: e.tensor_scalar(out=tb[:, i, :], in0=xt[:, i, :],
                                                                   scalar1=y2[:, ii:ii + 1], scalar2=None, op0=ALU.mult),
                      reads=["xt%d" % i, c2], writes=["tb%da" % i, "tb%db" % i])

        def hn2T(i):
            b = balloc()

            def _tr(e):
                ins = None
                for k in range(8):
                    ins = e.transpose(psb16(b)[:, k * 128:(k + 1) * 128], tb[:, i, k * 128:(k + 1) * 128], identb[:])
                return ins
            tk.op("pe", _tr, reads=["tb%da" % i, "tb%db" % i, "identb"], writes=["bank%d" % b])

            g_bc = bass.AP(vecP, VP_GPRE2, [[VP_N, 128], [1, 8], [0, 128]])

            def _ev(e):
                return e.tensor_tensor(out=aT[:, 1, :, i * 128:(i + 1) * 128],
                                       in0=psb16(b).rearrange("p (k t) -> p k t", k=8), in1=g_bc, op=ALU.mult)
            tk.op("dve", _ev, reads=["bank%d" % b, "vecP"], writes=["aQ%d" % i])
            bfree(b)

        def gu_chunk(m, c, w):
            ws, wres = w
            for jj in range(2):
                j = 2 * c + jj
                bg = balloc()
                bu = balloc()

                def _mg(e, bg=bg, jj=jj):
                    ins = None
                    for k in range(8):
                        ins = e.matmul(psb(bg), lhsT=wbuf[:, ws, k, jj * 128:(jj + 1) * 128], rhs=aT[:, 1, k, 0:T],
                                       start=(k == 0), stop=(k == 7))
                    return ins

                def _mu(e, bu=bu, jj=jj):
                    ins = None
                    for k in range(8):
                        ins = e.matmul(psb(bu), lhsT=wbuf[:, ws, k, 256 + jj * 128:256 + (jj + 1) * 128],
                                       rhs=aT[:, 1, k, 0:T], start=(k == 0), stop=(k == 7))
                    return ins
                tk.op("pe", _mg, reads=AT_Q + [wres], writes=["bank%d" % bg])
                tk.op("pe", _mu, reads=AT_Q + [wres], writes=["bank%d" % bu])
                ss = j % 2
                tk.op("act", lambda e, bg=bg, ss=ss: e.activation(out=sil[:, ss, :], in_=psb(bg), func=AF.Silu),
                      reads=["bank%d" % bg], writes=["sig%d" % ss])
                bfree(bg)
                extra = []
                tk.op("dve", lambda e, bu=bu, ss=ss, j=j: e.tensor_tensor(out=actT[:, j, :], in0=sil[:, ss, :],
                                                                          in1=psb(bu), op=ALU.mult),
                      reads=["bank%d" % bu, "sig%d" % ss], writes=["actT%d" % j] + extra)
                bfree(bu)

        def down_half(m, half, wch, hook=None):
            banks = [balloc() for _ in range(4)]
            for kg in range(3):
                ws, wres = wch[kg]
                nk = 8 if kg < 2 else 6
                for i in range(4):
                    def _mm(e, ws=ws, kg=kg, nk=nk, i=i, b=banks[i]):
                        ins = None
                        for kk in range(nk):
                            k = kg * 8 + kk
                            ins = e.matmul(psb(b), lhsT=actT[:, k, i * 128:(i + 1) * 128], rhs=wbuf[:, ws, kk, :],
                                           start=(k == 0), stop=(k == NFB - 1))
                        return ins
                    tk.op("pe", _mm, reads=["actT%d" % k for k in range(kg * 8, kg * 8 + nk)] + [wres],
                          writes=["bank%d" % banks[i]])
                issue_chunk()
                if hook is not None:
                    hook(half * 3 + kg)
            return banks

        pending_y = []

        def flush_y():
            for (m_, i) in pending_y:
                tk.dma("pool", lambda e, i=i, m_=m_: e.dma_start(
                    out=y_d[m_ * T + i * 128:m_ * T + (i + 1) * 128, 0:512], in_=oa[:, i, :]),
                    s_y[i], reads=["oa%d" % i])
                tk.dma("pool", lambda e, i=i, m_=m_: e.dma_start(
                    out=y_d[m_ * T + i * 128:m_ * T + (i + 1) * 128, 512:1024], in_=yc[:, i, :]),
                    s_y2[i], reads=["yc%d" % i])
            del pending_y[:]

        def mixer_front_all(m):
            A_sp_a(m)
            A_sp_b(m)
            B_conv(m)
            for i in range(4):
                T_half(i, 0)
            B_tok_a(m)
            B_tok_b(m)
            B_tok_c(m)

        def w_in_all(m):
            wu = next_chunk()
            wv = next_chunk()
            A_mm_all(m, wu, wv)
            wval = next_chunk()
            wgate = next_chunk()
            B_proj(m, wval, wgate)
            B_shift(m)
            issue_chunk()
            issue_chunk()
            A_ln(m)

        def wout_all(m):
            w0 = next_chunk()
            w1 = next_chunk()
            wout_wave(m, [0, 1], w0, w1, 0)
            wout_wave(m, [2, 3], w0, w1, 1)
            wout_wave_b(m, [0, 1], 0)
            wout_wave_b(m, [2, 3], 1)
            issue_chunk()
            issue_chunk()

        def wout_win_interleaved(m_out, m_in):
            w0 = next_chunk()
            w1 = next_chunk()
            wout_wave(m_out, [0, 1], w0, w1, 0)
            wout_wave(m_out, [2, 3], w0, w1, 1)
            issue_chunk()
            issue_chunk()
            wu = next_chunk()
            wv = next_chunk()
            A_mm_all(m_in, wu, wv)
            wout_wave_b(m_out, [0, 1], 0)
            wout_wave_b(m_out, [2, 3], 1)
            wval = next_chunk()
            wgate = next_chunk()
            B_proj(m_in, wval, wgate, mid_hook=lambda: (hn2T(0), hn2T(1)))
            B_shift(m_in)
            issue_chunk()
            issue_chunk()
            for i in (2, 3):
                hn2T(i)
            A_ln(m_in)

        def ffn_down(m, hook):
            wch0 = [next_chunk() for _ in range(3)]
            banks0 = down_half(m, 0, wch0, hook)
            for i in range(4):
                b = banks0[i]
                tk.op("act", lambda e, b=b, i=i: e.activation(out=sig[:, i % 2, :], in_=psb(b), func=AF.Square,
                                                              accum_out=st[:, 7, F_SS + 2 * i:F_SS + 2 * i + 1]),
                      reads=["bank%d" % b, "stall"], writes=["sig%d" % (i % 2), "s7ss%d_0" % i])
                tk.op("act", lambda e, b=b, i=i: e.activation(out=oa[:, i, :], in_=psb(b), func=AF.Copy),
                      reads=["bank%d" % b], writes=["oa%d" % i])
                bfree(b)
                tk.op("dve", lambda e, i=i: e.tensor_tensor(out=oa[:, i, :], in0=oa[:, i, :],
                                                            in1=vecB[:, VB_GPOST2:VB_GPOST2 + 512], op=ALU.mult),
                      reads=["oa%d" % i, "vecB"], writes=["oa%d" % i])
            wch1 = [next_chunk() for _ in range(3)]
            banks1 = down_half(m, 1, wch1, hook)
            for i in range(4):
                b = banks1[i]
                tk.op("act", lambda e, b=b, i=i: e.activation(out=sig[:, i % 2, :], in_=psb(b), func=AF.Square,
                                                              accum_out=st[:, 7, F_SS + 2 * i + 1:F_SS + 2 * i + 2]),
                      reads=["bank%d" % b, "stall"], writes=["sig%d" % (i % 2), "s7ss%d_1" % i])
            sums = st[:, 7, F_SS + 8:F_SS + 12]
            pair = st[:, 7, F_SS:F_SS + 8].rearrange("p (n c) -> p n c", c=2)
            tk.op("dve", lambda e: e.tensor_tensor(out=sums, in0=pair[:, :, 0], in1=pair[:, :, 1], op=ALU.add),
                  reads=["s7ss%d_%d" % (i, h) for i in range(4) for h in range(2)], writes=["s7sum"])
            y, cres = chain(7, sums, 1.0 / D, 4, 0, ["s7sum"], "")
            for i in range(4):
                b = banks1[i]
                xres = "xt%d" % i
                tk.op("dve", lambda e, i=i: e.scalar_tensor_tensor(
                    out=oa[:, i, :], in0=oa[:, i, :], scalar=y[:, i:i + 1], in1=xt[:, i, 0:512],
                    op0=ALU.mult, op1=ALU.add), reads=[xres, "oa%d" % i, cres], writes=["oa%d" % i])
                tk.op("dve", lambda e, i=i, b=b: e.scalar_tensor_tensor(
                    out=yc[:, i, :], in0=psb(b), scalar=y[:, i:i + 1],
                    in1=vecB[:, VB_GPOST2 + 512:VB_GPOST2 + 1024],
                    op0=ALU.mult, op1=ALU.mult), reads=["bank%d" % b, cres, "vecB"], writes=["yc%d" % i])
                bfree(b)
                tk.op("pool", lambda e, i=i: e.tensor_tensor(out=yc[:, i, :], in0=yc[:, i, :],
                                                             in1=xt[:, i, 512:1024], op=ALU.add),
                      reads=[xres, "yc%d" % i], writes=["yc%d" % i])
                pending_y.append((m, i))

        p1_s0(0, 0)
        p1_s0(0, 1)
        for _ in range(NW):
            issue_chunk()
        for j in range(5):
            p1_s12(0, j)
            if j + 2 < 5:
                p1_s0(0, j + 2)
            phase1_pe(0, j)
        w_in_all(0)
        if NT > 1:
            for j in range(5):
                phase1_nonpe(1, j)
                phase1_pe(1, j)
        mixer_front_all(0)
        for i in range(4):
            T_half(i, 1)
        x_reload(0)
        wout_all(0)
        if NT > 1:
            w_in_all(1)
        for i in range(4):
            hn2T(i)

        for m in range(NT):
            nxt = m + 1 < NT
            nxt2 = m + 2 < NT
            for c in range(11):
                if nxt:
                    if c == 2:
                        B_conv(m + 1)
                    if c == 4:
                        A_sp_a(m + 1)
                        B_tok_a(m + 1)
                    if c == 7:
                        for i in range(4):
                            T_half(i, 0)
                    if c == 10:
                        for i in range(4):
                            T_half(i, 1)
                w = next_chunk()
                gu_chunk(m, c, w)
                if nxt:
                    if c == 5:
                        A_sp_b(m + 1)
                    if c == 6:
                        B_tok_b(m + 1)
                    if c == 7:
                        B_tok_c(m + 1)
                if nxt2:
                    if c == 8:
                        p1_s0(m + 2, 0)
                    if c == 9:
                        p1_s0(m + 2, 1)
                    if c == 10:
                        p1_s12(m + 2, 0)
                        p1_s0(m + 2, 2)
                issue_chunk()

            def hook(bd, m=m, nxt2=nxt2):
                if not nxt2:
                    return
                if 0 <= bd - 1 < 5:
                    phase1_pe(m + 2, bd - 1)
                if bd + 1 < 5:
                    p1_s12(m + 2, bd + 1)
                if bd + 3 < 5:
                    p1_s0(m + 2, bd + 3)
            ffn_down(m, hook)
            flush_y()
            if nxt:
                x_reload(m + 1)
                if nxt2:
                    wout_win_interleaved(m + 1, m + 2)
                else:
                    wout_all(m + 1)
                    for i in range(4):
                        hn2T(i)
        flush_y()
        if debug:
            pass
        tk.final_wait("pool", s_y + s_y2)

        with nc.Block() as block:
            @block.sync
            def _(e):
                tk.replay("sp", e)

            @block.gpsimd
            def _(e):
                tk.replay("pool", e)

            @block.scalar
            def _(e):
                tk.replay("act", e)

            @block.vector
            def _(e):
                tk.replay("dve", e)

            @block.tensor
            def _(e):
                tk.replay("pe", e)
    return nc


def _prep_shared(mix_pre_g, a_ln_g, a_ln_b, a_sp_w, a_sp_b, b_conv_w, b_conv_b, b_ln_g, b_ln_b, grp_g,
                 mix_post_g, ffn_pre_g, ffn_post_g):
    f = np.float32
    vecP = np.zeros((128, VP_N), f)
    vecP[:, VP_CB:VP_CB + 4] = np.asarray(b_conv_b[0], f).reshape(4, 128).T
    vecP[:, VP_SPB:VP_SPB + 4] = np.asarray(a_sp_b[0], f).T
    cw = np.concatenate([np.asarray(b_conv_w[0], f), np.zeros((1, 512), f)], axis=0)
    ck = cw.reshape(4, 8, 4, 4, 32)
    vecP[:, VP_CW:VP_CW + 128] = ck.transpose(0, 4, 2, 1, 3).reshape(128, 128)
    vecP[:, VP_GPRE:VP_GPRE + 8] = np.asarray(mix_pre_g[0], f).reshape(8, 128).T
    vecP[:, VP_GPRE2:VP_GPRE2 + 8] = np.asarray(ffn_pre_g[0], f).reshape(8, 128).T
    vecP[:, VP_GRP:VP_GRP + 8] = np.asarray(grp_g[0], f).reshape(8, 128).T
    vb = np.concatenate([np.asarray(v[0], f).reshape(-1) for v in
                         (mix_post_g, ffn_post_g, a_ln_g, a_ln_b, b_ln_g, b_ln_b)])
    vecB = np.ascontiguousarray(np.broadcast_to(vb[None, :], (128, VB_N)))
    wsT = np.ascontiguousarray(np.asarray(a_sp_w[0], f).transpose(2, 0, 1)).reshape(128, 512)
    ident = np.eye(128, dtype=f)
    return vecP, vecB, wsT, ident


def _halo(X, t0):
    h = np.zeros((32, X.shape[1]), X.dtype)
    if t0 % SEQ != 0:
        h[0:16] = X[t0 - 16:t0]
    if (t0 + T) % SEQ != 0:
        h[16:32] = X[t0 + T:t0 + T + 16]
    return h


_NC_CACHE = {}
MASK = np.ascontiguousarray(np.tile(np.eye(32, dtype=np.float32), (4, 1)))


def kernel(x_prompt, x_sample, mix_pre_g, w_in, a_ln_g, a_ln_b, a_sp_w, a_sp_b, b_conv_w, b_conv_b,
           b_ln_g, b_ln_b, grp_g, w_out, mix_post_g, ffn_pre_g, w_gate_up, w_down, ffn_post_g):
    f = np.float32
    xp = np.asarray(x_prompt, f)
    xs = np.asarray(x_sample, f)
    X = np.concatenate([xp.reshape(-1, D), xs.reshape(-1, D)], axis=0)
    ntot = X.shape[0]
    assert ntot == NCORES * TOK_PER_CORE
    vecP, vecB, wsT, ident = _prep_shared(mix_pre_g, a_ln_g, a_ln_b, a_sp_w, a_sp_b, b_conv_w, b_conv_b,
                                          b_ln_g, b_ln_b, grp_g, mix_post_g, ffn_pre_g, ffn_post_g)
    NT = NT_FULL
    if NT not in _NC_CACHE:
        _NC_CACHE[NT] = build(NT)
    nc = _NC_CACHE[NT]
    w_in_ = np.ascontiguousarray(np.asarray(w_in, f)[0])
    w_out_ = np.ascontiguousarray(np.asarray(w_out, f)[0])
    w_gu_ = np.ascontiguousarray(np.asarray(w_gate_up, f)[0])
    w_dn_ = np.ascontiguousarray(np.asarray(w_down, f)[0])
    in_maps = []
    for c in range(NCORES):
        t0 = c * TOK_PER_CORE
        xh = np.concatenate([_halo(X, t0 + m * T) for m in range(NT)], axis=0)
        in_maps.append({
            "x": np.ascontiguousarray(X[t0:t0 + TOK_PER_CORE]), "xh": xh,
            "w_in": w_in_, "w_out": w_out_, "w_gu": w_gu_, "w_dn": w_dn_,
            "vecP": vecP, "vecB": vecB, "wsT": wsT, "ident": ident, "mask": MASK,
        })
    res = run_bass_kernel_spmd(nc, in_maps, core_ids=list(range(NCORES)))
    Y = np.concatenate([np.asarray(r["y"], f) for r in res.results], axis=0)
    npr = xp.shape[0] * xp.shape[1]
    y_prompt = Y[:npr].reshape(xp.shape)
    y_sample = Y[npr:].reshape(xs.shape)
    return (y_prompt, y_sample)
```

```python
import numpy as np
from contextlib import ExitStack
import concourse.bass as bass
import concourse.mybir as mybir
from concourse.bass_utils import run_bass_kernel_spmd

F32 = mybir.dt.float32
BF16 = mybir.dt.bfloat16
I32 = mybir.dt.int32
AF = mybir.ActivationFunctionType
ALU = mybir.AluOpType

D = 1024
DFF = 2816
NFB = DFF // 128
T = 512
NCORES = 8
TOK_PER_CORE = 10240
NT_FULL = TOK_PER_CORE // T
SEQ = 4096
EPS = 1e-6
NW = 4
MAGIC = 1597463007.0

VB_GPOST, VB_GPOST2 = 0, 1024
VB_ALG, VB_ALB, VB_BLG, VB_BLB = 2048, 2560, 3072, 3584
VB_N = 4096
YC_ALIAS = False
VP_CB, VP_SPB, VP_CW = 0, 4, 8
VP_GPRE, VP_GPRE2, VP_GRP = 136, 144, 152
VP_N = 160


class Trk:
    def __init__(self, nc, es):
        self.nc = nc
        self.es = es
        self.engs = ("pe", "act", "dve", "pool", "sp")
        self.streams = {e: [] for e in self.engs}
        self.psem = {e: es.enter_context(nc.semaphore("prog_" + e)) for e in ("pe", "act", "dve", "pool")}
        self.cnt = {e: 0 for e in self.psem}
        self.dcnt = {}
        self.waited = {e: {} for e in self.engs}
        self.res = {}

    def new_dma_sem(self, name):
        s = self.es.enter_context(self.nc.semaphore(name))
        self.dcnt[s.num] = 0
        return s

    def _deps(self, eng, reads, writes):
        evs = {}

        def add(ev):
            if ev is None:
                return
            sem, val = ev
            if evs.get(sem.num, (None, 0))[1] < val:
                evs[sem.num] = (sem, val)

        for r in reads:
            st = self.res.get(r)
            if st:
                add(st[0])
        for w in writes:
            st = self.res.get(w)
            if st:
                if st[1]:
                    for ev in st[1].values():
                        add(ev)
                else:
                    add(st[0])
        out = []
        for key, (sem, val) in evs.items():
            if eng == "pe" and sem.num == self.psem["pe"].num:
                continue
            if self.waited[eng].get(key, 0) >= val:
                continue
            self.waited[eng][key] = val
            out.append((sem, val))
        return out

    def _update(self, ev, reads, writes):
        for r in reads:
            st = self.res.setdefault(r, [None, {}])
            st[1][ev[0].num] = ev
        for w in writes:
            self.res[w] = [ev, {}]

    def op(self, eng, fn, reads=(), writes=()):
        waits = self._deps(eng, reads, writes)
        self.cnt[eng] += 1
        ev = (self.psem[eng], self.cnt[eng])
        self.streams[eng].append((waits, fn, (self.psem[eng], 1)))
        self._update(ev, reads, writes)

    def dma(self, q, fn, sem, reads=(), writes=()):
        waits = self._deps(q, reads, writes)
        self.dcnt[sem.num] += 16
        ev = (sem, self.dcnt[sem.num])
        self.streams[q].append((waits, fn, (sem, 16)))
        self._update(ev, reads, writes)

    def final_wait(self, q, sems):
        waits = []
        for s in sems:
            waits.append((s, self.dcnt[s.num]))
        self.streams[q].append((waits, None, None))

    def replay(self, eng_name, eng):
        for waits, fn, inc in self.streams[eng_name]:
            for sem, val in waits:
                eng.wait_ge(sem, val)
            if fn is None:
                continue
            ins = fn(eng)
            if inc is not None:
                ins.then_inc(inc[0], inc[1])


def build(NT, debug=False):
    nc = bass.Bass("TRN2", target_bir_lowering=False)
    ntok = NT * T
    x_d = nc.dram_tensor("x", [ntok, D], F32, kind="ExternalInput").ap()
    xh_d = nc.dram_tensor("xh", [NT * 32, D], F32, kind="ExternalInput").ap()
    w_in_d = nc.dram_tensor("w_in", [D, 2 * D], F32, kind="ExternalInput").ap()
    w_out_d = nc.dram_tensor("w_out", [D, D], F32, kind="ExternalInput").ap()
    w_gu_d = nc.dram_tensor("w_gu", [D, 2 * DFF], F32, kind="ExternalInput").ap()
    w_dn_d = nc.dram_tensor("w_dn", [DFF, D], F32, kind="ExternalInput").ap()
    vecP_d = nc.dram_tensor("vecP", [128, VP_N], F32, kind="ExternalInput").ap()
    vecB_d = nc.dram_tensor("vecB", [128, VB_N], F32, kind="ExternalInput").ap()
    wsT_d = nc.dram_tensor("wsT", [128, 4 * 128], F32, kind="ExternalInput").ap()
    ident_d = nc.dram_tensor("ident", [128, 128], F32, kind="ExternalInput").ap()
    mask_d = nc.dram_tensor("mask", [128, 32], F32, kind="ExternalInput").ap()
    y_d = nc.dram_tensor("y", [ntok, D], F32, kind="ExternalOutput").ap()
    win_b = nc.dram_tensor("win_b", [D, 2 * D], BF16, kind="Internal").ap()
    wout_b = nc.dram_tensor("wout_b", [D, D], BF16, kind="Internal").ap()
    wgu_b = nc.dram_tensor("wgu_b", [D, 11, 512], BF16, kind="Internal").ap()
    wdn_b = nc.dram_tensor("wdn_b", [DFF, D], BF16, kind="Internal").ap()

    es = ExitStack()
    with es:
        def sb(name, shape, dt):
            return es.enter_context(nc.sbuf_tensor(name, shape, dt))

        xin = sb("xin", [128, 2, D], F32)
        xt = sb("xt", [128, 4, D], F32)
        tb = sb("tb", [128, 4, D], BF16)
        aT = sb("aT", [128, 2, 8, T + 32], BF16)
        u_sb = sb("u_sb", [128, 4, 512], F32)
        vn = sb("vn", [128, 4, 512], BF16)
        sig = sb("sig", [128, 2, 512], F32)
        sil = sig
        zbuf = sb("zbuf", [128, 4, T + 32], BF16)
        oa = sb("oa", [128, 4, 512], F32)
        v_sb = oa
        ob = sb("ob", [128, 4, 512], F32)
        actT = sb("actT", [128, NFB, T], BF16)
        mixT = sb("mixT", [128, 8, T], BF16)
        yc = sb("yc", [128, 4, 512], F32) if not YC_ALIAS else None
        st = sb("st", [128, 8, 192], F32)
        identf = sb("identf", [128, 128], F32)
        identb = sb("identb", [128, 128], BF16)
        wsT_b = sb("wsT_b", [128, 4, 128], BF16)
        diagK = sb("diagK", [128, 128, 32], BF16)
        ZS = sb("ZS", [128, 4, 4, 520], BF16)
        maskf = sb("maskf", [128, 32], F32)
        vecP = sb("vecP_s", [128, VP_N], F32)
        vecB = sb("vecB_s", [128, VB_N], F32)
        wbuf = sb("wbuf", [128, NW, 8, 512], BF16)
        ps = [es.enter_context(nc.psum_tensor("ps%d" % i, [128, 512], F32)) for i in range(8)]
        if YC_ALIAS:
            actT_f = actT.bitcast(F32)

            def yc_blk(j):
                return actT_f[:, 8 + 2 * j:10 + 2 * j, :].rearrange("p a b -> p (a b)")

            def yc_tile(j, i):
                return actT_f[:, 8 + 2 * j + i // 2, (i % 2) * 128:(i % 2) * 128 + 128]
        else:
            def yc_blk(j):
                return yc[:, j, :]

            def yc_tile(j, i):
                return yc[:, j, i * 128:(i + 1) * 128]

        tk = Trk(nc, es)
        s_cast = {k: tk.new_dma_sem("s_cast_" + k) for k in "abcde"}
        s_const = [tk.new_dma_sem("s_const%d" % i) for i in range(5)]
        s_zs = tk.new_dma_sem("s_zs")
        s_xin = [tk.new_dma_sem("s_xin%d" % i) for i in range(2)]
        s_xt = [tk.new_dma_sem("s_xt%d" % i) for i in range(4)]
        s_y = [tk.new_dma_sem("s_y%d" % i) for i in range(4)]
        s_y2 = [tk.new_dma_sem("s_yb%d" % i) for i in range(4)]
        s_w = [tk.new_dma_sem("s_w%d" % i) for i in range(NW)]

        free_banks = list(range(8))

        def balloc():
            assert free_banks, "PSUM banks exhausted at emission"
            return free_banks.pop(0)

        def bfree(b):
            free_banks.append(b)

        def psb(b):
            return ps[b][:]

        ps16 = [p.bitcast(BF16) for p in ps]

        def psb16(b):
            return ps16[b][:]

        F_BN, F_MV, F_A, F_Y, F_T, F_SS = 0, 96, 128, 144, 160, 176

        def chain(slot, src_ap, scale, n, c0, reads, tag):
            res = "s%dch%s" % (slot, tag)
            a = st[:, slot, F_A + c0:F_A + c0 + n]
            y = st[:, slot, F_Y + c0:F_Y + c0 + n]
            t = st[:, slot, F_T + c0:F_T + c0 + n]
            tk.op("dve", lambda e: e.tensor_scalar(out=a, in0=src_ap, scalar1=scale, scalar2=EPS,
                                                   op0=ALU.mult, op1=ALU.add), reads=list(reads) + [res], writes=[res])
            tk.op("dve", lambda e: e.tensor_scalar(out=y.bitcast(I32), in0=a.bitcast(I32), scalar1=-0.5,
                                                   scalar2=MAGIC, op0=ALU.mult, op1=ALU.add),
                  reads=[res], writes=[res])
            for _ in range(3):
                tk.op("dve", lambda e: e.scalar_tensor_tensor(out=t, in0=y, scalar=-0.5, in1=y,
                                                              op0=ALU.mult, op1=ALU.mult),
                      reads=[res], writes=[res])
                tk.op("dve", lambda e: e.tensor_tensor(out=t, in0=t, in1=a, op=ALU.mult),
                      reads=[res], writes=[res])
                tk.op("dve", lambda e: e.scalar_tensor_tensor(out=y, in0=t, scalar=1.5, in1=y,
                                                              op0=ALU.add, op1=ALU.mult),
                      reads=[res], writes=[res])
            return y, res

        s_cast_win = [tk.new_dma_sem("s_cast_win%d" % c) for c in range(4)]
        for c in range(4):
            for hb in range(2):
                r0 = hb * 512
                tk.dma("pool", lambda e, r0=r0, c=c: e.dma_start(
                    out=win_b[r0:r0 + 512, c * 512:(c + 1) * 512], in_=w_in_d[r0:r0 + 512, c * 512:(c + 1) * 512]),
                    s_cast_win[c], writes=["wscr_a%d_%d" % (c, hb)])
        for rb in range(8):
            r0 = rb * 128
            tk.dma("pool", lambda e, r0=r0: e.dma_start(out=wout_b[r0:r0 + 128, :], in_=w_out_d[r0:r0 + 128, :]),
                   s_cast["b"], writes=["wscr%d_b" % rb])
        for rb in range(8):
            r0 = rb * 128
            tk.dma("pool", lambda e, r0=r0: e.dma_start(
                out=wgu_b[r0:r0 + 128, :, 0:256],
                in_=w_gu_d[r0:r0 + 128, 0:DFF].rearrange("p (c f) -> p c f", f=256)),
                s_cast["c"], writes=["wscr%d_c" % rb])
            tk.dma("pool", lambda e, r0=r0: e.dma_start(
                out=wgu_b[r0:r0 + 128, :, 256:512],
                in_=w_gu_d[r0:r0 + 128, DFF:2 * DFF].rearrange("p (c f) -> p c f", f=256)),
                s_cast["d"], writes=["wscr%d_d" % rb])
        for rb in range(NFB):
            r0 = rb * 128
            tk.dma("pool", lambda e, r0=r0: e.dma_start(out=wdn_b[r0:r0 + 128, :], in_=w_dn_d[r0:r0 + 128, :]),
                   s_cast["e"], writes=["wscr%d_e" % rb])
        WSCR = {"win0": ["wscr_a0_0", "wscr_a0_1"], "win1": ["wscr_a1_0", "wscr_a1_1"],
                "win2": ["wscr_a2_0", "wscr_a2_1"], "win3": ["wscr_a3_0", "wscr_a3_1"],
                "wout": ["wscr%d_b" % i for i in range(8)],
                "wgu": ["wscr%d_c" % i for i in range(8)] + ["wscr%d_d" % i for i in range(8)],
                "wdn": ["wscr%d_e" % i for i in range(NFB)]}

        tk.dma("sp", lambda e: e.dma_start(out=vecP[:], in_=vecP_d), s_const[0], writes=["vecP"])
        tk.dma("sp", lambda e: e.dma_start(out=vecB[:], in_=vecB_d), s_const[1], writes=["vecB"])
        tk.dma("sp", lambda e: e.dma_start(out=identf[:], in_=ident_d), s_const[2], writes=["identf"])
        tk.dma("sp", lambda e: e.dma_start(out=ob[:, 0, :], in_=wsT_d), s_const[3], writes=["ob0"])
        tk.op("dve", lambda e: e.tensor_copy(out=identb[:], in_=identf[:]), reads=["identf"], writes=["identb"])
        tk.op("dve", lambda e: e.tensor_copy(out=wsT_b[:].rearrange("p h q -> p (h q)"), in_=ob[:, 0, :]),
              reads=["ob0"], writes=["wsT_b"])
        tk.dma("sp", lambda e: e.dma_start(out=maskf[:], in_=mask_d), s_const[4], writes=["maskf"])
        tk.op("dve", lambda e: e.tensor_scalar(out=vecP[:, VP_CW:VP_CW + 128], in0=vecP[:, VP_CW:VP_CW + 128],
                                               scalar1=0.5, scalar2=None, op0=ALU.mult),
              reads=["vecP"], writes=["vecP"])
        for jk in range(128):
            def _f(e, jk=jk):
                return e.tensor_scalar(out=diagK[:, jk, :], in0=maskf[:], scalar1=vecP[:, VP_CW + jk:VP_CW + jk + 1],
                                       scalar2=None, op0=ALU.mult)
            tk.op("dve", _f, reads=["maskf", "vecP"], writes=["diag_jk%d" % jk])
        tk.op("dve", lambda e: e.memset(st[:], 0.0), reads=["diag_jk%d" % jk for jk in range(128)],
              writes=["diagall", "stall"])

        WIN4 = [("win", c) for c in range(4)]
        PLAN = list(WIN4) + [("wout", 0), ("wout", 1)]
        if NT > 1:
            PLAN += WIN4
        for m_ in range(NT):
            PLAN += [("wgu", c) for c in range(11)] + [("wdn", c) for c in range(6)]
            if m_ + 1 < NT:
                PLAN += [("wout", 0), ("wout", 1)]
            if m_ + 2 < NT:
                PLAN += WIN4
        wstate = {"next_issue": 0, "total": len(PLAN)}

        def issue_chunk():
            g = wstate["next_issue"]
            if g >= wstate["total"]:
                return
            wstate["next_issue"] += 1
            kind, c = PLAN[g]
            slot = g % NW
            res = "wslot%d" % slot
            if kind == "win":
                src = win_b.rearrange("(k p) f -> p k f", p=128)[:, :, c * 512:(c + 1) * 512]
                dst = wbuf[:, slot, :, :]
            elif kind == "wout":
                src = wout_b.rearrange("(k p) f -> p k f", p=128)[:, :, c * 512:(c + 1) * 512]
                dst = wbuf[:, slot, :, :]
            elif kind == "wgu":
                src = wgu_b.rearrange("(k p) c f -> p k c f", p=128)[:, :, c, :]
                dst = wbuf[:, slot, :, :]
            else:
                half, kg = c // 3, c % 3
                nk = 8 if kg < 2 else 6
                src = wdn_b.rearrange("(k p) f -> p k f", p=128)[:, kg * 8:kg * 8 + nk, half * 512:(half + 1) * 512]
                dst = wbuf[:, slot, 0:nk, :]
            wkey = ("win%d" % c) if kind == "win" else kind
            tk.dma("sp", lambda e: e.dma_start(out=dst, in_=src), s_w[slot], reads=WSCR[wkey], writes=[res])

        cons = {"n": 0}

        def next_chunk():
            g = cons["n"]
            cons["n"] += 1
            return g % NW, "wslot%d" % (g % NW)

        AT_P = ["aP0", "aP1", "aP2", "aP3"]
        AT_Q = ["aQ0", "aQ1", "aQ2", "aQ3"]

        def p1_s0(m, j):
            rows = 128 if j < 4 else 32
            xs = j % 2
            if j < 4:
                src = x_d[m * T + j * 128: m * T + (j + 1) * 128, :]
            else:
                src = xh_d[m * 32:(m + 1) * 32, :]
            tk.dma("sp", lambda e: e.dma_start(out=xin[0:rows, xs, :], in_=src), s_xin[xs], writes=["xin%d" % xs])

        def p1_s12(m, j):
            rows = 128 if j < 4 else 32
            xs = j % 2
            xres = "xin%d" % xs
            sres = "s0ss%d" % j
            tk.op("act", lambda e: e.activation(out=tb[0:rows, j % 4, :], in_=xin[0:rows, xs, :], func=AF.Square,
                                                accum_out=st[0:rows, 0, F_SS + j:F_SS + j + 1]),
                  reads=[xres, "stall"], writes=["tb%da" % (j % 4), "tb%db" % (j % 4), sres])
            y, cres = chain(0, st[:, 0, F_SS + j:F_SS + j + 1], 1.0 / D, 1, j, [sres], "j%d" % j)
            ts = j % 4
            tres = ["tb%da" % ts, "tb%db" % ts]
            tk.op("dve", lambda e: e.tensor_scalar(out=tb[0:rows, ts, :], in0=xin[0:rows, xs, :],
                                                   scalar1=y[0:rows, :], scalar2=None, op0=ALU.mult),
                  reads=[xres, cres], writes=tres)

        def phase1_nonpe(m, j):
            p1_s0(m, j)
            p1_s12(m, j)

        def phase1_pe(m, j):
            rows = 128 if j < 4 else 32
            ts = j % 4
            tres = ["tb%da" % ts, "tb%db" % ts]
            b = balloc()
            bres = "bank%d" % b

            def _tr(e):
                ins = None
                for k in range(8):
                    ins = e.transpose(psb16(b)[:, k * 128:k * 128 + rows],
                                      tb[0:rows, ts, k * 128:(k + 1) * 128], identb[0:rows, 0:rows])
                return ins
            tk.op("pe", _tr, reads=tres + ["identb"], writes=[bres])
            ares = "aP%d" % j

            def _ev(e):
                ins = None
                for k in range(8):
                    ins = e.activation(out=aT[:, 0, k, j * 128:j * 128 + rows],
                                       in_=psb16(b)[:, k * 128:k * 128 + rows], func=AF.Copy,
                                       scale=vecP[:, VP_GPRE + k:VP_GPRE + k + 1])
                return ins
            tk.op("act", _ev, reads=[bres, "vecP"], writes=[ares])
            bfree(b)

        def A_mm_all(m, wu, wv):
            for which, (ws, wres) in enumerate((wu, wv)):
                for i in range(4):
                    b = balloc()

                    def _mm(e, ws=ws, b=b, i=i):
                        ins = None
                        for k in range(8):
                            ins = e.matmul(psb(b), lhsT=aT[:, 0, k, i * 128:(i + 1) * 128], rhs=wbuf[:, ws, k, :],
                                           start=(k == 0), stop=(k == 7))
                        return ins
                    tk.op("pe", _mm, reads=["aP%d" % i, wres], writes=["bank%d" % b])
                    dst = u_sb if which == 0 else v_sb
                    dres = ("u%d" if which == 0 else "oa%d") % i
                    tk.op("act", lambda e, b=b, dst=dst, i=i: e.activation(out=dst[:, i, :], in_=psb(b), func=AF.Gelu),
                          reads=["bank%d" % b], writes=[dres])
                    bfree(b)
                if which == 0:
                    issue_chunk()
            issue_chunk()

        def A_ln(m):
            for i in range(4):
                def _bn(e, i=i):
                    ins = None
                    for h in range(4):
                        ins = e.bn_stats(out=st[:, 1, F_BN + (i * 4 + h) * 6:F_BN + (i * 4 + h + 1) * 6],
                                         in_=v_sb[:, i, h * 128:(h + 1) * 128])
                    return ins
                tk.op("dve", _bn, reads=["oa%d" % i, "stall"], writes=["s1bn%d" % i])

                def _ag(e, i=i):
                    ins = None
                    for h in range(4):
                        ins = e.bn_aggr(out=st[:, 1, F_MV + (i * 4 + h) * 2:F_MV + (i * 4 + h + 1) * 2],
                                        in_=st[:, 1, F_BN + (i * 4 + h) * 6:F_BN + (i * 4 + h + 1) * 6])
                    return ins
                tk.op("dve", _ag, reads=["s1bn%d" % i], writes=["s1mv%d" % i])

            var_ap = st[:, 1, F_MV:F_MV + 32].rearrange("p (n c) -> p n c", c=2)[:, :, 1]
            y, cres = chain(1, var_ap, 1.0, 16, 0, ["s1mv%d" % i for i in range(4)], "")
            for i in range(4):
                def _nrm(e, i=i):
                    ins = None
                    for h in range(4):
                        c = i * 4 + h
                        ins = e.tensor_scalar(out=v_sb[:, i, h * 128:(h + 1) * 128],
                                              in0=v_sb[:, i, h * 128:(h + 1) * 128],
                                              scalar1=st[:, 1, F_MV + 2 * c:F_MV + 2 * c + 1], scalar2=y[:, c:c + 1],
                                              op0=ALU.subtract, op1=ALU.mult)
                    return ins
                tk.op("dve", _nrm, reads=["oa%d" % i, cres, "s1mv%d" % i], writes=["oa%d" % i])
                tk.op("pool", lambda e, i=i: e.tensor_tensor(out=v_sb[:, i, :], in0=v_sb[:, i, :],
                                                             in1=vecB[:, VB_ALG:VB_ALG + 512], op=ALU.mult),
                      reads=["oa%d" % i, "vecB"], writes=["oa%d" % i])
                tk.op("pool", lambda e, i=i: e.tensor_tensor(out=vn[:, i, :], in0=v_sb[:, i, :],
                                                             in1=vecB[:, VB_ALB:VB_ALB + 512], op=ALU.add),
                      reads=["oa%d" % i, "vecB"], writes=["vn%d" % i])

        def A_sp_a(m):
            for i in range(4):
                b = balloc()

                def _mm(e, b=b, i=i):
                    ins = None
                    for h in range(4):
                        ins = e.matmul(psb(b)[:, h * 128:(h + 1) * 128], lhsT=wsT_b[:, h, :],
                                       rhs=vn[:, i, h * 128:(h + 1) * 128], start=True, stop=True)
                    return ins
                tk.op("pe", _mm, reads=["vn%d" % i, "wsT_b"], writes=["bank%d" % b])

                def _oa(e, b=b, i=i):
                    ins = None
                    for h in range(4):
                        ins = e.scalar_tensor_tensor(out=oa[:, i, h * 128:(h + 1) * 128],
                                                     in0=psb(b)[:, h * 128:(h + 1) * 128],
                                                     scalar=vecP[:, VP_SPB + h:VP_SPB + h + 1],
                                                     in1=u_sb[:, i, h * 128:(h + 1) * 128],
                                                     op0=ALU.add, op1=ALU.mult)
                    return ins
                tk.op("dve", _oa, reads=["bank%d" % b, "u%d" % i, "vecP"], writes=["oa%d" % i])
                bfree(b)
                tk.op("act", lambda e, i=i: e.activation(out=tb[:, i, 0:512], in_=oa[:, i, :], func=AF.Square,
                                                         accum_out=st[:, 2, F_SS + i:F_SS + i + 1]),
                      reads=["oa%d" % i, "stall"], writes=["tb%da" % i, "s2ss%d" % i])

        def A_sp_b(m):
            y, cres = chain(2, st[:, 2, F_SS:F_SS + 4], 1.0 / 512, 4, 0, ["s2ss%d" % i for i in range(4)], "")
            for i in range(4):
                tk.op("dve", lambda e, i=i: e.tensor_scalar(out=tb[:, i, 0:512], in0=oa[:, i, :],
                                                            scalar1=y[:, i:i + 1], scalar2=None, op0=ALU.mult),
                      reads=["oa%d" % i, cres], writes=["tb%da" % i])

        def T_half(i, half):
            b = balloc()
            c0 = half * 512
            tres = "tb%d%s" % (i, "ab"[half])

            def _tr(e):
                ins = None
                for k in range(4):
                    ins = e.transpose(psb16(b)[:, k * 128:(k + 1) * 128],
                                      tb[:, i, c0 + k * 128:c0 + (k + 1) * 128], identb[:])
                return ins
            tk.op("pe", _tr, reads=[tres, "identb"], writes=["bank%d" % b])

            def _ev(e):
                ins = None
                for k in range(4):
                    kk = half * 4 + k
                    ins = e.activation(out=mixT[:, kk, i * 128:(i + 1) * 128],
                                       in_=psb16(b)[:, k * 128:(k + 1) * 128], func=AF.Copy,
                                       scale=vecP[:, VP_GRP + kk:VP_GRP + kk + 1])
                return ins
            tk.op("act", _ev, reads=["bank%d" % b, "vecP"], writes=["mixT%d_%d" % (i, half)])
            bfree(b)

        def B_proj(m, wval, wgate, mid_hook=None):
            (sv_, rv), (sg_, rg) = wval, wgate
            bh = balloc()

            def _mh(e):
                ins = None
                for j in range(4):
                    for k in range(8):
                        ins = e.matmul(psb(bh)[:, j * 32:(j + 1) * 32], lhsT=wbuf[:, sv_, k, j * 128:(j + 1) * 128],
                                       rhs=aT[:, 0, k, T:T + 32], start=(k == 0), stop=(k == 7))
                for j in range(4):
                    for k in range(8):
                        ins = e.matmul(psb(bh)[:, 128 + j * 32:128 + (j + 1) * 32],
                                       lhsT=wbuf[:, sg_, k, j * 128:(j + 1) * 128],
                                       rhs=aT[:, 0, k, T:T + 32], start=(k == 0), stop=(k == 7))
                return ins
            tk.op("pe", _mh, reads=["aP4", rv, rg], writes=["bank%d" % bh])
            tk.op("act", lambda e: e.activation(out=sig[:, 0, 0:128], in_=psb(bh)[:, 128:256],
                                                func=AF.Tanh, scale=0.5),
                  reads=["bank%d" % bh], writes=["sig0"])
            for side in range(2):
                def _zh(e, side=side):
                    dst = zbuf[:, :, 0:16] if side == 0 else zbuf[:, :, T + 16:T + 32]
                    a = sig[:, 0, 0:128].rearrange("p (j t) -> p j t", j=4)[:, :, side * 16:side * 16 + 16]
                    bb = psb(bh)[:, 0:128].rearrange("p (j t) -> p j t", j=4)[:, :, side * 16:side * 16 + 16]
                    return e.scalar_tensor_tensor(out=dst, in0=a, scalar=1.0, in1=bb, op0=ALU.add, op1=ALU.mult)
                tk.op("dve", _zh, reads=["bank%d" % bh, "sig0"], writes=["zh%d" % side])
            bfree(bh)
            for j in range(4):
                bv = balloc()
                bg = balloc()

                def _mv(e, bv=bv, j=j):
                    ins = None
                    for k in range(8):
                        ins = e.matmul(psb(bv), lhsT=wbuf[:, sv_, k, j * 128:(j + 1) * 128], rhs=aT[:, 0, k, 0:T],
                                       start=(k == 0), stop=(k == 7))
                    return ins

                def _mg(e, bg=bg, j=j):
                    ins = None
                    for k in range(8):
                        ins = e.matmul(psb(bg), lhsT=wbuf[:, sg_, k, j * 128:(j + 1) * 128], rhs=aT[:, 0, k, 0:T],
                                       start=(k == 0), stop=(k == 7))
                    return ins
                tk.op("pe", _mg, reads=AT_P + [rg], writes=["bank%d" % bg])
                tk.op("pe", _mv, reads=AT_P + [rv], writes=["bank%d" % bv])
                ss = (j + 1) % 2
                tk.op("act", lambda e, bg=bg, ss=ss: e.activation(out=sig[:, ss, :], in_=psb(bg), func=AF.Tanh,
                                                                  scale=0.5),
                      reads=["bank%d" % bg], writes=["sig%d" % ss])
                bfree(bg)
                tk.op("dve", lambda e, bv=bv, ss=ss, j=j: e.scalar_tensor_tensor(
                    out=zbuf[:, j, 16:16 + T], in0=sig[:, ss, :], scalar=1.0, in1=psb(bv),
                    op0=ALU.add, op1=ALU.mult), reads=["bank%d" % bv, "sig%d" % ss], writes=["z%d" % j])
                bfree(bv)
                if j == 1 and mid_hook is not None:
                    mid_hook()

        ZS_RES = ["zs%d_%d" % (g, jj) for g in range(4) for jj in range(4)]

        def B_shift(m):
            for g in range(4):
                for jj in range(4):
                    tk.dma("pool", lambda e, g=g, jj=jj: e.dma_start(
                        out=ZS[32 * jj:32 * jj + 32, g, :, :], in_=zbuf[32 * g:32 * g + 32, :, 8 * jj:8 * jj + 520]),
                        s_zs, reads=["z0", "z1", "z2", "z3", "zh0", "zh1"], writes=["zs%d_%d" % (g, jj)])

        def B_conv(m):
            for j in range(4):
                b = balloc()

                def _cv(e, b=b, j=j):
                    ins = None
                    for mm in range(8):
                        for g in range(4):
                            ins = e.matmul(ps[b][32 * g:32 * g + 32, :], lhsT=diagK[:, j * 32 + mm * 4 + g, :],
                                           rhs=ZS[:, g, j, mm + 1:mm + 1 + T], start=(mm == 0), stop=(mm == 7),
                                           tile_position=(0, 32 * g))
                    return ins
                tk.op("pe", _cv, reads=ZS_RES + ["diagall"], writes=["bank%d" % b])
                tk.op("act", lambda e, b=b, j=j: e.activation(out=yc_blk(j), in_=psb(b), func=AF.Identity,
                                                              bias=vecP[:, VP_CB + j:VP_CB + j + 1]),
                      reads=["bank%d" % b, "vecP"], writes=["yc%d" % j])
                bfree(b)

        def B_tok_a(m):
            banks = []
            for i in range(4):
                b = balloc()
                banks.append(b)

                def _tr(e, b=b, i=i):
                    ins = None
                    for j in range(4):
                        ins = e.transpose(psb(b)[:, j * 128:(j + 1) * 128], yc_tile(j, i), identf[:])
                    return ins
                tk.op("pe", _tr, reads=["yc0", "yc1", "yc2", "yc3", "identf"], writes=["bank%d" % b])

                def _bn(e, b=b, i=i):
                    ins = None
                    for h in range(4):
                        ins = e.bn_stats(out=st[:, 3, F_BN + (i * 4 + h) * 6:F_BN + (i * 4 + h + 1) * 6],
                                         in_=psb(b)[:, h * 128:(h + 1) * 128])
                    return ins
                tk.op("dve", _bn, reads=["bank%d" % b, "stall"], writes=["s3bn%d" % i])

                def _ag(e, i=i):
                    ins = None
                    for h in range(4):
                        ins = e.bn_aggr(out=st[:, 3, F_MV + (i * 4 + h) * 2:F_MV + (i * 4 + h + 1) * 2],
                                        in_=st[:, 3, F_BN + (i * 4 + h) * 6:F_BN + (i * 4 + h + 1) * 6])
                    return ins
                tk.op("dve", _ag, reads=["s3bn%d" % i], writes=["s3mv%d" % i])
            var_ap = st[:, 3, F_MV:F_MV + 32].rearrange("p (n c) -> p n c", c=2)[:, :, 1]
            y, cres = chain(3, var_ap, 1.0, 16, 0, ["s3mv%d" % i for i in range(4)], "")
            for i in range(4):
                b = banks[i]

                def _nrm(e, b=b, i=i):
                    ins = None
                    for h in range(4):
                        c = i * 4 + h
                        ins = e.tensor_scalar(out=ob[:, i, h * 128:(h + 1) * 128], in0=psb(b)[:, h * 128:(h + 1) * 128],
                                              scalar1=st[:, 3, F_MV + 2 * c:F_MV + 2 * c + 1], scalar2=y[:, c:c + 1],
                                              op0=ALU.subtract, op1=ALU.mult)
                    return ins
                tk.op("dve", _nrm, reads=["bank%d" % b, cres, "s3mv%d" % i], writes=["ob%d" % i])
                bfree(b)
                geng = "pool"
                tk.op(geng, lambda e, i=i: e.tensor_tensor(out=ob[:, i, :], in0=ob[:, i, :],
                                                           in1=vecB[:, VB_BLG:VB_BLG + 512], op=ALU.mult),
                      reads=["ob%d" % i, "vecB"], writes=["ob%d" % i])
                tk.op(geng, lambda e, i=i: e.tensor_tensor(out=ob[:, i, :], in0=ob[:, i, :],
                                                           in1=vecB[:, VB_BLB:VB_BLB + 512], op=ALU.add),
                      reads=["ob%d" % i, "vecB"], writes=["ob%d" % i])

        def B_tok_b(m):
            for i in range(4):
                tk.op("act", lambda e, i=i: e.activation(out=ob[:, i, :], in_=ob[:, i, :], func=AF.Silu),
                      reads=["ob%d" % i], writes=["ob%d" % i])
                tk.op("act", lambda e, i=i: e.activation(out=tb[:, i, 512:1024], in_=ob[:, i, :], func=AF.Square,
                                                         accum_out=st[:, 4, F_SS + i:F_SS + i + 1]),
                      reads=["ob%d" % i, "stall"], writes=["tb%db" % i, "s4ss%d" % i])

        def B_tok_c(m):
            y2, c2 = chain(4, st[:, 4, F_SS:F_SS + 4], 1.0 / 512, 4, 0, ["s4ss%d" % i for i in range(4)], "")
            for i in range(4):
                tk.op("dve", lambda e, i=i: e.tensor_scalar(out=tb[:, i, 512:1024], in0=ob[:, i, :],
                                                            scalar1=y2[:, i:i + 1], scalar2=None, op0=ALU.mult),
                      reads=["ob%d" % i, c2], writes=["tb%db" % i])

        def x_reload(m):
            for i in range(4):
                tk.dma("act", lambda e, i=i: e.dma_start(out=xt[:, i, :],
                                                         in_=x_d[m * T + i * 128:m * T + (i + 1) * 128, :]),
                       s_xt[i], writes=["xt%d" % i])

        def wout_wave(m, tiles, w0, w1, wave):
            tmpbuf = ob if wave == 0 else yc
            tmpname = "ob%d" if wave == 0 else "yc%d"
            for i in tiles:
                for half, (ws, wres) in enumerate((w0, w1)):
                    b = balloc()
                    q = 2 * (i % 2) + half

                    def _mm(e, ws=ws, b=b, i=i):
                        ins = None
                        for k in range(8):
                            ins = e.matmul(psb(b), lhsT=mixT[:, k, i * 128:(i + 1) * 128], rhs=wbuf[:, ws, k, :],
                                           start=(k == 0), stop=(k == 7))
                        return ins
                    tk.op("pe", _mm, reads=["mixT%d_0" % i, "mixT%d_1" % i, wres], writes=["bank%d" % b])
                    tk.op("act", lambda e, b=b, q=q: e.activation(out=tmpbuf[:, q, :], in_=psb(b), func=AF.Copy),
                          reads=["bank%d" % b], writes=[tmpname % q])
                    bfree(b)
                    tk.op("act", lambda e, q=q, i=i, half=half: e.activation(
                        out=tb[:, i, half * 512:(half + 1) * 512], in_=tmpbuf[:, q, :], func=AF.Square,
                        accum_out=st[:, 5, F_SS + 2 * i + half:F_SS + 2 * i + half + 1]),
                        reads=[tmpname % q, "stall"], writes=["tb%d%s" % (i, "ab"[half]), "s5ss%d_%d" % (i, half)])
                    tk.op("pool", lambda e, q=q, half=half: e.tensor_tensor(
                        out=tmpbuf[:, q, :], in0=tmpbuf[:, q, :],
                        in1=vecB[:, VB_GPOST + half * 512:VB_GPOST + (half + 1) * 512], op=ALU.mult),
                        reads=[tmpname % q, "vecB"], writes=[tmpname % q])

        def wout_wave_b(m, tiles, wave):
            tmpbuf = ob if wave == 0 else yc
            tmpname = "ob%d" if wave == 0 else "yc%d"
            i0 = tiles[0]
            n = len(tiles)
            sres = ["s5ss%d_%d" % (i, h) for i in tiles for h in range(2)]
            sums = st[:, 5, F_SS + 8 + i0:F_SS + 8 + i0 + n]
            pair = st[:, 5, F_SS + 2 * i0:F_SS + 2 * i0 + 2 * n].rearrange("p (n c) -> p n c", c=2)
            tk.op("dve", lambda e: e.tensor_tensor(out=sums, in0=pair[:, :, 0], in1=pair[:, :, 1], op=ALU.add),
                  reads=sres, writes=["s5sum%d" % wave])
            y, cres = chain(5, sums, 1.0 / D, n, i0, ["s5sum%d" % wave], "w%d" % wave)
            for ii, i in enumerate(tiles):
                for half in range(2):
                    q = 2 * (i % 2) + half
                    tmp = tmpbuf[:, q, :]
                    tres = tmpname % q
                    tk.op("dve", lambda e, half=half, i=i, ii=ii, tmp=tmp: e.scalar_tensor_tensor(
                        out=xt[:, i, half * 512:(half + 1) * 512], in0=tmp, scalar=y[:, ii:ii + 1],
                        in1=xt[:, i, half * 512:(half + 1) * 512], op0=ALU.mult, op1=ALU.add),
                        reads=["xt%d" % i, tres, cres], writes=["xt%d" % i])
                tk.op("dve", lambda e, i=i: e.scalar_tensor_tensor(
                    out=tb[:, i, :], in0=xt[:, i, :], scalar=1.0, in1=xt[:, i, :], op0=ALU.mult, op1=ALU.mult,
                    accum_out=st[:, 6, F_SS + i:F_SS + i + 1]),
                    reads=["xt%d" % i, "stall"], writes=["s6ss%d" % i, "tb%da" % i, "tb%db" % i])
            y2, c2 = chain(6, st[:, 6, F_SS + i0:F_SS + i0 + n], 1.0 / D, n, i0, ["s6ss%d" % i for i in tiles],
                           "w%d" % wave)
            for ii, i in enumerate(tiles):
                tk.op("dve", lambda e, i=i, ii=ii: e.tensor_scalar(out=tb[:, i, :], in0=xt[:, i, :],
                                                                   scalar1=y2[:, ii:ii + 1], scalar2=None, op0=ALU.mult),
                      reads=["xt%d" % i, c2], writes=["tb%da" % i, "tb%db" % i])

        def hn2T(i):
            b = balloc()

            def _tr(e):
                ins = None
                for k in range(8):
                    ins = e.transpose(psb16(b)[:, k * 128:(k + 1) * 128], tb[:, i, k * 128:(k + 1) * 128], identb[:])
                return ins
            tk.op("pe", _tr, reads=["tb%da" % i, "tb%db" % i, "identb"], writes=["bank%d" % b])

            g_bc = bass.AP(vecP, VP_GPRE2, [[VP_N, 128], [1, 8], [0, 128]])

            def _ev(e):
                return e.tensor_tensor(out=aT[:, 1, :, i * 128:(i + 1) * 128],
                                       in0=psb16(b).rearrange("p (k t) -> p k t", k=8), in1=g_bc, op=ALU.mult)
            tk.op("dve", _ev, reads=["bank%d" % b, "vecP"], writes=["aQ%d" % i])
            bfree(b)

        def gu_chunk(m, c, w):
            ws, wres = w
            for jj in range(2):
                j = 2 * c + jj
                bg = balloc()
                bu = balloc()

                def _mg(e, bg=bg, jj=jj):
                    ins = None
                    for k in range(8):
                        ins = e.matmul(psb(bg), lhsT=wbuf[:, ws, k, jj * 128:(jj + 1) * 128], rhs=aT[:, 1, k, 0:T],
                                       start=(k == 0), stop=(k == 7))
                    return ins

                def _mu(e, bu=bu, jj=jj):
                    ins = None
                    for k in range(8):
                        ins = e.matmul(psb(bu), lhsT=wbuf[:, ws, k, 256 + jj * 128:256 + (jj + 1) * 128],
                                       rhs=aT[:, 1, k, 0:T], start=(k == 0), stop=(k == 7))
                    return ins
                tk.op("pe", _mg, reads=AT_Q + [wres], writes=["bank%d" % bg])
                tk.op("pe", _mu, reads=AT_Q + [wres], writes=["bank%d" % bu])
                ss = j % 2
                tk.op("act", lambda e, bg=bg, ss=ss: e.activation(out=sil[:, ss, :], in_=psb(bg), func=AF.Silu),
                      reads=["bank%d" % bg], writes=["sig%d" % ss])
                bfree(bg)
                extra = []
                tk.op("dve", lambda e, bu=bu, ss=ss, j=j: e.tensor_tensor(out=actT[:, j, :], in0=sil[:, ss, :],
                                                                          in1=psb(bu), op=ALU.mult),
                      reads=["bank%d" % bu, "sig%d" % ss], writes=["actT%d" % j] + extra)
                bfree(bu)

        def down_half(m, half, wch, hook=None):
            banks = [balloc() for _ in range(4)]
            for kg in range(3):
                ws, wres = wch[kg]
                nk = 8 if kg < 2 else 6
                for i in range(4):
                    def _mm(e, ws=ws, kg=kg, nk=nk, i=i, b=banks[i]):
                        ins = None
                        for kk in range(nk):
                            k = kg * 8 + kk
                            ins = e.matmul(psb(b), lhsT=actT[:, k, i * 128:(i + 1) * 128], rhs=wbuf[:, ws, kk, :],
                                           start=(k == 0), stop=(k == NFB - 1))
                        return ins
                    tk.op("pe", _mm, reads=["actT%d" % k for k in range(kg * 8, kg * 8 + nk)] + [wres],
                          writes=["bank%d" % banks[i]])
                issue_chunk()
                if hook is not None:
                    hook(half * 3 + kg)
            return banks

        pending_y = []

        def flush_y():
            for (m_, i) in pending_y:
                tk.dma("pool", lambda e, i=i, m_=m_: e.dma_start(
                    out=y_d[m_ * T + i * 128:m_ * T + (i + 1) * 128, 0:512], in_=oa[:, i, :]),
                    s_y[i], reads=["oa%d" % i])
                tk.dma("pool", lambda e, i=i, m_=m_: e.dma_start(
                    out=y_d[m_ * T + i * 128:m_ * T + (i + 1) * 128, 512:1024], in_=yc[:, i, :]),
                    s_y2[i], reads=["yc%d" % i])
            del pending_y[:]

        def mixer_front_all(m):
            A_sp_a(m)
            A_sp_b(m)
            B_conv(m)
            for i in range(4):
                T_half(i, 0)
            B_tok_a(m)
            B_tok_b(m)
            B_tok_c(m)

        def w_in_all(m):
            wu = next_chunk()
            wv = next_chunk()
            A_mm_all(m, wu, wv)
            wval = next_chunk()
            wgate = next_chunk()
            B_proj(m, wval, wgate)
            B_shift(m)
            issue_chunk()
            issue_chunk()
            A_ln(m)

        def wout_all(m):
            w0 = next_chunk()
            w1 = next_chunk()
            wout_wave(m, [0, 1], w0, w1, 0)
            wout_wave(m, [2, 3], w0, w1, 1)
            wout_wave_b(m, [0, 1], 0)
            wout_wave_b(m, [2, 3], 1)
            issue_chunk()
            issue_chunk()

        def wout_win_interleaved(m_out, m_in):
            w0 = next_chunk()
            w1 = next_chunk()
            wout_wave(m_out, [0, 1], w0, w1, 0)
            wout_wave(m_out, [2, 3], w0, w1, 1)
            issue_chunk()
            issue_chunk()
            wu = next_chunk()
            wv = next_chunk()
            A_mm_all(m_in, wu, wv)
            wout_wave_b(m_out, [0, 1], 0)
            wval = next_chunk()
            wgate = next_chunk()
            B_proj(m_in, wval, wgate, mid_hook=lambda: wout_wave_b(m_out, [2, 3], 1))
            B_shift(m_in)
            issue_chunk()
            issue_chunk()
            for i in range(4):
                hn2T(i)
            A_ln(m_in)

        def ffn_down(m, hook):
            wch0 = [next_chunk() for _ in range(3)]
            banks0 = down_half(m, 0, wch0, hook)
            for i in range(4):
                b = banks0[i]
                tk.op("act", lambda e, b=b, i=i: e.activation(out=sig[:, i % 2, :], in_=psb(b), func=AF.Square,
                                                              accum_out=st[:, 7, F_SS + 2 * i:F_SS + 2 * i + 1]),
                      reads=["bank%d" % b, "stall"], writes=["sig%d" % (i % 2), "s7ss%d_0" % i])
                tk.op("act", lambda e, b=b, i=i: e.activation(out=oa[:, i, :], in_=psb(b), func=AF.Copy),
                      reads=["bank%d" % b], writes=["oa%d" % i])
                bfree(b)
                tk.op("dve", lambda e, i=i: e.tensor_tensor(out=oa[:, i, :], in0=oa[:, i, :],
                                                            in1=vecB[:, VB_GPOST2:VB_GPOST2 + 512], op=ALU.mult),
                      reads=["oa%d" % i, "vecB"], writes=["oa%d" % i])
            wch1 = [next_chunk() for _ in range(3)]
            banks1 = down_half(m, 1, wch1, hook)
            for i in range(4):
                b = banks1[i]
                tk.op("act", lambda e, b=b, i=i: e.activation(out=sig[:, i % 2, :], in_=psb(b), func=AF.Square,
                                                              accum_out=st[:, 7, F_SS + 2 * i + 1:F_SS + 2 * i + 2]),
                      reads=["bank%d" % b, "stall"], writes=["sig%d" % (i % 2), "s7ss%d_1" % i])
            sums = st[:, 7, F_SS + 8:F_SS + 12]
            pair = st[:, 7, F_SS:F_SS + 8].rearrange("p (n c) -> p n c", c=2)
            tk.op("dve", lambda e: e.tensor_tensor(out=sums, in0=pair[:, :, 0], in1=pair[:, :, 1], op=ALU.add),
                  reads=["s7ss%d_%d" % (i, h) for i in range(4) for h in range(2)], writes=["s7sum"])
            y, cres = chain(7, sums, 1.0 / D, 4, 0, ["s7sum"], "")
            for i in range(4):
                b = banks1[i]
                xres = "xt%d" % i
                tk.op("dve", lambda e, i=i: e.scalar_tensor_tensor(
                    out=oa[:, i, :], in0=oa[:, i, :], scalar=y[:, i:i + 1], in1=xt[:, i, 0:512],
                    op0=ALU.mult, op1=ALU.add), reads=[xres, "oa%d" % i, cres], writes=["oa%d" % i])
                tk.op("dve", lambda e, i=i, b=b: e.scalar_tensor_tensor(
                    out=yc[:, i, :], in0=psb(b), scalar=y[:, i:i + 1],
                    in1=vecB[:, VB_GPOST2 + 512:VB_GPOST2 + 1024],
                    op0=ALU.mult, op1=ALU.mult), reads=["bank%d" % b, cres, "vecB"], writes=["yc%d" % i])
                bfree(b)
                tk.op("pool", lambda e, i=i: e.tensor_tensor(out=yc[:, i, :], in0=yc[:, i, :],
                                                             in1=xt[:, i, 512:1024], op=ALU.add),
                      reads=[xres, "yc%d" % i], writes=["yc%d" % i])
                pending_y.append((m, i))

        p1_s0(0, 0)
        p1_s0(0, 1)
        for _ in range(NW):
            issue_chunk()
        for j in range(5):
            p1_s12(0, j)
            if j + 2 < 5:
                p1_s0(0, j + 2)
            phase1_pe(0, j)
        w_in_all(0)
        if NT > 1:
            for j in range(5):
                phase1_nonpe(1, j)
                phase1_pe(1, j)
        mixer_front_all(0)
        for i in range(4):
            T_half(i, 1)
        x_reload(0)
        wout_all(0)
        if NT > 1:
            w_in_all(1)
        for i in range(4):
            hn2T(i)

        for m in range(NT):
            nxt = m + 1 < NT
            nxt2 = m + 2 < NT
            for c in range(11):
                if nxt:
                    if c == 2:
                        B_conv(m + 1)
                    if c == 4:
                        A_sp_a(m + 1)
                        B_tok_a(m + 1)
                    if c == 7:
                        for i in range(4):
                            T_half(i, 0)
                    if c == 10:
                        for i in range(4):
                            T_half(i, 1)
                w = next_chunk()
                gu_chunk(m, c, w)
                if nxt:
                    if c == 5:
                        A_sp_b(m + 1)
                    if c == 6:
                        B_tok_b(m + 1)
                    if c == 7:
                        B_tok_c(m + 1)
                if nxt2:
                    if c == 8:
                        p1_s0(m + 2, 0)
                    if c == 9:
                        p1_s0(m + 2, 1)
                    if c == 10:
                        p1_s12(m + 2, 0)
                        p1_s0(m + 2, 2)
                issue_chunk()

            def hook(bd, m=m, nxt2=nxt2):
                if not nxt2:
                    return
                if 0 <= bd - 1 < 5:
                    phase1_pe(m + 2, bd - 1)
                if bd + 1 < 5:
                    p1_s12(m + 2, bd + 1)
                if bd + 3 < 5:
                    p1_s0(m + 2, bd + 3)
            ffn_down(m, hook)
            flush_y()
            if nxt:
                x_reload(m + 1)
                if nxt2:
                    wout_win_interleaved(m + 1, m + 2)
                else:
                    wout_all(m + 1)
                    for i in range(4):
                        hn2T(i)
        flush_y()
        if debug:
            pass
        tk.final_wait("pool", s_y + s_y2)

        with nc.Block() as block:
            @block.sync
            def _(e):
                tk.replay("sp", e)

            @block.gpsimd
            def _(e):
                tk.replay("pool", e)

            @block.scalar
            def _(e):
                tk.replay("act", e)

            @block.vector
            def _(e):
                tk.replay("dve", e)

            @block.tensor
            def _(e):
                tk.replay("pe", e)
    return nc


def _prep_shared(mix_pre_g, a_ln_g, a_ln_b, a_sp_w, a_sp_b, b_conv_w, b_conv_b, b_ln_g, b_ln_b, grp_g,
                 mix_post_g, ffn_pre_g, ffn_post_g):
    f = np.float32
    vecP = np.zeros((128, VP_N), f)
    vecP[:, VP_CB:VP_CB + 4] = np.asarray(b_conv_b[0], f).reshape(4, 128).T
    vecP[:, VP_SPB:VP_SPB + 4] = np.asarray(a_sp_b[0], f).T
    cw = np.concatenate([np.asarray(b_conv_w[0], f), np.zeros((1, 512), f)], axis=0)
    ck = cw.reshape(4, 8, 4, 4, 32)
    vecP[:, VP_CW:VP_CW + 128] = ck.transpose(0, 4, 2, 1, 3).reshape(128, 128)
    vecP[:, VP_GPRE:VP_GPRE + 8] = np.asarray(mix_pre_g[0], f).reshape(8, 128).T
    vecP[:, VP_GPRE2:VP_GPRE2 + 8] = np.asarray(ffn_pre_g[0], f).reshape(8, 128).T
    vecP[:, VP_GRP:VP_GRP + 8] = np.asarray(grp_g[0], f).reshape(8, 128).T
    vb = np.concatenate([np.asarray(v[0], f).reshape(-1) for v in
                         (mix_post_g, ffn_post_g, a_ln_g, a_ln_b, b_ln_g, b_ln_b)])
    vecB = np.ascontiguousarray(np.broadcast_to(vb[None, :], (128, VB_N)))
    wsT = np.ascontiguousarray(np.asarray(a_sp_w[0], f).transpose(2, 0, 1)).reshape(128, 512)
    ident = np.eye(128, dtype=f)
    return vecP, vecB, wsT, ident


def _halo(X, t0):
    h = np.zeros((32, X.shape[1]), X.dtype)
    if t0 % SEQ != 0:
        h[0:16] = X[t0 - 16:t0]
    if (t0 + T) % SEQ != 0:
        h[16:32] = X[t0 + T:t0 + T + 16]
    return h


_NC_CACHE = {}
MASK = np.ascontiguousarray(np.tile(np.eye(32, dtype=np.float32), (4, 1)))


def kernel(x_prompt, x_sample, mix_pre_g, w_in, a_ln_g, a_ln_b, a_sp_w, a_sp_b, b_conv_w, b_conv_b,
           b_ln_g, b_ln_b, grp_g, w_out, mix_post_g, ffn_pre_g, w_gate_up, w_down, ffn_post_g):
    f = np.float32
    xp = np.asarray(x_prompt, f)
    xs = np.asarray(x_sample, f)
    X = np.concatenate([xp.reshape(-1, D), xs.reshape(-1, D)], axis=0)
    ntot = X.shape[0]
    assert ntot == NCORES * TOK_PER_CORE
    vecP, vecB, wsT, ident = _prep_shared(mix_pre_g, a_ln_g, a_ln_b, a_sp_w, a_sp_b, b_conv_w, b_conv_b,
                                          b_ln_g, b_ln_b, grp_g, mix_post_g, ffn_pre_g, ffn_post_g)
    NT = NT_FULL
    if NT not in _NC_CACHE:
        _NC_CACHE[NT] = build(NT)
    nc = _NC_CACHE[NT]
    w_in_ = np.ascontiguousarray(np.asarray(w_in, f)[0])
    w_out_ = np.ascontiguousarray(np.asarray(w_out, f)[0])
    w_gu_ = np.ascontiguousarray(np.asarray(w_gate_up, f)[0])
    w_dn_ = np.ascontiguousarray(np.asarray(w_down, f)[0])
    in_maps = []
    for c in range(NCORES):
        t0 = c * TOK_PER_CORE
        xh = np.concatenate([_halo(X, t0 + m * T) for m in range(NT)], axis=0)
        in_maps.append({
            "x": np.ascontiguousarray(X[t0:t0 + TOK_PER_CORE]), "xh": xh,
            "w_in": w_in_, "w_out": w_out_, "w_gu": w_gu_, "w_dn": w_dn_,
            "vecP": vecP, "vecB": vecB, "wsT": wsT, "ident": ident, "mask": MASK,
        })
    res = run_bass_kernel_spmd(nc, in_maps, core_ids=list(range(NCORES)))
    Y = np.concatenate([np.asarray(r["y"], f) for r in res.results], axis=0)
    npr = xp.shape[0] * xp.shape[1]
    y_prompt = Y[:npr].reshape(xp.shape)
    y_sample = Y[npr:].reshape(xs.shape)
    return (y_prompt, y_sample)
```

```python
import numpy as np
from contextlib import ExitStack
import concourse.bass as bass
import concourse.mybir as mybir
from concourse.bass_utils import run_bass_kernel_spmd

F32 = mybir.dt.float32
BF16 = mybir.dt.bfloat16
I32 = mybir.dt.int32
AF = mybir.ActivationFunctionType
ALU = mybir.AluOpType

D = 1024
DFF = 2816
NFB = DFF // 128
T = 512
NCORES = 8
TOK_PER_CORE = 10240
NT_FULL = TOK_PER_CORE // T
SEQ = 4096
EPS = 1e-6
NW = 4
MAGIC = 1597463007.0

VB_GPOST, VB_GPOST2 = 0, 1024
VB_ALG, VB_ALB, VB_BLG, VB_BLB = 2048, 2560, 3072, 3584
VB_N = 4096
YC_ALIAS = False
VP_CB, VP_SPB, VP_CW = 0, 4, 8
VP_GPRE, VP_GPRE2, VP_GRP = 136, 144, 152
VP_N = 160


class Trk:
    def __init__(self, nc, es):
        self.nc = nc
        self.es = es
        self.engs = ("pe", "act", "dve", "pool", "sp")
        self.streams = {e: [] for e in self.engs}
        self.psem = {e: es.enter_context(nc.semaphore("prog_" + e)) for e in ("pe", "act", "dve", "pool")}
        self.cnt = {e: 0 for e in self.psem}
        self.dcnt = {}
        self.waited = {e: {} for e in self.engs}
        self.res = {}

    def new_dma_sem(self, name):
        s = self.es.enter_context(self.nc.semaphore(name))
        self.dcnt[s.num] = 0
        return s

    def _deps(self, eng, reads, writes):
        evs = {}

        def add(ev):
            if ev is None:
                return
            sem, val = ev
            if evs.get(sem.num, (None, 0))[1] < val:
                evs[sem.num] = (sem, val)

        for r in reads:
            st = self.res.get(r)
            if st:
                add(st[0])
        for w in writes:
            st = self.res.get(w)
            if st:
                if st[1]:
                    for ev in st[1].values():
                        add(ev)
                else:
                    add(st[0])
        out = []
        for key, (sem, val) in evs.items():
            if eng == "pe" and sem.num == self.psem["pe"].num:
                continue
            if self.waited[eng].get(key, 0) >= val:
                continue
            self.waited[eng][key] = val
            out.append((sem, val))
        return out

    def _update(self, ev, reads, writes):
        for r in reads:
            st = self.res.setdefault(r, [None, {}])
            st[1][ev[0].num] = ev
        for w in writes:
            self.res[w] = [ev, {}]

    def op(self, eng, fn, reads=(), writes=()):
        waits = self._deps(eng, reads, writes)
        self.cnt[eng] += 1
        ev = (self.psem[eng], self.cnt[eng])
        self.streams[eng].append((waits, fn, (self.psem[eng], 1)))
        self._update(ev, reads, writes)

    def dma(self, q, fn, sem, reads=(), writes=()):
        waits = self._deps(q, reads, writes)
        self.dcnt[sem.num] += 16
        ev = (sem, self.dcnt[sem.num])
        self.streams[q].append((waits, fn, (sem, 16)))
        self._update(ev, reads, writes)

    def final_wait(self, q, sems):
        waits = []
        for s in sems:
            waits.append((s, self.dcnt[s.num]))
        self.streams[q].append((waits, None, None))

    def replay(self, eng_name, eng):
        for waits, fn, inc in self.streams[eng_name]:
            for sem, val in waits:
                eng.wait_ge(sem, val)
            if fn is None:
                continue
            ins = fn(eng)
            if inc is not None:
                ins.then_inc(inc[0], inc[1])


def build(NT, debug=False):
    nc = bass.Bass("TRN2", target_bir_lowering=False)
    ntok = NT * T
    x_d = nc.dram_tensor("x", [ntok, D], F32, kind="ExternalInput").ap()
    xh_d = nc.dram_tensor("xh", [NT * 32, D], F32, kind="ExternalInput").ap()
    w_in_d = nc.dram_tensor("w_in", [D, 2 * D], F32, kind="ExternalInput").ap()
    w_out_d = nc.dram_tensor("w_out", [D, D], F32, kind="ExternalInput").ap()
    w_gu_d = nc.dram_tensor("w_gu", [D, 2 * DFF], F32, kind="ExternalInput").ap()
    w_dn_d = nc.dram_tensor("w_dn", [DFF, D], F32, kind="ExternalInput").ap()
    vecP_d = nc.dram_tensor("vecP", [128, VP_N], F32, kind="ExternalInput").ap()
    vecB_d = nc.dram_tensor("vecB", [128, VB_N], F32, kind="ExternalInput").ap()
    wsT_d = nc.dram_tensor("wsT", [128, 4 * 128], F32, kind="ExternalInput").ap()
    ident_d = nc.dram_tensor("ident", [128, 128], F32, kind="ExternalInput").ap()
    mask_d = nc.dram_tensor("mask", [128, 32], F32, kind="ExternalInput").ap()
    y_d = nc.dram_tensor("y", [ntok, D], F32, kind="ExternalOutput").ap()
    win_b = nc.dram_tensor("win_b", [D, 2 * D], BF16, kind="Internal").ap()
    wout_b = nc.dram_tensor("wout_b", [D, D], BF16, kind="Internal").ap()
    wgu_b = nc.dram_tensor("wgu_b", [D, 11, 512], BF16, kind="Internal").ap()
    wdn_b = nc.dram_tensor("wdn_b", [DFF, D], BF16, kind="Internal").ap()

    es = ExitStack()
    with es:
        def sb(name, shape, dt):
            return es.enter_context(nc.sbuf_tensor(name, shape, dt))

        xin = sb("xin", [128, 2, D], F32)
        xt = sb("xt", [128, 4, D], F32)
        tb = sb("tb", [128, 4, D], BF16)
        aT = sb("aT", [128, 2, 8, T + 32], BF16)
        u_sb = sb("u_sb", [128, 4, 512], F32)
        vn = sb("vn", [128, 4, 512], BF16)
        sig = sb("sig", [128, 2, 512], F32)
        sil = sig
        zbuf = sb("zbuf", [128, 4, T + 32], BF16)
        oa = sb("oa", [128, 4, 512], F32)
        v_sb = oa
        ob = sb("ob", [128, 4, 512], F32)
        actT = sb("actT", [128, NFB, T], BF16)
        mixT = sb("mixT", [128, 8, T], BF16)
        yc = sb("yc", [128, 4, 512], F32) if not YC_ALIAS else None
        st = sb("st", [128, 8, 192], F32)
        identf = sb("identf", [128, 128], F32)
        identb = sb("identb", [128, 128], BF16)
        wsT_b = sb("wsT_b", [128, 4, 128], BF16)
        diagK = sb("diagK", [128, 128, 32], BF16)
        ZS = sb("ZS", [128, 4, 4, 520], BF16)
        maskf = sb("maskf", [128, 32], F32)
        vecP = sb("vecP_s", [128, VP_N], F32)
        vecB = sb("vecB_s", [128, VB_N], F32)
        wbuf = sb("wbuf", [128, NW, 8, 512], BF16)
        ps = [es.enter_context(nc.psum_tensor("ps%d" % i, [128, 512], F32)) for i in range(8)]
        if YC_ALIAS:
            actT_f = actT.bitcast(F32)

            def yc_blk(j):
                return actT_f[:, 8 + 2 * j:10 + 2 * j, :].rearrange("p a b -> p (a b)")

            def yc_tile(j, i):
                return actT_f[:, 8 + 2 * j + i // 2, (i % 2) * 128:(i % 2) * 128 + 128]
        else:
            def yc_blk(j):
                return yc[:, j, :]

            def yc_tile(j, i):
                return yc[:, j, i * 128:(i + 1) * 128]

        tk = Trk(nc, es)
        s_cast = {k: tk.new_dma_sem("s_cast_" + k) for k in "abcde"}
        s_const = [tk.new_dma_sem("s_const%d" % i) for i in range(5)]
        s_zs = tk.new_dma_sem("s_zs")
        s_xin = [tk.new_dma_sem("s_xin%d" % i) for i in range(2)]
        s_xt = [tk.new_dma_sem("s_xt%d" % i) for i in range(4)]
        s_y = [tk.new_dma_sem("s_y%d" % i) for i in range(4)]
        s_y2 = [tk.new_dma_sem("s_yb%d" % i) for i in range(4)]
        s_w = [tk.new_dma_sem("s_w%d" % i) for i in range(NW)]

        free_banks = list(range(8))

        def balloc():
            assert free_banks, "PSUM banks exhausted at emission"
            return free_banks.pop(0)

        def bfree(b):
            free_banks.append(b)

        def psb(b):
            return ps[b][:]

        ps16 = [p.bitcast(BF16) for p in ps]

        def psb16(b):
            return ps16[b][:]

        F_BN, F_MV, F_A, F_Y, F_T, F_SS = 0, 96, 128, 144, 160, 176

        def chain(slot, src_ap, scale, n, c0, reads, tag):
            res = "s%dch%s" % (slot, tag)
            a = st[:, slot, F_A + c0:F_A + c0 + n]
            y = st[:, slot, F_Y + c0:F_Y + c0 + n]
            t = st[:, slot, F_T + c0:F_T + c0 + n]
            tk.op("dve", lambda e: e.tensor_scalar(out=a, in0=src_ap, scalar1=scale, scalar2=EPS,
                                                   op0=ALU.mult, op1=ALU.add), reads=list(reads) + [res], writes=[res])
            tk.op("dve", lambda e: e.tensor_scalar(out=y.bitcast(I32), in0=a.bitcast(I32), scalar1=-0.5,
                                                   scalar2=MAGIC, op0=ALU.mult, op1=ALU.add),
                  reads=[res], writes=[res])
            for _ in range(3):
                tk.op("dve", lambda e: e.scalar_tensor_tensor(out=t, in0=y, scalar=-0.5, in1=y,
                                                              op0=ALU.mult, op1=ALU.mult),
                      reads=[res], writes=[res])
                tk.op("dve", lambda e: e.tensor_tensor(out=t, in0=t, in1=a, op=ALU.mult),
                      reads=[res], writes=[res])
                tk.op("dve", lambda e: e.scalar_tensor_tensor(out=y, in0=t, scalar=1.5, in1=y,
                                                              op0=ALU.add, op1=ALU.mult),
                      reads=[res], writes=[res])
            return y, res

        s_cast_win = [tk.new_dma_sem("s_cast_win%d" % c) for c in range(4)]
        for c in range(4):
            for hb in range(2):
                r0 = hb * 512
                tk.dma("pool", lambda e, r0=r0, c=c: e.dma_start(
                    out=win_b[r0:r0 + 512, c * 512:(c + 1) * 512], in_=w_in_d[r0:r0 + 512, c * 512:(c + 1) * 512]),
                    s_cast_win[c], writes=["wscr_a%d_%d" % (c, hb)])
        for rb in range(8):
            r0 = rb * 128
            tk.dma("pool", lambda e, r0=r0: e.dma_start(out=wout_b[r0:r0 + 128, :], in_=w_out_d[r0:r0 + 128, :]),
                   s_cast["b"], writes=["wscr%d_b" % rb])
        for rb in range(8):
            r0 = rb * 128
            tk.dma("pool", lambda e, r0=r0: e.dma_start(
                out=wgu_b[r0:r0 + 128, :, 0:256],
                in_=w_gu_d[r0:r0 + 128, 0:DFF].rearrange("p (c f) -> p c f", f=256)),
                s_cast["c"], writes=["wscr%d_c" % rb])
            tk.dma("pool", lambda e, r0=r0: e.dma_start(
                out=wgu_b[r0:r0 + 128, :, 256:512],
                in_=w_gu_d[r0:r0 + 128, DFF:2 * DFF].rearrange("p (c f) -> p c f", f=256)),
                s_cast["d"], writes=["wscr%d_d" % rb])
        for rb in range(NFB):
            r0 = rb * 128
            tk.dma("pool", lambda e, r0=r0: e.dma_start(out=wdn_b[r0:r0 + 128, :], in_=w_dn_d[r0:r0 + 128, :]),
                   s_cast["e"], writes=["wscr%d_e" % rb])
        WSCR = {"win0": ["wscr_a0_0", "wscr_a0_1"], "win1": ["wscr_a1_0", "wscr_a1_1"],
                "win2": ["wscr_a2_0", "wscr_a2_1"], "win3": ["wscr_a3_0", "wscr_a3_1"],
                "wout": ["wscr%d_b" % i for i in range(8)],
                "wgu": ["wscr%d_c" % i for i in range(8)] + ["wscr%d_d" % i for i in range(8)],
                "wdn": ["wscr%d_e" % i for i in range(NFB)]}

        tk.dma("sp", lambda e: e.dma_start(out=vecP[:], in_=vecP_d), s_const[0], writes=["vecP"])
        tk.dma("sp", lambda e: e.dma_start(out=vecB[:], in_=vecB_d), s_const[1], writes=["vecB"])
        tk.dma("sp", lambda e: e.dma_start(out=identf[:], in_=ident_d), s_const[2], writes=["identf"])
        tk.dma("sp", lambda e: e.dma_start(out=ob[:, 0, :], in_=wsT_d), s_const[3], writes=["ob0"])
        tk.op("dve", lambda e: e.tensor_copy(out=identb[:], in_=identf[:]), reads=["identf"], writes=["identb"])
        tk.op("dve", lambda e: e.tensor_copy(out=wsT_b[:].rearrange("p h q -> p (h q)"), in_=ob[:, 0, :]),
              reads=["ob0"], writes=["wsT_b"])
        tk.dma("sp", lambda e: e.dma_start(out=maskf[:], in_=mask_d), s_const[4], writes=["maskf"])
        tk.op("dve", lambda e: e.tensor_scalar(out=vecP[:, VP_CW:VP_CW + 128], in0=vecP[:, VP_CW:VP_CW + 128],
                                               scalar1=0.5, scalar2=None, op0=ALU.mult),
              reads=["vecP"], writes=["vecP"])
        for jk in range(128):
            def _f(e, jk=jk):
                return e.tensor_scalar(out=diagK[:, jk, :], in0=maskf[:], scalar1=vecP[:, VP_CW + jk:VP_CW + jk + 1],
                                       scalar2=None, op0=ALU.mult)
            tk.op("dve", _f, reads=["maskf", "vecP"], writes=["diag_jk%d" % jk])
        tk.op("dve", lambda e: e.memset(st[:], 0.0), reads=["diag_jk%d" % jk for jk in range(128)],
              writes=["diagall", "stall"])

        WIN4 = [("win", c) for c in range(4)]
        PLAN = list(WIN4) + [("wout", 0), ("wout", 1)]
        if NT > 1:
            PLAN += WIN4
        for m_ in range(NT):
            PLAN += [("wgu", c) for c in range(11)] + [("wdn", c) for c in range(6)]
            if m_ + 1 < NT:
                PLAN += [("wout", 0), ("wout", 1)]
            if m_ + 2 < NT:
                PLAN += WIN4
        wstate = {"next_issue": 0, "total": len(PLAN)}

        def issue_chunk():
            g = wstate["next_issue"]
            if g >= wstate["total"]:
                return
            wstate["next_issue"] += 1
            kind, c = PLAN[g]
            slot = g % NW
            res = "wslot%d" % slot
            if kind == "win":
                src = win_b.rearrange("(k p) f -> p k f", p=128)[:, :, c * 512:(c + 1) * 512]
                dst = wbuf[:, slot, :, :]
            elif kind == "wout":
                src = wout_b.rearrange("(k p) f -> p k f", p=128)[:, :, c * 512:(c + 1) * 512]
                dst = wbuf[:, slot, :, :]
            elif kind == "wgu":
                src = wgu_b.rearrange("(k p) c f -> p k c f", p=128)[:, :, c, :]
                dst = wbuf[:, slot, :, :]
            else:
                half, kg = c // 3, c % 3
                nk = 8 if kg < 2 else 6
                src = wdn_b.rearrange("(k p) f -> p k f", p=128)[:, kg * 8:kg * 8 + nk, half * 512:(half + 1) * 512]
                dst = wbuf[:, slot, 0:nk, :]
            wkey = ("win%d" % c) if kind == "win" else kind
            tk.dma("sp", lambda e: e.dma_start(out=dst, in_=src), s_w[slot], reads=WSCR[wkey], writes=[res])

        cons = {"n": 0}

        def next_chunk():
            g = cons["n"]
            cons["n"] += 1
            return g % NW, "wslot%d" % (g % NW)

        AT_P = ["aP0", "aP1", "aP2", "aP3"]
        AT_Q = ["aQ0", "aQ1", "aQ2", "aQ3"]

        def p1_s0(m, j):
            rows = 128 if j < 4 else 32
            xs = j % 2
            if j < 4:
                src = x_d[m * T + j * 128: m * T + (j + 1) * 128, :]
            else:
                src = xh_d[m * 32:(m + 1) * 32, :]
            tk.dma("sp", lambda e: e.dma_start(out=xin[0:rows, xs, :], in_=src), s_xin[xs], writes=["xin%d" % xs])

        def p1_s12(m, j):
            rows = 128 if j < 4 else 32
            xs = j % 2
            xres = "xin%d" % xs
            sres = "s0ss%d" % j
            tk.op("act", lambda e: e.activation(out=tb[0:rows, j % 4, :], in_=xin[0:rows, xs, :], func=AF.Square,
                                                accum_out=st[0:rows, 0, F_SS + j:F_SS + j + 1]),
                  reads=[xres, "stall"], writes=["tb%da" % (j % 4), "tb%db" % (j % 4), sres])
            y, cres = chain(0, st[:, 0, F_SS + j:F_SS + j + 1], 1.0 / D, 1, j, [sres], "j%d" % j)
            ts = j % 4
            tres = ["tb%da" % ts, "tb%db" % ts]
            tk.op("dve", lambda e: e.tensor_scalar(out=tb[0:rows, ts, :], in0=xin[0:rows, xs, :],
                                                   scalar1=y[0:rows, :], scalar2=None, op0=ALU.mult),
                  reads=[xres, cres], writes=tres)

        def phase1_nonpe(m, j):
            p1_s0(m, j)
            p1_s12(m, j)

        def phase1_pe(m, j):
            rows = 128 if j < 4 else 32
            ts = j % 4
            tres = ["tb%da" % ts, "tb%db" % ts]
            b = balloc()
            bres = "bank%d" % b

            def _tr(e):
                ins = None
                for k in range(8):
                    ins = e.transpose(psb16(b)[:, k * 128:k * 128 + rows],
                                      tb[0:rows, ts, k * 128:(k + 1) * 128], identb[0:rows, 0:rows])
                return ins
            tk.op("pe", _tr, reads=tres + ["identb"], writes=[bres])
            ares = "aP%d" % j

            def _ev(e):
                ins = None
                for k in range(8):
                    ins = e.activation(out=aT[:, 0, k, j * 128:j * 128 + rows],
                                       in_=psb16(b)[:, k * 128:k * 128 + rows], func=AF.Copy,
                                       scale=vecP[:, VP_GPRE + k:VP_GPRE + k + 1])
                return ins
            tk.op("act", _ev, reads=[bres, "vecP"], writes=[ares])
            bfree(b)

        def A_mm_all(m, wu, wv):
            for which, (ws, wres) in enumerate((wu, wv)):
                for i in range(4):
                    b = balloc()

                    def _mm(e, ws=ws, b=b, i=i):
                        ins = None
                        for k in range(8):
                            ins = e.matmul(psb(b), lhsT=aT[:, 0, k, i * 128:(i + 1) * 128], rhs=wbuf[:, ws, k, :],
                                           start=(k == 0), stop=(k == 7))
                        return ins
                    tk.op("pe", _mm, reads=["aP%d" % i, wres], writes=["bank%d" % b])
                    dst = u_sb if which == 0 else v_sb
                    dres = ("u%d" if which == 0 else "oa%d") % i
                    tk.op("act", lambda e, b=b, dst=dst, i=i: e.activation(out=dst[:, i, :], in_=psb(b), func=AF.Gelu),
                          reads=["bank%d" % b], writes=[dres])
                    bfree(b)
                if which == 0:
                    issue_chunk()
            issue_chunk()

        def A_ln(m):
            for i in range(4):
                def _bn(e, i=i):
                    ins = None
                    for h in range(4):
                        ins = e.bn_stats(out=st[:, 1, F_BN + (i * 4 + h) * 6:F_BN + (i * 4 + h + 1) * 6],
                                         in_=v_sb[:, i, h * 128:(h + 1) * 128])
                    return ins
                tk.op("dve", _bn, reads=["oa%d" % i, "stall"], writes=["s1bn%d" % i])

                def _ag(e, i=i):
                    ins = None
                    for h in range(4):
                        ins = e.bn_aggr(out=st[:, 1, F_MV + (i * 4 + h) * 2:F_MV + (i * 4 + h + 1) * 2],
                                        in_=st[:, 1, F_BN + (i * 4 + h) * 6:F_BN + (i * 4 + h + 1) * 6])
                    return ins
                tk.op("dve", _ag, reads=["s1bn%d" % i], writes=["s1mv%d" % i])

            var_ap = st[:, 1, F_MV:F_MV + 32].rearrange("p (n c) -> p n c", c=2)[:, :, 1]
            y, cres = chain(1, var_ap, 1.0, 16, 0, ["s1mv%d" % i for i in range(4)], "")
            for i in range(4):
                def _nrm(e, i=i):
                    ins = None
                    for h in range(4):
                        c = i * 4 + h
                        ins = e.tensor_scalar(out=v_sb[:, i, h * 128:(h + 1) * 128],
                                              in0=v_sb[:, i, h * 128:(h + 1) * 128],
                                              scalar1=st[:, 1, F_MV + 2 * c:F_MV + 2 * c + 1], scalar2=y[:, c:c + 1],
                                              op0=ALU.subtract, op1=ALU.mult)
                    return ins
                tk.op("dve", _nrm, reads=["oa%d" % i, cres, "s1mv%d" % i], writes=["oa%d" % i])
                tk.op("pool", lambda e, i=i: e.tensor_tensor(out=v_sb[:, i, :], in0=v_sb[:, i, :],
                                                             in1=vecB[:, VB_ALG:VB_ALG + 512], op=ALU.mult),
                      reads=["oa%d" % i, "vecB"], writes=["oa%d" % i])
                tk.op("pool", lambda e, i=i: e.tensor_tensor(out=vn[:, i, :], in0=v_sb[:, i, :],
                                                             in1=vecB[:, VB_ALB:VB_ALB + 512], op=ALU.add),
                      reads=["oa%d" % i, "vecB"], writes=["vn%d" % i])

        def A_sp_a(m):
            for i in range(4):
                b = balloc()

                def _mm(e, b=b, i=i):
                    ins = None
                    for h in range(4):
                        ins = e.matmul(psb(b)[:, h * 128:(h + 1) * 128], lhsT=wsT_b[:, h, :],
                                       rhs=vn[:, i, h * 128:(h + 1) * 128], start=True, stop=True)
                    return ins
                tk.op("pe", _mm, reads=["vn%d" % i, "wsT_b"], writes=["bank%d" % b])

                def _oa(e, b=b, i=i):
                    ins = None
                    for h in range(4):
                        ins = e.scalar_tensor_tensor(out=oa[:, i, h * 128:(h + 1) * 128],
                                                     in0=psb(b)[:, h * 128:(h + 1) * 128],
                                                     scalar=vecP[:, VP_SPB + h:VP_SPB + h + 1],
                                                     in1=u_sb[:, i, h * 128:(h + 1) * 128],
                                                     op0=ALU.add, op1=ALU.mult)
                    return ins
                tk.op("dve", _oa, reads=["bank%d" % b, "u%d" % i, "vecP"], writes=["oa%d" % i])
                bfree(b)
                tk.op("act", lambda e, i=i: e.activation(out=tb[:, i, 0:512], in_=oa[:, i, :], func=AF.Square,
                                                         accum_out=st[:, 2, F_SS + i:F_SS + i + 1]),
                      reads=["oa%d" % i, "stall"], writes=["tb%da" % i, "s2ss%d" % i])

        def A_sp_b(m):
            y, cres = chain(2, st[:, 2, F_SS:F_SS + 4], 1.0 / 512, 4, 0, ["s2ss%d" % i for i in range(4)], "")
            for i in range(4):
                tk.op("dve", lambda e, i=i: e.tensor_scalar(out=tb[:, i, 0:512], in0=oa[:, i, :],
                                                            scalar1=y[:, i:i + 1], scalar2=None, op0=ALU.mult),
                      reads=["oa%d" % i, cres], writes=["tb%da" % i])

        def T_half(i, half):
            b = balloc()
            c0 = half * 512
            tres = "tb%d%s" % (i, "ab"[half])

            def _tr(e):
                ins = None
                for k in range(4):
                    ins = e.transpose(psb16(b)[:, k * 128:(k + 1) * 128],
                                      tb[:, i, c0 + k * 128:c0 + (k + 1) * 128], identb[:])
                return ins
            tk.op("pe", _tr, reads=[tres, "identb"], writes=["bank%d" % b])

            def _ev(e):
                ins = None
                for k in range(4):
                    kk = half * 4 + k
                    ins = e.activation(out=mixT[:, kk, i * 128:(i + 1) * 128],
                                       in_=psb16(b)[:, k * 128:(k + 1) * 128], func=AF.Copy,
                                       scale=vecP[:, VP_GRP + kk:VP_GRP + kk + 1])
                return ins
            tk.op("act", _ev, reads=["bank%d" % b, "vecP"], writes=["mixT%d_%d" % (i, half)])
            bfree(b)

        def B_proj(m, wval, wgate, mid_hook=None, mid_hook2=None):
            (sv_, rv), (sg_, rg) = wval, wgate
            bh = balloc()

            def _mh(e):
                ins = None
                for j in range(4):
                    for k in range(8):
                        ins = e.matmul(psb(bh)[:, j * 32:(j + 1) * 32], lhsT=wbuf[:, sv_, k, j * 128:(j + 1) * 128],
                                       rhs=aT[:, 0, k, T:T + 32], start=(k == 0), stop=(k == 7))
                for j in range(4):
                    for k in range(8):
                        ins = e.matmul(psb(bh)[:, 128 + j * 32:128 + (j + 1) * 32],
                                       lhsT=wbuf[:, sg_, k, j * 128:(j + 1) * 128],
                                       rhs=aT[:, 0, k, T:T + 32], start=(k == 0), stop=(k == 7))
                return ins
            tk.op("pe", _mh, reads=["aP4", rv, rg], writes=["bank%d" % bh])
            tk.op("act", lambda e: e.activation(out=sig[:, 0, 0:128], in_=psb(bh)[:, 128:256],
                                                func=AF.Tanh, scale=0.5),
                  reads=["bank%d" % bh], writes=["sig0"])
            for side in range(2):
                def _zh(e, side=side):
                    dst = zbuf[:, :, 0:16] if side == 0 else zbuf[:, :, T + 16:T + 32]
                    a = sig[:, 0, 0:128].rearrange("p (j t) -> p j t", j=4)[:, :, side * 16:side * 16 + 16]
                    bb = psb(bh)[:, 0:128].rearrange("p (j t) -> p j t", j=4)[:, :, side * 16:side * 16 + 16]
                    return e.scalar_tensor_tensor(out=dst, in0=a, scalar=1.0, in1=bb, op0=ALU.add, op1=ALU.mult)
                tk.op("dve", _zh, reads=["bank%d" % bh, "sig0"], writes=["zh%d" % side])
            bfree(bh)
            for j in range(4):
                bv = balloc()
                bg = balloc()

                def _mv(e, bv=bv, j=j):
                    ins = None
                    for k in range(8):
                        ins = e.matmul(psb(bv), lhsT=wbuf[:, sv_, k, j * 128:(j + 1) * 128], rhs=aT[:, 0, k, 0:T],
                                       start=(k == 0), stop=(k == 7))
                    return ins

                def _mg(e, bg=bg, j=j):
                    ins = None
                    for k in range(8):
                        ins = e.matmul(psb(bg), lhsT=wbuf[:, sg_, k, j * 128:(j + 1) * 128], rhs=aT[:, 0, k, 0:T],
                                       start=(k == 0), stop=(k == 7))
                    return ins
                tk.op("pe", _mg, reads=AT_P + [rg], writes=["bank%d" % bg])
                tk.op("pe", _mv, reads=AT_P + [rv], writes=["bank%d" % bv])
                ss = (j + 1) % 2
                tk.op("act", lambda e, bg=bg, ss=ss: e.activation(out=sig[:, ss, :], in_=psb(bg), func=AF.Tanh,
                                                                  scale=0.5),
                      reads=["bank%d" % bg], writes=["sig%d" % ss])
                bfree(bg)
                tk.op("dve", lambda e, bv=bv, ss=ss, j=j: e.scalar_tensor_tensor(
                    out=zbuf[:, j, 16:16 + T], in0=sig[:, ss, :], scalar=1.0, in1=psb(bv),
                    op0=ALU.add, op1=ALU.mult), reads=["bank%d" % bv, "sig%d" % ss], writes=["z%d" % j])
                bfree(bv)
                if j == 1 and mid_hook is not None:
                    mid_hook()
                if j == 2 and mid_hook2 is not None:
                    mid_hook2()

        ZS_RES = ["zs%d_%d" % (g, jj) for g in range(4) for jj in range(4)]

        def B_shift(m):
            for g in range(4):
                for jj in range(4):
                    tk.dma("pool", lambda e, g=g, jj=jj: e.dma_start(
                        out=ZS[32 * jj:32 * jj + 32, g, :, :], in_=zbuf[32 * g:32 * g + 32, :, 8 * jj:8 * jj + 520]),
                        s_zs, reads=["z0", "z1", "z2", "z3", "zh0", "zh1"], writes=["zs%d_%d" % (g, jj)])

        def B_conv(m):
            for j in range(4):
                b = balloc()

                def _cv(e, b=b, j=j):
                    ins = None
                    for mm in range(8):
                        for g in range(4):
                            ins = e.matmul(ps[b][32 * g:32 * g + 32, :], lhsT=diagK[:, j * 32 + mm * 4 + g, :],
                                           rhs=ZS[:, g, j, mm + 1:mm + 1 + T], start=(mm == 0), stop=(mm == 7),
                                           tile_position=(0, 32 * g))
                    return ins
                tk.op("pe", _cv, reads=ZS_RES + ["diagall"], writes=["bank%d" % b])
                tk.op("act", lambda e, b=b, j=j: e.activation(out=yc_blk(j), in_=psb(b), func=AF.Identity,
                                                              bias=vecP[:, VP_CB + j:VP_CB + j + 1]),
                      reads=["bank%d" % b, "vecP"], writes=["yc%d" % j])
                bfree(b)

        def B_tok_a(m):
            banks = []
            for i in range(4):
                b = balloc()
                banks.append(b)

                def _tr(e, b=b, i=i):
                    ins = None
                    for j in range(4):
                        ins = e.transpose(psb(b)[:, j * 128:(j + 1) * 128], yc_tile(j, i), identf[:])
                    return ins
                tk.op("pe", _tr, reads=["yc0", "yc1", "yc2", "yc3", "identf"], writes=["bank%d" % b])

                def _bn(e, b=b, i=i):
                    ins = None
                    for h in range(4):
                        ins = e.bn_stats(out=st[:, 3, F_BN + (i * 4 + h) * 6:F_BN + (i * 4 + h + 1) * 6],
                                         in_=psb(b)[:, h * 128:(h + 1) * 128])
                    return ins
                tk.op("dve", _bn, reads=["bank%d" % b, "stall"], writes=["s3bn%d" % i])

                def _ag(e, i=i):
                    ins = None
                    for h in range(4):
                        ins = e.bn_aggr(out=st[:, 3, F_MV + (i * 4 + h) * 2:F_MV + (i * 4 + h + 1) * 2],
                                        in_=st[:, 3, F_BN + (i * 4 + h) * 6:F_BN + (i * 4 + h + 1) * 6])
                    return ins
                tk.op("dve", _ag, reads=["s3bn%d" % i], writes=["s3mv%d" % i])
            var_ap = st[:, 3, F_MV:F_MV + 32].rearrange("p (n c) -> p n c", c=2)[:, :, 1]
            y, cres = chain(3, var_ap, 1.0, 16, 0, ["s3mv%d" % i for i in range(4)], "")
            for i in range(4):
                b = banks[i]

                def _nrm(e, b=b, i=i):
                    ins = None
                    for h in range(4):
                        c = i * 4 + h
                        ins = e.tensor_scalar(out=ob[:, i, h * 128:(h + 1) * 128], in0=psb(b)[:, h * 128:(h + 1) * 128],
                                              scalar1=st[:, 3, F_MV + 2 * c:F_MV + 2 * c + 1], scalar2=y[:, c:c + 1],
                                              op0=ALU.subtract, op1=ALU.mult)
                    return ins
                tk.op("dve", _nrm, reads=["bank%d" % b, cres, "s3mv%d" % i], writes=["ob%d" % i])
                bfree(b)
                geng = "pool"
                tk.op(geng, lambda e, i=i: e.tensor_tensor(out=ob[:, i, :], in0=ob[:, i, :],
                                                           in1=vecB[:, VB_BLG:VB_BLG + 512], op=ALU.mult),
                      reads=["ob%d" % i, "vecB"], writes=["ob%d" % i])
                tk.op(geng, lambda e, i=i: e.tensor_tensor(out=ob[:, i, :], in0=ob[:, i, :],
                                                           in1=vecB[:, VB_BLB:VB_BLB + 512], op=ALU.add),
                      reads=["ob%d" % i, "vecB"], writes=["ob%d" % i])

        def B_tok_b(m):
            for i in range(4):
                tk.op("act", lambda e, i=i: e.activation(out=ob[:, i, :], in_=ob[:, i, :], func=AF.Silu),
                      reads=["ob%d" % i], writes=["ob%d" % i])
                tk.op("act", lambda e, i=i: e.activation(out=tb[:, i, 512:1024], in_=ob[:, i, :], func=AF.Square,
                                                         accum_out=st[:, 4, F_SS + i:F_SS + i + 1]),
                      reads=["ob%d" % i, "stall"], writes=["tb%db" % i, "s4ss%d" % i])

        def B_tok_c(m):
            y2, c2 = chain(4, st[:, 4, F_SS:F_SS + 4], 1.0 / 512, 4, 0, ["s4ss%d" % i for i in range(4)], "")
            for i in range(4):
                tk.op("dve", lambda e, i=i: e.tensor_scalar(out=tb[:, i, 512:1024], in0=ob[:, i, :],
                                                            scalar1=y2[:, i:i + 1], scalar2=None, op0=ALU.mult),
                      reads=["ob%d" % i, c2], writes=["tb%db" % i])

        def x_reload(m):
            for i in range(4):
                tk.dma("act", lambda e, i=i: e.dma_start(out=xt[:, i, :],
                                                         in_=x_d[m * T + i * 128:m * T + (i + 1) * 128, :]),
                       s_xt[i], writes=["xt%d" % i])

        def wout_wave(m, tiles, w0, w1, wave):
            tmpbuf = ob if wave == 0 else yc
            tmpname = "ob%d" if wave == 0 else "yc%d"
            for i in tiles:
                for half, (ws, wres) in enumerate((w0, w1)):
                    b = balloc()
                    q = 2 * (i % 2) + half

                    def _mm(e, ws=ws, b=b, i=i):
                        ins = None
                        for k in range(8):
                            ins = e.matmul(psb(b), lhsT=mixT[:, k, i * 128:(i + 1) * 128], rhs=wbuf[:, ws, k, :],
                                           start=(k == 0), stop=(k == 7))
                        return ins
                    tk.op("pe", _mm, reads=["mixT%d_0" % i, "mixT%d_1" % i, wres], writes=["bank%d" % b])
                    tk.op("act", lambda e, b=b, q=q: e.activation(out=tmpbuf[:, q, :], in_=psb(b), func=AF.Copy),
                          reads=["bank%d" % b], writes=[tmpname % q])
                    bfree(b)
                    tk.op("act", lambda e, q=q, i=i, half=half: e.activation(
                        out=tb[:, i, half * 512:(half + 1) * 512], in_=tmpbuf[:, q, :], func=AF.Square,
                        accum_out=st[:, 5, F_SS + 2 * i + half:F_SS + 2 * i + half + 1]),
                        reads=[tmpname % q, "stall"], writes=["tb%d%s" % (i, "ab"[half]), "s5ss%d_%d" % (i, half)])
                    tk.op("pool", lambda e, q=q, half=half: e.tensor_tensor(
                        out=tmpbuf[:, q, :], in0=tmpbuf[:, q, :],
                        in1=vecB[:, VB_GPOST + half * 512:VB_GPOST + (half + 1) * 512], op=ALU.mult),
                        reads=[tmpname % q, "vecB"], writes=[tmpname % q])

        def wout_wave_b(m, tiles, wave):
            tmpbuf = ob if wave == 0 else yc
            tmpname = "ob%d" if wave == 0 else "yc%d"
            i0 = tiles[0]
            n = len(tiles)
            sres = ["s5ss%d_%d" % (i, h) for i in tiles for h in range(2)]
            sums = st[:, 5, F_SS + 8 + i0:F_SS + 8 + i0 + n]
            pair = st[:, 5, F_SS + 2 * i0:F_SS + 2 * i0 + 2 * n].rearrange("p (n c) -> p n c", c=2)
            tk.op("dve", lambda e: e.tensor_tensor(out=sums, in0=pair[:, :, 0], in1=pair[:, :, 1], op=ALU.add),
                  reads=sres, writes=["s5sum%d" % wave])
            y, cres = chain(5, sums, 1.0 / D, n, i0, ["s5sum%d" % wave], "w%d" % wave)
            for ii, i in enumerate(tiles):
                for half in range(2):
                    q = 2 * (i % 2) + half
                    tmp = tmpbuf[:, q, :]
                    tres = tmpname % q
                    tk.op("dve", lambda e, half=half, i=i, ii=ii, tmp=tmp: e.scalar_tensor_tensor(
                        out=xt[:, i, half * 512:(half + 1) * 512], in0=tmp, scalar=y[:, ii:ii + 1],
                        in1=xt[:, i, half * 512:(half + 1) * 512], op0=ALU.mult, op1=ALU.add),
                        reads=["xt%d" % i, tres, cres], writes=["xt%d" % i])
                tk.op("dve", lambda e, i=i: e.scalar_tensor_tensor(
                    out=tb[:, i, :], in0=xt[:, i, :], scalar=1.0, in1=xt[:, i, :], op0=ALU.mult, op1=ALU.mult,
                    accum_out=st[:, 6, F_SS + i:F_SS + i + 1]),
                    reads=["xt%d" % i, "stall"], writes=["s6ss%d" % i, "tb%da" % i, "tb%db" % i])
            y2, c2 = chain(6, st[:, 6, F_SS + i0:F_SS + i0 + n], 1.0 / D, n, i0, ["s6ss%d" % i for i in tiles],
                           "w%d" % wave)
            for ii, i in enumerate(tiles):
                tk.op("dve", lambda e, i=i, ii=ii: e.tensor_scalar(out=tb[:, i, :], in0=xt[:, i, :],
                                                                   scalar1=y2[:, ii:ii + 1], scalar2=None, op0=ALU.mult),
                      reads=["xt%d" % i, c2], writes=["tb%da" % i, "tb%db" % i])

        def hn2T(i):
            b = balloc()

            def _tr(e):
                ins = None
                for k in range(8):
                    ins = e.transpose(psb16(b)[:, k * 128:(k + 1) * 128], tb[:, i, k * 128:(k + 1) * 128], identb[:])
                return ins
            tk.op("pe", _tr, reads=["tb%da" % i, "tb%db" % i, "identb"], writes=["bank%d" % b])

            g_bc = bass.AP(vecP, VP_GPRE2, [[VP_N, 128], [1, 8], [0, 128]])

            def _ev(e):
                return e.tensor_tensor(out=aT[:, 1, :, i * 128:(i + 1) * 128],
                                       in0=psb16(b).rearrange("p (k t) -> p k t", k=8), in1=g_bc, op=ALU.mult)
            tk.op("dve", _ev, reads=["bank%d" % b, "vecP"], writes=["aQ%d" % i])
            bfree(b)

        def gu_chunk(m, c, w):
            ws, wres = w
            for jj in range(2):
                j = 2 * c + jj
                bg = balloc()
                bu = balloc()

                def _mg(e, bg=bg, jj=jj):
                    ins = None
                    for k in range(8):
                        ins = e.matmul(psb(bg), lhsT=wbuf[:, ws, k, jj * 128:(jj + 1) * 128], rhs=aT[:, 1, k, 0:T],
                                       start=(k == 0), stop=(k == 7))
                    return ins

                def _mu(e, bu=bu, jj=jj):
                    ins = None
                    for k in range(8):
                        ins = e.matmul(psb(bu), lhsT=wbuf[:, ws, k, 256 + jj * 128:256 + (jj + 1) * 128],
                                       rhs=aT[:, 1, k, 0:T], start=(k == 0), stop=(k == 7))
                    return ins
                tk.op("pe", _mg, reads=AT_Q + [wres], writes=["bank%d" % bg])
                tk.op("pe", _mu, reads=AT_Q + [wres], writes=["bank%d" % bu])
                ss = j % 2
                tk.op("act", lambda e, bg=bg, ss=ss: e.activation(out=sil[:, ss, :], in_=psb(bg), func=AF.Silu),
                      reads=["bank%d" % bg], writes=["sig%d" % ss])
                bfree(bg)
                extra = []
                tk.op("dve", lambda e, bu=bu, ss=ss, j=j: e.tensor_tensor(out=actT[:, j, :], in0=sil[:, ss, :],
                                                                          in1=psb(bu), op=ALU.mult),
                      reads=["bank%d" % bu, "sig%d" % ss], writes=["actT%d" % j] + extra)
                bfree(bu)

        def down_half(m, half, wch, hook=None):
            banks = [balloc() for _ in range(4)]
            for kg in range(3):
                ws, wres = wch[kg]
                nk = 8 if kg < 2 else 6
                for i in range(4):
                    def _mm(e, ws=ws, kg=kg, nk=nk, i=i, b=banks[i]):
                        ins = None
                        for kk in range(nk):
                            k = kg * 8 + kk
                            ins = e.matmul(psb(b), lhsT=actT[:, k, i * 128:(i + 1) * 128], rhs=wbuf[:, ws, kk, :],
                                           start=(k == 0), stop=(k == NFB - 1))
                        return ins
                    tk.op("pe", _mm, reads=["actT%d" % k for k in range(kg * 8, kg * 8 + nk)] + [wres],
                          writes=["bank%d" % banks[i]])
                issue_chunk()
                if hook is not None:
                    hook(half * 3 + kg)
            return banks

        pending_y = []

        def flush_y():
            for (m_, i) in pending_y:
                tk.dma("pool", lambda e, i=i, m_=m_: e.dma_start(
                    out=y_d[m_ * T + i * 128:m_ * T + (i + 1) * 128, 0:512], in_=oa[:, i, :]),
                    s_y[i], reads=["oa%d" % i])
                tk.dma("pool", lambda e, i=i, m_=m_: e.dma_start(
                    out=y_d[m_ * T + i * 128:m_ * T + (i + 1) * 128, 512:1024], in_=yc[:, i, :]),
                    s_y2[i], reads=["yc%d" % i])
            del pending_y[:]

        def mixer_front_all(m):
            A_sp_a(m)
            A_sp_b(m)
            B_conv(m)
            for i in range(4):
                T_half(i, 0)
            B_tok_a(m)
            B_tok_b(m)
            B_tok_c(m)

        def w_in_all(m):
            wu = next_chunk()
            wv = next_chunk()
            A_mm_all(m, wu, wv)
            wval = next_chunk()
            wgate = next_chunk()
            B_proj(m, wval, wgate)
            B_shift(m)
            issue_chunk()
            issue_chunk()
            A_ln(m)

        def wout_all(m):
            w0 = next_chunk()
            w1 = next_chunk()
            wout_wave(m, [0, 1], w0, w1, 0)
            wout_wave(m, [2, 3], w0, w1, 1)
            wout_wave_b(m, [0, 1], 0)
            wout_wave_b(m, [2, 3], 1)
            issue_chunk()
            issue_chunk()

        def wout_win_interleaved(m_out, m_in):
            w0 = next_chunk()
            w1 = next_chunk()
            wout_wave(m_out, [0, 1], w0, w1, 0)
            wout_wave(m_out, [2, 3], w0, w1, 1)
            issue_chunk()
            issue_chunk()
            wu = next_chunk()
            wv = next_chunk()
            A_mm_all(m_in, wu, wv)
            wout_wave_b(m_out, [0, 1], 0)
            wval = next_chunk()
            wgate = next_chunk()
            B_proj(m_in, wval, wgate, mid_hook=lambda: wout_wave_b(m_out, [2, 3], 1),
                   mid_hook2=lambda: (hn2T(0), hn2T(1)))
            B_shift(m_in)
            issue_chunk()
            issue_chunk()
            for i in (2, 3):
                hn2T(i)
            A_ln(m_in)

        def ffn_down(m, hook):
            wch0 = [next_chunk() for _ in range(3)]
            banks0 = down_half(m, 0, wch0, hook)
            for i in range(4):
                b = banks0[i]
                tk.op("act", lambda e, b=b, i=i: e.activation(out=sig[:, i % 2, :], in_=psb(b), func=AF.Square,
                                                              accum_out=st[:, 7, F_SS + 2 * i:F_SS + 2 * i + 1]),
                      reads=["bank%d" % b, "stall"], writes=["sig%d" % (i % 2), "s7ss%d_0" % i])
                tk.op("act", lambda e, b=b, i=i: e.activation(out=oa[:, i, :], in_=psb(b), func=AF.Copy),
                      reads=["bank%d" % b], writes=["oa%d" % i])
                bfree(b)
                tk.op("dve", lambda e, i=i: e.tensor_tensor(out=oa[:, i, :], in0=oa[:, i, :],
                                                            in1=vecB[:, VB_GPOST2:VB_GPOST2 + 512], op=ALU.mult),
                      reads=["oa%d" % i, "vecB"], writes=["oa%d" % i])
            wch1 = [next_chunk() for _ in range(3)]
            banks1 = down_half(m, 1, wch1, hook)
            for i in range(4):
                b = banks1[i]
                tk.op("act", lambda e, b=b, i=i: e.activation(out=sig[:, i % 2, :], in_=psb(b), func=AF.Square,
                                                              accum_out=st[:, 7, F_SS + 2 * i + 1:F_SS + 2 * i + 2]),
                      reads=["bank%d" % b, "stall"], writes=["sig%d" % (i % 2), "s7ss%d_1" % i])
            sums = st[:, 7, F_SS + 8:F_SS + 12]
            pair = st[:, 7, F_SS:F_SS + 8].rearrange("p (n c) -> p n c", c=2)
            tk.op("dve", lambda e: e.tensor_tensor(out=sums, in0=pair[:, :, 0], in1=pair[:, :, 1], op=ALU.add),
                  reads=["s7ss%d_%d" % (i, h) for i in range(4) for h in range(2)], writes=["s7sum"])
            y, cres = chain(7, sums, 1.0 / D, 4, 0, ["s7sum"], "")
            for i in range(4):
                b = banks1[i]
                xres = "xt%d" % i
                tk.op("dve", lambda e, i=i: e.scalar_tensor_tensor(
                    out=oa[:, i, :], in0=oa[:, i, :], scalar=y[:, i:i + 1], in1=xt[:, i, 0:512],
                    op0=ALU.mult, op1=ALU.add), reads=[xres, "oa%d" % i, cres], writes=["oa%d" % i])
                tk.op("dve", lambda e, i=i, b=b: e.scalar_tensor_tensor(
                    out=yc[:, i, :], in0=psb(b), scalar=y[:, i:i + 1],
                    in1=vecB[:, VB_GPOST2 + 512:VB_GPOST2 + 1024],
                    op0=ALU.mult, op1=ALU.mult), reads=["bank%d" % b, cres, "vecB"], writes=["yc%d" % i])
                bfree(b)
                tk.op("pool", lambda e, i=i: e.tensor_tensor(out=yc[:, i, :], in0=yc[:, i, :],
                                                             in1=xt[:, i, 512:1024], op=ALU.add),
                      reads=[xres, "yc%d" % i], writes=["yc%d" % i])
                pending_y.append((m, i))

        p1_s0(0, 0)
        p1_s0(0, 1)
        for _ in range(NW):
            issue_chunk()
        for j in range(5):
            p1_s12(0, j)
            if j + 2 < 5:
                p1_s0(0, j + 2)
            phase1_pe(0, j)
        w_in_all(0)
        if NT > 1:
            for j in range(5):
                phase1_nonpe(1, j)
                phase1_pe(1, j)
        mixer_front_all(0)
        for i in range(4):
            T_half(i, 1)
        x_reload(0)
        wout_all(0)
        if NT > 1:
            w_in_all(1)
        for i in range(4):
            hn2T(i)

        for m in range(NT):
            nxt = m + 1 < NT
            nxt2 = m + 2 < NT
            for c in range(11):
                if nxt:
                    if c == 2:
                        B_conv(m + 1)
                    if c == 4:
                        A_sp_a(m + 1)
                        B_tok_a(m + 1)
                    if c == 7:
                        for i in range(4):
                            T_half(i, 0)
                    if c == 10:
                        for i in range(4):
                            T_half(i, 1)
                w = next_chunk()
                gu_chunk(m, c, w)
                if nxt:
                    if c == 5:
                        A_sp_b(m + 1)
                    if c == 6:
                        B_tok_b(m + 1)
                    if c == 7:
                        B_tok_c(m + 1)
                if nxt2:
                    if c == 8:
                        p1_s0(m + 2, 0)
                    if c == 9:
                        p1_s0(m + 2, 1)
                    if c == 10:
                        p1_s12(m + 2, 0)
                        p1_s0(m + 2, 2)
                issue_chunk()

            def hook(bd, m=m, nxt2=nxt2):
                if not nxt2:
                    return
                if 0 <= bd - 1 < 5:
                    phase1_pe(m + 2, bd - 1)
                if bd + 1 < 5:
                    p1_s12(m + 2, bd + 1)
                if bd + 3 < 5:
                    p1_s0(m + 2, bd + 3)
            ffn_down(m, hook)
            flush_y()
            if nxt:
                x_reload(m + 1)
                if nxt2:
                    wout_win_interleaved(m + 1, m + 2)
                else:
                    wout_all(m + 1)
                    for i in range(4):
                        hn2T(i)
        flush_y()
        if debug:
            pass
        tk.final_wait("pool", s_y + s_y2)

        with nc.Block() as block:
            @block.sync
            def _(e):
                tk.replay("sp", e)

            @block.gpsimd
            def _(e):
                tk.replay("pool", e)

            @block.scalar
            def _(e):
                tk.replay("act", e)

            @block.vector
            def _(e):
                tk.replay("dve", e)

            @block.tensor
            def _(e):
                tk.replay("pe", e)
    return nc


def _prep_shared(mix_pre_g, a_ln_g, a_ln_b, a_sp_w, a_sp_b, b_conv_w, b_conv_b, b_ln_g, b_ln_b, grp_g,
                 mix_post_g, ffn_pre_g, ffn_post_g):
    f = np.float32
    vecP = np.zeros((128, VP_N), f)
    vecP[:, VP_CB:VP_CB + 4] = np.asarray(b_conv_b[0], f).reshape(4, 128).T
    vecP[:, VP_SPB:VP_SPB + 4] = np.asarray(a_sp_b[0], f).T
    cw = np.concatenate([np.asarray(b_conv_w[0], f), np.zeros((1, 512), f)], axis=0)
    ck = cw.reshape(4, 8, 4, 4, 32)
    vecP[:, VP_CW:VP_CW + 128] = ck.transpose(0, 4, 2, 1, 3).reshape(128, 128)
    vecP[:, VP_GPRE:VP_GPRE + 8] = np.asarray(mix_pre_g[0], f).reshape(8, 128).T
    vecP[:, VP_GPRE2:VP_GPRE2 + 8] = np.asarray(ffn_pre_g[0], f).reshape(8, 128).T
    vecP[:, VP_GRP:VP_GRP + 8] = np.asarray(grp_g[0], f).reshape(8, 128).T
    vb = np.concatenate([np.asarray(v[0], f).reshape(-1) for v in
                         (mix_post_g, ffn_post_g, a_ln_g, a_ln_b, b_ln_g, b_ln_b)])
    vecB = np.ascontiguousarray(np.broadcast_to(vb[None, :], (128, VB_N)))
    wsT = np.ascontiguousarray(np.asarray(a_sp_w[0], f).transpose(2, 0, 1)).reshape(128, 512)
    ident = np.eye(128, dtype=f)
    return vecP, vecB, wsT, ident


def _halo(X, t0):
    h = np.zeros((32, X.shape[1]), X.dtype)
    if t0 % SEQ != 0:
        h[0:16] = X[t0 - 16:t0]
    if (t0 + T) % SEQ != 0:
        h[16:32] = X[t0 + T:t0 + T + 16]
    return h


_NC_CACHE = {}
MASK = np.ascontiguousarray(np.tile(np.eye(32, dtype=np.float32), (4, 1)))


def kernel(x_prompt, x_sample, mix_pre_g, w_in, a_ln_g, a_ln_b, a_sp_w, a_sp_b, b_conv_w, b_conv_b,
           b_ln_g, b_ln_b, grp_g, w_out, mix_post_g, ffn_pre_g, w_gate_up, w_down, ffn_post_g):
    f = np.float32
    xp = np.asarray(x_prompt, f)
    xs = np.asarray(x_sample, f)
    X = np.concatenate([xp.reshape(-1, D), xs.reshape(-1, D)], axis=0)
    ntot = X.shape[0]
    assert ntot == NCORES * TOK_PER_CORE
    vecP, vecB, wsT, ident = _prep_shared(mix_pre_g, a_ln_g, a_ln_b, a_sp_w, a_sp_b, b_conv_w, b_conv_b,
                                          b_ln_g, b_ln_b, grp_g, mix_post_g, ffn_pre_g, ffn_post_g)
    NT = NT_FULL
    if NT not in _NC_CACHE:
        _NC_CACHE[NT] = build(NT)
    nc = _NC_CACHE[NT]
    w_in_ = np.ascontiguousarray(np.asarray(w_in, f)[0])
    w_out_ = np.ascontiguousarray(np.asarray(w_out, f)[0])
    w_gu_ = np.ascontiguousarray(np.asarray(w_gate_up, f)[0])
    w_dn_ = np.ascontiguousarray(np.asarray(w_down, f)[0])
    in_maps = []
    for c in range(NCORES):
        t0 = c * TOK_PER_CORE
        xh = np.concatenate([_halo(X, t0 + m * T) for m in range(NT)], axis=0)
        in_maps.append({
            "x": np.ascontiguousarray(X[t0:t0 + TOK_PER_CORE]), "xh": xh,
            "w_in": w_in_, "w_out": w_out_, "w_gu": w_gu_, "w_dn": w_dn_,
            "vecP": vecP, "vecB": vecB, "wsT": wsT, "ident": ident, "mask": MASK,
        })
    res = run_bass_kernel_spmd(nc, in_maps, core_ids=list(range(NCORES)))
    Y = np.concatenate([np.asarray(r["y"], f) for r in res.results], axis=0)
    npr = xp.shape[0] * xp.shape[1]
    y_prompt = Y[:npr].reshape(xp.shape)
    y_sample = Y[npr:].reshape(xs.shape)
    return (y_prompt, y_sample)
```

```python
import numpy as np
from contextlib import ExitStack
import concourse.bass as bass
import concourse.mybir as mybir
from concourse.bass_utils import run_bass_kernel_spmd

F32 = mybir.dt.float32
BF16 = mybir.dt.bfloat16
I32 = mybir.dt.int32
AF = mybir.ActivationFunctionType
ALU = mybir.AluOpType

D = 1024
DFF = 2816
NFB = DFF // 128
T = 512
NCORES = 8
TOK_PER_CORE = 10240
NT_FULL = TOK_PER_CORE // T
SEQ = 4096
EPS = 1e-6
NW = 4
MAGIC = 1597463007.0

VB_GPOST, VB_GPOST2 = 0, 1024
VB_ALG, VB_ALB, VB_BLG, VB_BLB = 2048, 2560, 3072, 3584
VB_N = 4096
YC_ALIAS = False
VP_CB, VP_SPB, VP_CW = 0, 4, 8
VP_GPRE, VP_GPRE2, VP_GRP = 136, 144, 152
VP_N = 160


class Trk:
    def __init__(self, nc, es):
        self.nc = nc
        self.es = es
        self.engs = ("pe", "act", "dve", "pool", "sp")
        self.streams = {e: [] for e in self.engs}
        self.psem = {e: es.enter_context(nc.semaphore("prog_" + e)) for e in ("pe", "act", "dve", "pool")}
        self.cnt = {e: 0 for e in self.psem}
        self.dcnt = {}
        self.waited = {e: {} for e in self.engs}
        self.res = {}

    def new_dma_sem(self, name):
        s = self.es.enter_context(self.nc.semaphore(name))
        self.dcnt[s.num] = 0
        return s

    def _deps(self, eng, reads, writes):
        evs = {}

        def add(ev):
            if ev is None:
                return
            sem, val = ev
            if evs.get(sem.num, (None, 0))[1] < val:
                evs[sem.num] = (sem, val)

        for r in reads:
            st = self.res.get(r)
            if st:
                add(st[0])
        for w in writes:
            st = self.res.get(w)
            if st:
                if st[1]:
                    for ev in st[1].values():
                        add(ev)
                else:
                    add(st[0])
        out = []
        for key, (sem, val) in evs.items():
            if eng == "pe" and sem.num == self.psem["pe"].num:
                continue
            if self.waited[eng].get(key, 0) >= val:
                continue
            self.waited[eng][key] = val
            out.append((sem, val))
        return out

    def _update(self, ev, reads, writes):
        for r in reads:
            st = self.res.setdefault(r, [None, {}])
            st[1][ev[0].num] = ev
        for w in writes:
            self.res[w] = [ev, {}]

    def op(self, eng, fn, reads=(), writes=()):
        waits = self._deps(eng, reads, writes)
        self.cnt[eng] += 1
        ev = (self.psem[eng], self.cnt[eng])
        self.streams[eng].append((waits, fn, (self.psem[eng], 1)))
        self._update(ev, reads, writes)

    def dma(self, q, fn, sem, reads=(), writes=()):
        waits = self._deps(q, reads, writes)
        self.dcnt[sem.num] += 16
        ev = (sem, self.dcnt[sem.num])
        self.streams[q].append((waits, fn, (sem, 16)))
        self._update(ev, reads, writes)

    def final_wait(self, q, sems):
        waits = []
        for s in sems:
            waits.append((s, self.dcnt[s.num]))
        self.streams[q].append((waits, None, None))

    def replay(self, eng_name, eng):
        for waits, fn, inc in self.streams[eng_name]:
            for sem, val in waits:
                eng.wait_ge(sem, val)
            if fn is None:
                continue
            ins = fn(eng)
            if inc is not None:
                ins.then_inc(inc[0], inc[1])


def build(NT, debug=False):
    nc = bass.Bass("TRN2", target_bir_lowering=False)
    ntok = NT * T
    x_d = nc.dram_tensor("x", [ntok, D], F32, kind="ExternalInput").ap()
    xh_d = nc.dram_tensor("xh", [NT * 32, D], F32, kind="ExternalInput").ap()
    w_in_d = nc.dram_tensor("w_in", [D, 2 * D], F32, kind="ExternalInput").ap()
    w_out_d = nc.dram_tensor("w_out", [D, D], F32, kind="ExternalInput").ap()
    w_gu_d = nc.dram_tensor("w_gu", [D, 2 * DFF], F32, kind="ExternalInput").ap()
    w_dn_d = nc.dram_tensor("w_dn", [DFF, D], F32, kind="ExternalInput").ap()
    vecP_d = nc.dram_tensor("vecP", [128, VP_N], F32, kind="ExternalInput").ap()
    vecB_d = nc.dram_tensor("vecB", [128, VB_N], F32, kind="ExternalInput").ap()
    wsT_d = nc.dram_tensor("wsT", [128, 4 * 128], F32, kind="ExternalInput").ap()
    ident_d = nc.dram_tensor("ident", [128, 128], F32, kind="ExternalInput").ap()
    mask_d = nc.dram_tensor("mask", [128, 32], F32, kind="ExternalInput").ap()
    y_d = nc.dram_tensor("y", [ntok, D], F32, kind="ExternalOutput").ap()
    win_b = nc.dram_tensor("win_b", [D, 2 * D], BF16, kind="Internal").ap()
    wout_b = nc.dram_tensor("wout_b", [D, D], BF16, kind="Internal").ap()
    wgu_b = nc.dram_tensor("wgu_b", [D, 11, 512], BF16, kind="Internal").ap()
    wdn_b = nc.dram_tensor("wdn_b", [DFF, D], BF16, kind="Internal").ap()

    es = ExitStack()
    with es:
        def sb(name, shape, dt):
            return es.enter_context(nc.sbuf_tensor(name, shape, dt))

        xin = sb("xin", [128, 2, D], F32)
        xt = sb("xt", [128, 4, D], F32)
        tb = sb("tb", [128, 4, D], BF16)
        aT = sb("aT", [128, 2, 8, T + 32], BF16)
        u_sb = sb("u_sb", [128, 4, 512], F32)
        vn = sb("vn", [128, 4, 512], BF16)
        sig = sb("sig", [128, 2, 512], F32)
        sil = sig
        zbuf = sb("zbuf", [128, 4, T + 32], BF16)
        oa = sb("oa", [128, 4, 512], F32)
        v_sb = oa
        ob = sb("ob", [128, 4, 512], F32)
        actT = sb("actT", [128, NFB, T], BF16)
        mixT = sb("mixT", [128, 8, T], BF16)
        yc = sb("yc", [128, 4, 512], F32) if not YC_ALIAS else None
        st = sb("st", [128, 8, 192], F32)
        identf = sb("identf", [128, 128], F32)
        identb = sb("identb", [128, 128], BF16)
        wsT_b = sb("wsT_b", [128, 4, 128], BF16)
        diagK = sb("diagK", [128, 128, 32], BF16)
        ZS = sb("ZS", [128, 4, 4, 520], BF16)
        maskf = sb("maskf", [128, 32], F32)
        vecP = sb("vecP_s", [128, VP_N], F32)
        vecB = sb("vecB_s", [128, VB_N], F32)
        wbuf = sb("wbuf", [128, NW, 8, 512], BF16)
        ps = [es.enter_context(nc.psum_tensor("ps%d" % i, [128, 512], F32)) for i in range(8)]
        if YC_ALIAS:
            actT_f = actT.bitcast(F32)

            def yc_blk(j):
                return actT_f[:, 8 + 2 * j:10 + 2 * j, :].rearrange("p a b -> p (a b)")

            def yc_tile(j, i):
                return actT_f[:, 8 + 2 * j + i // 2, (i % 2) * 128:(i % 2) * 128 + 128]
        else:
            def yc_blk(j):
                return yc[:, j, :]

            def yc_tile(j, i):
                return yc[:, j, i * 128:(i + 1) * 128]

        tk = Trk(nc, es)
        s_cast = {k: tk.new_dma_sem("s_cast_" + k) for k in "abcde"}
        s_const = [tk.new_dma_sem("s_const%d" % i) for i in range(5)]
        s_zs = tk.new_dma_sem("s_zs")
        s_xin = [tk.new_dma_sem("s_xin%d" % i) for i in range(2)]
        s_xt = [tk.new_dma_sem("s_xt%d" % i) for i in range(4)]
        s_y = [tk.new_dma_sem("s_y%d" % i) for i in range(4)]
        s_y2 = [tk.new_dma_sem("s_yb%d" % i) for i in range(4)]
        s_w = [tk.new_dma_sem("s_w%d" % i) for i in range(NW)]

        free_banks = list(range(8))

        def balloc():
            assert free_banks, "PSUM banks exhausted at emission"
            return free_banks.pop(0)

        def bfree(b):
            free_banks.append(b)

        def psb(b):
            return ps[b][:]

        ps16 = [p.bitcast(BF16) for p in ps]

        def psb16(b):
            return ps16[b][:]

        F_BN, F_MV, F_A, F_Y, F_T, F_SS = 0, 96, 128, 144, 160, 176

        def chain(slot, src_ap, scale, n, c0, reads, tag):
            res = "s%dch%s" % (slot, tag)
            a = st[:, slot, F_A + c0:F_A + c0 + n]
            y = st[:, slot, F_Y + c0:F_Y + c0 + n]
            t = st[:, slot, F_T + c0:F_T + c0 + n]
            tk.op("dve", lambda e: e.tensor_scalar(out=a, in0=src_ap, scalar1=scale, scalar2=EPS,
                                                   op0=ALU.mult, op1=ALU.add), reads=list(reads) + [res], writes=[res])
            tk.op("dve", lambda e: e.tensor_scalar(out=y.bitcast(I32), in0=a.bitcast(I32), scalar1=-0.5,
                                                   scalar2=MAGIC, op0=ALU.mult, op1=ALU.add),
                  reads=[res], writes=[res])
            for _ in range(3):
                tk.op("dve", lambda e: e.scalar_tensor_tensor(out=t, in0=y, scalar=-0.5, in1=y,
                                                              op0=ALU.mult, op1=ALU.mult),
                      reads=[res], writes=[res])
                tk.op("dve", lambda e: e.tensor_tensor(out=t, in0=t, in1=a, op=ALU.mult),
                      reads=[res], writes=[res])
                tk.op("dve", lambda e: e.scalar_tensor_tensor(out=y, in0=t, scalar=1.5, in1=y,
                                                              op0=ALU.add, op1=ALU.mult),
                      reads=[res], writes=[res])
            return y, res

        s_cast_win = [tk.new_dma_sem("s_cast_win%d" % c) for c in range(4)]
        for c in range(4):
            for hb in range(2):
                r0 = hb * 512
                tk.dma("pool", lambda e, r0=r0, c=c: e.dma_start(
                    out=win_b[r0:r0 + 512, c * 512:(c + 1) * 512], in_=w_in_d[r0:r0 + 512, c * 512:(c + 1) * 512]),
                    s_cast_win[c], writes=["wscr_a%d_%d" % (c, hb)])
        for rb in range(8):
            r0 = rb * 128
            tk.dma("pool", lambda e, r0=r0: e.dma_start(out=wout_b[r0:r0 + 128, :], in_=w_out_d[r0:r0 + 128, :]),
                   s_cast["b"], writes=["wscr%d_b" % rb])
        for rb in range(8):
            r0 = rb * 128
            tk.dma("pool", lambda e, r0=r0: e.dma_start(
                out=wgu_b[r0:r0 + 128, :, 0:256],
                in_=w_gu_d[r0:r0 + 128, 0:DFF].rearrange("p (c f) -> p c f", f=256)),
                s_cast["c"], writes=["wscr%d_c" % rb])
            tk.dma("pool", lambda e, r0=r0: e.dma_start(
                out=wgu_b[r0:r0 + 128, :, 256:512],
                in_=w_gu_d[r0:r0 + 128, DFF:2 * DFF].rearrange("p (c f) -> p c f", f=256)),
                s_cast["d"], writes=["wscr%d_d" % rb])
        for rb in range(NFB):
            r0 = rb * 128
            tk.dma("pool", lambda e, r0=r0: e.dma_start(out=wdn_b[r0:r0 + 128, :], in_=w_dn_d[r0:r0 + 128, :]),
                   s_cast["e"], writes=["wscr%d_e" % rb])
        WSCR = {"win0": ["wscr_a0_0", "wscr_a0_1"], "win1": ["wscr_a1_0", "wscr_a1_1"],
                "win2": ["wscr_a2_0", "wscr_a2_1"], "win3": ["wscr_a3_0", "wscr_a3_1"],
                "wout": ["wscr%d_b" % i for i in range(8)],
                "wgu": ["wscr%d_c" % i for i in range(8)] + ["wscr%d_d" % i for i in range(8)],
                "wdn": ["wscr%d_e" % i for i in range(NFB)]}

        tk.dma("sp", lambda e: e.dma_start(out=vecP[:], in_=vecP_d), s_const[0], writes=["vecP"])
        tk.dma("sp", lambda e: e.dma_start(out=vecB[:], in_=vecB_d), s_const[1], writes=["vecB"])
        tk.dma("sp", lambda e: e.dma_start(out=identf[:], in_=ident_d), s_const[2], writes=["identf"])
        tk.dma("sp", lambda e: e.dma_start(out=ob[:, 0, :], in_=wsT_d), s_const[3], writes=["ob0"])
        tk.op("dve", lambda e: e.tensor_copy(out=identb[:], in_=identf[:]), reads=["identf"], writes=["identb"])
        tk.op("dve", lambda e: e.tensor_copy(out=wsT_b[:].rearrange("p h q -> p (h q)"), in_=ob[:, 0, :]),
              reads=["ob0"], writes=["wsT_b"])
        tk.dma("sp", lambda e: e.dma_start(out=maskf[:], in_=mask_d), s_const[4], writes=["maskf"])
        tk.op("dve", lambda e: e.tensor_scalar(out=vecP[:, VP_CW:VP_CW + 128], in0=vecP[:, VP_CW:VP_CW + 128],
                                               scalar1=0.5, scalar2=None, op0=ALU.mult),
              reads=["vecP"], writes=["vecP"])
        for jk in range(128):
            def _f(e, jk=jk):
                return e.tensor_scalar(out=diagK[:, jk, :], in0=maskf[:], scalar1=vecP[:, VP_CW + jk:VP_CW + jk + 1],
                                       scalar2=None, op0=ALU.mult)
            tk.op("dve", _f, reads=["maskf", "vecP"], writes=["diag_jk%d" % jk])
        tk.op("dve", lambda e: e.memset(st[:], 0.0), reads=["diag_jk%d" % jk for jk in range(128)],
              writes=["diagall", "stall"])

        WIN4 = [("win", c) for c in range(4)]
        PLAN = list(WIN4) + [("wout", 0), ("wout", 1)]
        if NT > 1:
            PLAN += WIN4
        for m_ in range(NT):
            PLAN += [("wgu", c) for c in range(11)] + [("wdn", c) for c in range(6)]
            if m_ + 1 < NT:
                PLAN += [("wout", 0), ("wout", 1)]
            if m_ + 2 < NT:
                PLAN += WIN4
        wstate = {"next_issue": 0, "total": len(PLAN)}

        def issue_chunk():
            g = wstate["next_issue"]
            if g >= wstate["total"]:
                return
            wstate["next_issue"] += 1
            kind, c = PLAN[g]
            slot = g % NW
            res = "wslot%d" % slot
            if kind == "win":
                src = win_b.rearrange("(k p) f -> p k f", p=128)[:, :, c * 512:(c + 1) * 512]
                dst = wbuf[:, slot, :, :]
            elif kind == "wout":
                src = wout_b.rearrange("(k p) f -> p k f", p=128)[:, :, c * 512:(c + 1) * 512]
                dst = wbuf[:, slot, :, :]
            elif kind == "wgu":
                src = wgu_b.rearrange("(k p) c f -> p k c f", p=128)[:, :, c, :]
                dst = wbuf[:, slot, :, :]
            else:
                half, kg = c // 3, c % 3
                nk = 8 if kg < 2 else 6
                src = wdn_b.rearrange("(k p) f -> p k f", p=128)[:, kg * 8:kg * 8 + nk, half * 512:(half + 1) * 512]
                dst = wbuf[:, slot, 0:nk, :]
            wkey = ("win%d" % c) if kind == "win" else kind
            tk.dma("sp", lambda e: e.dma_start(out=dst, in_=src), s_w[slot], reads=WSCR[wkey], writes=[res])

        cons = {"n": 0}

        def next_chunk():
            g = cons["n"]
            cons["n"] += 1
            return g % NW, "wslot%d" % (g % NW)

        AT_P = ["aP0", "aP1", "aP2", "aP3"]
        AT_Q = ["aQ0", "aQ1", "aQ2", "aQ3"]

        def p1_s0(m, j):
            rows = 128 if j < 4 else 32
            xs = j % 2
            if j < 4:
                src = x_d[m * T + j * 128: m * T + (j + 1) * 128, :]
            else:
                src = xh_d[m * 32:(m + 1) * 32, :]
            tk.dma("sp", lambda e: e.dma_start(out=xin[0:rows, xs, :], in_=src), s_xin[xs], writes=["xin%d" % xs])

        def p1_s12(m, j):
            rows = 128 if j < 4 else 32
            xs = j % 2
            xres = "xin%d" % xs
            sres = "s0ss%d" % j
            tk.op("act", lambda e: e.activation(out=tb[0:rows, j % 4, :], in_=xin[0:rows, xs, :], func=AF.Square,
                                                accum_out=st[0:rows, 0, F_SS + j:F_SS + j + 1]),
                  reads=[xres, "stall"], writes=["tb%da" % (j % 4), "tb%db" % (j % 4), sres])
            y, cres = chain(0, st[:, 0, F_SS + j:F_SS + j + 1], 1.0 / D, 1, j, [sres], "j%d" % j)
            ts = j % 4
            tres = ["tb%da" % ts, "tb%db" % ts]
            tk.op("dve", lambda e: e.tensor_scalar(out=tb[0:rows, ts, :], in0=xin[0:rows, xs, :],
                                                   scalar1=y[0:rows, :], scalar2=None, op0=ALU.mult),
                  reads=[xres, cres], writes=tres)

        def phase1_nonpe(m, j):
            p1_s0(m, j)
            p1_s12(m, j)

        def phase1_pe(m, j):
            rows = 128 if j < 4 else 32
            ts = j % 4
            tres = ["tb%da" % ts, "tb%db" % ts]
            b = balloc()
            bres = "bank%d" % b

            def _tr(e):
                ins = None
                for k in range(8):
                    ins = e.transpose(psb16(b)[:, k * 128:k * 128 + rows],
                                      tb[0:rows, ts, k * 128:(k + 1) * 128], identb[0:rows, 0:rows])
                return ins
            tk.op("pe", _tr, reads=tres + ["identb"], writes=[bres])
            ares = "aP%d" % j

            g_bc1 = bass.AP(vecP, VP_GPRE, [[VP_N, 128], [1, 8], [0, rows]])

            def _ev(e):
                return e.tensor_tensor(out=aT[:, 0, :, j * 128:j * 128 + rows],
                                       in0=psb16(b).rearrange("p (k t) -> p k t", k=8)[:, :, 0:rows],
                                       in1=g_bc1, op=ALU.mult)
            tk.op("dve", _ev, reads=[bres, "vecP"], writes=[ares])
            bfree(b)

        def A_mm_all(m, wu, wv):
            for which, (ws, wres) in enumerate((wu, wv)):
                for i in range(4):
                    b = balloc()

                    def _mm(e, ws=ws, b=b, i=i):
                        ins = None
                        for k in range(8):
                            ins = e.matmul(psb(b), lhsT=aT[:, 0, k, i * 128:(i + 1) * 128], rhs=wbuf[:, ws, k, :],
                                           start=(k == 0), stop=(k == 7))
                        return ins
                    tk.op("pe", _mm, reads=["aP%d" % i, wres], writes=["bank%d" % b])
                    dst = u_sb if which == 0 else v_sb
                    dres = ("u%d" if which == 0 else "oa%d") % i
                    tk.op("act", lambda e, b=b, dst=dst, i=i: e.activation(out=dst[:, i, :], in_=psb(b), func=AF.Gelu),
                          reads=["bank%d" % b], writes=[dres])
                    bfree(b)
                if which == 0:
                    issue_chunk()
            issue_chunk()

        def A_ln(m):
            for i in range(4):
                def _bn(e, i=i):
                    ins = None
                    for h in range(4):
                        ins = e.bn_stats(out=st[:, 1, F_BN + (i * 4 + h) * 6:F_BN + (i * 4 + h + 1) * 6],
                                         in_=v_sb[:, i, h * 128:(h + 1) * 128])
                    return ins
                tk.op("dve", _bn, reads=["oa%d" % i, "stall"], writes=["s1bn%d" % i])

                def _ag(e, i=i):
                    ins = None
                    for h in range(4):
                        ins = e.bn_aggr(out=st[:, 1, F_MV + (i * 4 + h) * 2:F_MV + (i * 4 + h + 1) * 2],
                                        in_=st[:, 1, F_BN + (i * 4 + h) * 6:F_BN + (i * 4 + h + 1) * 6])
                    return ins
                tk.op("dve", _ag, reads=["s1bn%d" % i], writes=["s1mv%d" % i])

            var_ap = st[:, 1, F_MV:F_MV + 32].rearrange("p (n c) -> p n c", c=2)[:, :, 1]
            y, cres = chain(1, var_ap, 1.0, 16, 0, ["s1mv%d" % i for i in range(4)], "")
            for i in range(4):
                def _nrm(e, i=i):
                    ins = None
                    for h in range(4):
                        c = i * 4 + h
                        ins = e.tensor_scalar(out=v_sb[:, i, h * 128:(h + 1) * 128],
                                              in0=v_sb[:, i, h * 128:(h + 1) * 128],
                                              scalar1=st[:, 1, F_MV + 2 * c:F_MV + 2 * c + 1], scalar2=y[:, c:c + 1],
                                              op0=ALU.subtract, op1=ALU.mult)
                    return ins
                tk.op("dve", _nrm, reads=["oa%d" % i, cres, "s1mv%d" % i], writes=["oa%d" % i])
                tk.op("pool", lambda e, i=i: e.tensor_tensor(out=v_sb[:, i, :], in0=v_sb[:, i, :],
                                                             in1=vecB[:, VB_ALG:VB_ALG + 512], op=ALU.mult),
                      reads=["oa%d" % i, "vecB"], writes=["oa%d" % i])
                tk.op("pool", lambda e, i=i: e.tensor_tensor(out=vn[:, i, :], in0=v_sb[:, i, :],
                                                             in1=vecB[:, VB_ALB:VB_ALB + 512], op=ALU.add),
                      reads=["oa%d" % i, "vecB"], writes=["vn%d" % i])

        def A_sp_a(m):
            for i in range(4):
                b = balloc()

                def _mm(e, b=b, i=i):
                    ins = None
                    for h in range(4):
                        ins = e.matmul(psb(b)[:, h * 128:(h + 1) * 128], lhsT=wsT_b[:, h, :],
                                       rhs=vn[:, i, h * 128:(h + 1) * 128], start=True, stop=True)
                    return ins
                tk.op("pe", _mm, reads=["vn%d" % i, "wsT_b"], writes=["bank%d" % b])

                def _oa(e, b=b, i=i):
                    ins = None
                    for h in range(4):
                        ins = e.scalar_tensor_tensor(out=oa[:, i, h * 128:(h + 1) * 128],
                                                     in0=psb(b)[:, h * 128:(h + 1) * 128],
                                                     scalar=vecP[:, VP_SPB + h:VP_SPB + h + 1],
                                                     in1=u_sb[:, i, h * 128:(h + 1) * 128],
                                                     op0=ALU.add, op1=ALU.mult)
                    return ins
                tk.op("dve", _oa, reads=["bank%d" % b, "u%d" % i, "vecP"], writes=["oa%d" % i])
                bfree(b)
                tk.op("act", lambda e, i=i: e.activation(out=tb[:, i, 0:512], in_=oa[:, i, :], func=AF.Square,
                                                         accum_out=st[:, 2, F_SS + i:F_SS + i + 1]),
                      reads=["oa%d" % i, "stall"], writes=["tb%da" % i, "s2ss%d" % i])

        def A_sp_b(m):
            y, cres = chain(2, st[:, 2, F_SS:F_SS + 4], 1.0 / 512, 4, 0, ["s2ss%d" % i for i in range(4)], "")
            for i in range(4):
                tk.op("dve", lambda e, i=i: e.tensor_scalar(out=tb[:, i, 0:512], in0=oa[:, i, :],
                                                            scalar1=y[:, i:i + 1], scalar2=None, op0=ALU.mult),
                      reads=["oa%d" % i, cres], writes=["tb%da" % i])

        def T_half(i, half):
            b = balloc()
            c0 = half * 512
            tres = "tb%d%s" % (i, "ab"[half])

            def _tr(e):
                ins = None
                for k in range(4):
                    ins = e.transpose(psb16(b)[:, k * 128:(k + 1) * 128],
                                      tb[:, i, c0 + k * 128:c0 + (k + 1) * 128], identb[:])
                return ins
            tk.op("pe", _tr, reads=[tres, "identb"], writes=["bank%d" % b])

            def _ev(e):
                ins = None
                for k in range(4):
                    kk = half * 4 + k
                    ins = e.activation(out=mixT[:, kk, i * 128:(i + 1) * 128],
                                       in_=psb16(b)[:, k * 128:(k + 1) * 128], func=AF.Copy,
                                       scale=vecP[:, VP_GRP + kk:VP_GRP + kk + 1])
                return ins
            tk.op("act", _ev, reads=["bank%d" % b, "vecP"], writes=["mixT%d_%d" % (i, half)])
            bfree(b)

        def B_proj(m, wval, wgate, mid_hook=None):
            (sv_, rv), (sg_, rg) = wval, wgate
            bh = balloc()

            def _mh(e):
                ins = None
                for j in range(4):
                    for k in range(8):
                        ins = e.matmul(psb(bh)[:, j * 32:(j + 1) * 32], lhsT=wbuf[:, sv_, k, j * 128:(j + 1) * 128],
                                       rhs=aT[:, 0, k, T:T + 32], start=(k == 0), stop=(k == 7))
                for j in range(4):
                    for k in range(8):
                        ins = e.matmul(psb(bh)[:, 128 + j * 32:128 + (j + 1) * 32],
                                       lhsT=wbuf[:, sg_, k, j * 128:(j + 1) * 128],
                                       rhs=aT[:, 0, k, T:T + 32], start=(k == 0), stop=(k == 7))
                return ins
            tk.op("pe", _mh, reads=["aP4", rv, rg], writes=["bank%d" % bh])
            tk.op("act", lambda e: e.activation(out=sig[:, 0, 0:128], in_=psb(bh)[:, 128:256],
                                                func=AF.Tanh, scale=0.5),
                  reads=["bank%d" % bh], writes=["sig0"])
            for side in range(2):
                def _zh(e, side=side):
                    dst = zbuf[:, :, 0:16] if side == 0 else zbuf[:, :, T + 16:T + 32]
                    a = sig[:, 0, 0:128].rearrange("p (j t) -> p j t", j=4)[:, :, side * 16:side * 16 + 16]
                    bb = psb(bh)[:, 0:128].rearrange("p (j t) -> p j t", j=4)[:, :, side * 16:side * 16 + 16]
                    return e.scalar_tensor_tensor(out=dst, in0=a, scalar=1.0, in1=bb, op0=ALU.add, op1=ALU.mult)
                tk.op("dve", _zh, reads=["bank%d" % bh, "sig0"], writes=["zh%d" % side])
            bfree(bh)
            for j in range(4):
                bv = balloc()
                bg = balloc()

                def _mv(e, bv=bv, j=j):
                    ins = None
                    for k in range(8):
                        ins = e.matmul(psb(bv), lhsT=wbuf[:, sv_, k, j * 128:(j + 1) * 128], rhs=aT[:, 0, k, 0:T],
                                       start=(k == 0), stop=(k == 7))
                    return ins

                def _mg(e, bg=bg, j=j):
                    ins = None
                    for k in range(8):
                        ins = e.matmul(psb(bg), lhsT=wbuf[:, sg_, k, j * 128:(j + 1) * 128], rhs=aT[:, 0, k, 0:T],
                                       start=(k == 0), stop=(k == 7))
                    return ins
                tk.op("pe", _mg, reads=AT_P + [rg], writes=["bank%d" % bg])
                tk.op("pe", _mv, reads=AT_P + [rv], writes=["bank%d" % bv])
                ss = (j + 1) % 2
                tk.op("act", lambda e, bg=bg, ss=ss: e.activation(out=sig[:, ss, :], in_=psb(bg), func=AF.Tanh,
                                                                  scale=0.5),
                      reads=["bank%d" % bg], writes=["sig%d" % ss])
                bfree(bg)
                tk.op("dve", lambda e, bv=bv, ss=ss, j=j: e.scalar_tensor_tensor(
                    out=zbuf[:, j, 16:16 + T], in0=sig[:, ss, :], scalar=1.0, in1=psb(bv),
                    op0=ALU.add, op1=ALU.mult), reads=["bank%d" % bv, "sig%d" % ss], writes=["z%d" % j])
                bfree(bv)
                if j == 1 and mid_hook is not None:
                    mid_hook()

        ZS_RES = ["zs%d_%d" % (g, jj) for g in range(4) for jj in range(4)]

        def B_shift(m):
            for g in range(4):
                for jj in range(4):
                    tk.dma("pool", lambda e, g=g, jj=jj: e.dma_start(
                        out=ZS[32 * jj:32 * jj + 32, g, :, :], in_=zbuf[32 * g:32 * g + 32, :, 8 * jj:8 * jj + 520]),
                        s_zs, reads=["z0", "z1", "z2", "z3", "zh0", "zh1"], writes=["zs%d_%d" % (g, jj)])

        def B_conv(m):
            for j in range(4):
                b = balloc()

                def _cv(e, b=b, j=j):
                    ins = None
                    for mm in range(8):
                        for g in range(4):
                            ins = e.matmul(ps[b][32 * g:32 * g + 32, :], lhsT=diagK[:, j * 32 + mm * 4 + g, :],
                                           rhs=ZS[:, g, j, mm + 1:mm + 1 + T], start=(mm == 0), stop=(mm == 7),
                                           tile_position=(0, 32 * g))
                    return ins
                tk.op("pe", _cv, reads=ZS_RES + ["diagall"], writes=["bank%d" % b])
                tk.op("act", lambda e, b=b, j=j: e.activation(out=yc_blk(j), in_=psb(b), func=AF.Identity,
                                                              bias=vecP[:, VP_CB + j:VP_CB + j + 1]),
                      reads=["bank%d" % b, "vecP"], writes=["yc%d" % j])
                bfree(b)

        def B_tok_a(m):
            banks = []
            for i in range(4):
                b = balloc()
                banks.append(b)

                def _tr(e, b=b, i=i):
                    ins = None
                    for j in range(4):
                        ins = e.transpose(psb(b)[:, j * 128:(j + 1) * 128], yc_tile(j, i), identf[:])
                    return ins
                tk.op("pe", _tr, reads=["yc0", "yc1", "yc2", "yc3", "identf"], writes=["bank%d" % b])

                def _bn(e, b=b, i=i):
                    ins = None
                    for h in range(4):
                        ins = e.bn_stats(out=st[:, 3, F_BN + (i * 4 + h) * 6:F_BN + (i * 4 + h + 1) * 6],
                                         in_=psb(b)[:, h * 128:(h + 1) * 128])
                    return ins
                tk.op("dve", _bn, reads=["bank%d" % b, "stall"], writes=["s3bn%d" % i])

                def _ag(e, i=i):
                    ins = None
                    for h in range(4):
                        ins = e.bn_aggr(out=st[:, 3, F_MV + (i * 4 + h) * 2:F_MV + (i * 4 + h + 1) * 2],
                                        in_=st[:, 3, F_BN + (i * 4 + h) * 6:F_BN + (i * 4 + h + 1) * 6])
                    return ins
                tk.op("dve", _ag, reads=["s3bn%d" % i], writes=["s3mv%d" % i])
            var_ap = st[:, 3, F_MV:F_MV + 32].rearrange("p (n c) -> p n c", c=2)[:, :, 1]
            y, cres = chain(3, var_ap, 1.0, 16, 0, ["s3mv%d" % i for i in range(4)], "")
            for i in range(4):
                b = banks[i]

                def _nrm(e, b=b, i=i):
                    ins = None
                    for h in range(4):
                        c = i * 4 + h
                        ins = e.tensor_scalar(out=ob[:, i, h * 128:(h + 1) * 128], in0=psb(b)[:, h * 128:(h + 1) * 128],
                                              scalar1=st[:, 3, F_MV + 2 * c:F_MV + 2 * c + 1], scalar2=y[:, c:c + 1],
                                              op0=ALU.subtract, op1=ALU.mult)
                    return ins
                tk.op("dve", _nrm, reads=["bank%d" % b, cres, "s3mv%d" % i], writes=["ob%d" % i])
                bfree(b)
                geng = "pool"
                tk.op(geng, lambda e, i=i: e.tensor_tensor(out=ob[:, i, :], in0=ob[:, i, :],
                                                           in1=vecB[:, VB_BLG:VB_BLG + 512], op=ALU.mult),
                      reads=["ob%d" % i, "vecB"], writes=["ob%d" % i])
                tk.op(geng, lambda e, i=i: e.tensor_tensor(out=ob[:, i, :], in0=ob[:, i, :],
                                                           in1=vecB[:, VB_BLB:VB_BLB + 512], op=ALU.add),
                      reads=["ob%d" % i, "vecB"], writes=["ob%d" % i])

        def B_tok_b(m):
            for i in range(4):
                tk.op("act", lambda e, i=i: e.activation(out=ob[:, i, :], in_=ob[:, i, :], func=AF.Silu),
                      reads=["ob%d" % i], writes=["ob%d" % i])
                tk.op("act", lambda e, i=i: e.activation(out=tb[:, i, 512:1024], in_=ob[:, i, :], func=AF.Square,
                                                         accum_out=st[:, 4, F_SS + i:F_SS + i + 1]),
                      reads=["ob%d" % i, "stall"], writes=["tb%db" % i, "s4ss%d" % i])

        def B_tok_c(m):
            y2, c2 = chain(4, st[:, 4, F_SS:F_SS + 4], 1.0 / 512, 4, 0, ["s4ss%d" % i for i in range(4)], "")
            for i in range(4):
                tk.op("dve", lambda e, i=i: e.tensor_scalar(out=tb[:, i, 512:1024], in0=ob[:, i, :],
                                                            scalar1=y2[:, i:i + 1], scalar2=None, op0=ALU.mult),
                      reads=["ob%d" % i, c2], writes=["tb%db" % i])

        def x_reload(m):
            for i in range(4):
                tk.dma("act", lambda e, i=i: e.dma_start(out=xt[:, i, :],
                                                         in_=x_d[m * T + i * 128:m * T + (i + 1) * 128, :]),
                       s_xt[i], writes=["xt%d" % i])

        def wout_wave(m, tiles, w0, w1, wave):
            tmpbuf = ob if wave == 0 else yc
            tmpname = "ob%d" if wave == 0 else "yc%d"
            for i in tiles:
                for half, (ws, wres) in enumerate((w0, w1)):
                    b = balloc()
                    q = 2 * (i % 2) + half

                    def _mm(e, ws=ws, b=b, i=i):
                        ins = None
                        for k in range(8):
                            ins = e.matmul(psb(b), lhsT=mixT[:, k, i * 128:(i + 1) * 128], rhs=wbuf[:, ws, k, :],
                                           start=(k == 0), stop=(k == 7))
                        return ins
                    tk.op("pe", _mm, reads=["mixT%d_0" % i, "mixT%d_1" % i, wres], writes=["bank%d" % b])
                    tk.op("act", lambda e, b=b, q=q: e.activation(out=tmpbuf[:, q, :], in_=psb(b), func=AF.Copy),
                          reads=["bank%d" % b], writes=[tmpname % q])
                    bfree(b)
                    tk.op("act", lambda e, q=q, i=i, half=half: e.activation(
                        out=tb[:, i, half * 512:(half + 1) * 512], in_=tmpbuf[:, q, :], func=AF.Square,
                        accum_out=st[:, 5, F_SS + 2 * i + half:F_SS + 2 * i + half + 1]),
                        reads=[tmpname % q, "stall"], writes=["tb%d%s" % (i, "ab"[half]), "s5ss%d_%d" % (i, half)])
                    tk.op("pool", lambda e, q=q, half=half: e.tensor_tensor(
                        out=tmpbuf[:, q, :], in0=tmpbuf[:, q, :],
                        in1=vecB[:, VB_GPOST + half * 512:VB_GPOST + (half + 1) * 512], op=ALU.mult),
                        reads=[tmpname % q, "vecB"], writes=[tmpname % q])

        def wout_wave_b(m, tiles, wave):
            tmpbuf = ob if wave == 0 else yc
            tmpname = "ob%d" if wave == 0 else "yc%d"
            i0 = tiles[0]
            n = len(tiles)
            sres = ["s5ss%d_%d" % (i, h) for i in tiles for h in range(2)]
            sums = st[:, 5, F_SS + 8 + i0:F_SS + 8 + i0 + n]
            pair = st[:, 5, F_SS + 2 * i0:F_SS + 2 * i0 + 2 * n].rearrange("p (n c) -> p n c", c=2)
            tk.op("dve", lambda e: e.tensor_tensor(out=sums, in0=pair[:, :, 0], in1=pair[:, :, 1], op=ALU.add),
                  reads=sres, writes=["s5sum%d" % wave])
            y, cres = chain(5, sums, 1.0 / D, n, i0, ["s5sum%d" % wave], "w%d" % wave)
            for ii, i in enumerate(tiles):
                for half in range(2):
                    q = 2 * (i % 2) + half
                    tmp = tmpbuf[:, q, :]
                    tres = tmpname % q
                    tk.op("dve", lambda e, half=half, i=i, ii=ii, tmp=tmp: e.scalar_tensor_tensor(
                        out=xt[:, i, half * 512:(half + 1) * 512], in0=tmp, scalar=y[:, ii:ii + 1],
                        in1=xt[:, i, half * 512:(half + 1) * 512], op0=ALU.mult, op1=ALU.add),
                        reads=["xt%d" % i, tres, cres], writes=["xt%d" % i])
                tk.op("dve", lambda e, i=i: e.scalar_tensor_tensor(
                    out=tb[:, i, :], in0=xt[:, i, :], scalar=1.0, in1=xt[:, i, :], op0=ALU.mult, op1=ALU.mult,
                    accum_out=st[:, 6, F_SS + i:F_SS + i + 1]),
                    reads=["xt%d" % i, "stall"], writes=["s6ss%d" % i, "tb%da" % i, "tb%db" % i])
            y2, c2 = chain(6, st[:, 6, F_SS + i0:F_SS + i0 + n], 1.0 / D, n, i0, ["s6ss%d" % i for i in tiles],
                           "w%d" % wave)
            for ii, i in enumerate(tiles):
                tk.op("dve", lambda e, i=i, ii=ii: e.tensor_scalar(out=tb[:, i, :], in0=xt[:, i, :],
                                                                   scalar1=y2[:, ii:ii + 1], scalar2=None, op0=ALU.mult),
                      reads=["xt%d" % i, c2], writes=["tb%da" % i, "tb%db" % i])

        def hn2T(i):
            b = balloc()

            def _tr(e):
                ins = None
                for k in range(8):
                    ins = e.transpose(psb16(b)[:, k * 128:(k + 1) * 128], tb[:, i, k * 128:(k + 1) * 128], identb[:])
                return ins
            tk.op("pe", _tr, reads=["tb%da" % i, "tb%db" % i, "identb"], writes=["bank%d" % b])

            g_bc = bass.AP(vecP, VP_GPRE2, [[VP_N, 128], [1, 8], [0, 128]])

            def _ev(e):
                return e.tensor_tensor(out=aT[:, 1, :, i * 128:(i + 1) * 128],
                                       in0=psb16(b).rearrange("p (k t) -> p k t", k=8), in1=g_bc, op=ALU.mult)
            tk.op("dve", _ev, reads=["bank%d" % b, "vecP"], writes=["aQ%d" % i])
            bfree(b)

        def gu_chunk(m, c, w):
            ws, wres = w
            for jj in range(2):
                j = 2 * c + jj
                bg = balloc()
                bu = balloc()

                def _mg(e, bg=bg, jj=jj):
                    ins = None
                    for k in range(8):
                        ins = e.matmul(psb(bg), lhsT=wbuf[:, ws, k, jj * 128:(jj + 1) * 128], rhs=aT[:, 1, k, 0:T],
                                       start=(k == 0), stop=(k == 7))
                    return ins

                def _mu(e, bu=bu, jj=jj):
                    ins = None
                    for k in range(8):
                        ins = e.matmul(psb(bu), lhsT=wbuf[:, ws, k, 256 + jj * 128:256 + (jj + 1) * 128],
                                       rhs=aT[:, 1, k, 0:T], start=(k == 0), stop=(k == 7))
                    return ins
                tk.op("pe", _mg, reads=AT_Q + [wres], writes=["bank%d" % bg])
                tk.op("pe", _mu, reads=AT_Q + [wres], writes=["bank%d" % bu])
                ss = j % 2
                tk.op("act", lambda e, bg=bg, ss=ss: e.activation(out=sil[:, ss, :], in_=psb(bg), func=AF.Silu),
                      reads=["bank%d" % bg], writes=["sig%d" % ss])
                bfree(bg)
                extra = []
                tk.op("dve", lambda e, bu=bu, ss=ss, j=j: e.tensor_tensor(out=actT[:, j, :], in0=sil[:, ss, :],
                                                                          in1=psb(bu), op=ALU.mult),
                      reads=["bank%d" % bu, "sig%d" % ss], writes=["actT%d" % j] + extra)
                bfree(bu)

        def down_half(m, half, wch, hook=None):
            banks = [balloc() for _ in range(4)]
            for kg in range(3):
                ws, wres = wch[kg]
                nk = 8 if kg < 2 else 6
                for i in range(4):
                    def _mm(e, ws=ws, kg=kg, nk=nk, i=i, b=banks[i]):
                        ins = None
                        for kk in range(nk):
                            k = kg * 8 + kk
                            ins = e.matmul(psb(b), lhsT=actT[:, k, i * 128:(i + 1) * 128], rhs=wbuf[:, ws, kk, :],
                                           start=(k == 0), stop=(k == NFB - 1))
                        return ins
                    tk.op("pe", _mm, reads=["actT%d" % k for k in range(kg * 8, kg * 8 + nk)] + [wres],
                          writes=["bank%d" % banks[i]])
                issue_chunk()
                if hook is not None:
                    hook(half * 3 + kg)
            return banks

        pending_y = []

        def flush_y():
            for (m_, i) in pending_y:
                tk.dma("pool", lambda e, i=i, m_=m_: e.dma_start(
                    out=y_d[m_ * T + i * 128:m_ * T + (i + 1) * 128, 0:512], in_=oa[:, i, :]),
                    s_y[i], reads=["oa%d" % i])
                tk.dma("pool", lambda e, i=i, m_=m_: e.dma_start(
                    out=y_d[m_ * T + i * 128:m_ * T + (i + 1) * 128, 512:1024], in_=yc[:, i, :]),
                    s_y2[i], reads=["yc%d" % i])
            del pending_y[:]

        def mixer_front_all(m):
            A_sp_a(m)
            A_sp_b(m)
            B_conv(m)
            for i in range(4):
                T_half(i, 0)
            B_tok_a(m)
            B_tok_b(m)
            B_tok_c(m)

        def w_in_all(m):
            wu = next_chunk()
            wv = next_chunk()
            A_mm_all(m, wu, wv)
            wval = next_chunk()
            wgate = next_chunk()
            B_proj(m, wval, wgate)
            B_shift(m)
            issue_chunk()
            issue_chunk()
            A_ln(m)

        def wout_all(m):
            w0 = next_chunk()
            w1 = next_chunk()
            wout_wave(m, [0, 1], w0, w1, 0)
            wout_wave(m, [2, 3], w0, w1, 1)
            wout_wave_b(m, [0, 1], 0)
            wout_wave_b(m, [2, 3], 1)
            issue_chunk()
            issue_chunk()

        def wout_win_interleaved(m_out, m_in):
            w0 = next_chunk()
            w1 = next_chunk()
            wout_wave(m_out, [0, 1], w0, w1, 0)
            wout_wave(m_out, [2, 3], w0, w1, 1)
            issue_chunk()
            issue_chunk()
            wu = next_chunk()
            wv = next_chunk()
            A_mm_all(m_in, wu, wv)
            wout_wave_b(m_out, [0, 1], 0)
            wval = next_chunk()
            wgate = next_chunk()
            B_proj(m_in, wval, wgate,
                   mid_hook=lambda: (hn2T(0), hn2T(1), wout_wave_b(m_out, [2, 3], 1)))
            B_shift(m_in)
            issue_chunk()
            issue_chunk()
            for i in (2, 3):
                hn2T(i)
            A_ln(m_in)

        def ffn_down(m, hook):
            wch0 = [next_chunk() for _ in range(3)]
            banks0 = down_half(m, 0, wch0, hook)
            for i in range(4):
                b = banks0[i]
                tk.op("act", lambda e, b=b, i=i: e.activation(out=sig[:, i % 2, :], in_=psb(b), func=AF.Square,
                                                              accum_out=st[:, 7, F_SS + 2 * i:F_SS + 2 * i + 1]),
                      reads=["bank%d" % b, "stall"], writes=["sig%d" % (i % 2), "s7ss%d_0" % i])
                tk.op("act", lambda e, b=b, i=i: e.activation(out=oa[:, i, :], in_=psb(b), func=AF.Copy),
                      reads=["bank%d" % b], writes=["oa%d" % i])
                bfree(b)
                tk.op("dve", lambda e, i=i: e.tensor_tensor(out=oa[:, i, :], in0=oa[:, i, :],
                                                            in1=vecB[:, VB_GPOST2:VB_GPOST2 + 512], op=ALU.mult),
                      reads=["oa%d" % i, "vecB"], writes=["oa%d" % i])
            wch1 = [next_chunk() for _ in range(3)]
            banks1 = down_half(m, 1, wch1, hook)
            for i in range(4):
                b = banks1[i]
                tk.op("act", lambda e, b=b, i=i: e.activation(out=sig[:, i % 2, :], in_=psb(b), func=AF.Square,
                                                              accum_out=st[:, 7, F_SS + 2 * i + 1:F_SS + 2 * i + 2]),
                      reads=["bank%d" % b, "stall"], writes=["sig%d" % (i % 2), "s7ss%d_1" % i])
            sums = st[:, 7, F_SS + 8:F_SS + 12]
            pair = st[:, 7, F_SS:F_SS + 8].rearrange("p (n c) -> p n c", c=2)
            tk.op("dve", lambda e: e.tensor_tensor(out=sums, in0=pair[:, :, 0], in1=pair[:, :, 1], op=ALU.add),
                  reads=["s7ss%d_%d" % (i, h) for i in range(4) for h in range(2)], writes=["s7sum"])
            y, cres = chain(7, sums, 1.0 / D, 4, 0, ["s7sum"], "")
            for i in range(4):
                b = banks1[i]
                xres = "xt%d" % i
                tk.op("dve", lambda e, i=i: e.scalar_tensor_tensor(
                    out=oa[:, i, :], in0=oa[:, i, :], scalar=y[:, i:i + 1], in1=xt[:, i, 0:512],
                    op0=ALU.mult, op1=ALU.add), reads=[xres, "oa%d" % i, cres], writes=["oa%d" % i])
                tk.op("dve", lambda e, i=i, b=b: e.scalar_tensor_tensor(
                    out=yc[:, i, :], in0=psb(b), scalar=y[:, i:i + 1],
                    in1=vecB[:, VB_GPOST2 + 512:VB_GPOST2 + 1024],
                    op0=ALU.mult, op1=ALU.mult), reads=["bank%d" % b, cres, "vecB"], writes=["yc%d" % i])
                bfree(b)
                tk.op("pool", lambda e, i=i: e.tensor_tensor(out=yc[:, i, :], in0=yc[:, i, :],
                                                             in1=xt[:, i, 512:1024], op=ALU.add),
                      reads=[xres, "yc%d" % i], writes=["yc%d" % i])
                pending_y.append((m, i))

        p1_s0(0, 0)
        p1_s0(0, 1)
        for _ in range(NW):
            issue_chunk()
        for j in range(5):
            p1_s12(0, j)
            if j + 2 < 5:
                p1_s0(0, j + 2)
            phase1_pe(0, j)
        w_in_all(0)
        if NT > 1:
            for j in range(5):
                phase1_nonpe(1, j)
                phase1_pe(1, j)
        mixer_front_all(0)
        for i in range(4):
            T_half(i, 1)
        x_reload(0)
        wout_all(0)
        if NT > 1:
            w_in_all(1)
        for i in range(4):
            hn2T(i)

        for m in range(NT):
            nxt = m + 1 < NT
            nxt2 = m + 2 < NT
            for c in range(11):
                if nxt:
                    if c == 2:
                        B_conv(m + 1)
                    if c == 4:
                        A_sp_a(m + 1)
                        B_tok_a(m + 1)
                    if c == 7:
                        for i in range(4):
                            T_half(i, 0)
                    if c == 10:
                        for i in range(4):
                            T_half(i, 1)
                w = next_chunk()
                gu_chunk(m, c, w)
                if nxt:
                    if c == 5:
                        A_sp_b(m + 1)
                    if c == 6:
                        B_tok_b(m + 1)
                    if c == 7:
                        B_tok_c(m + 1)
                if nxt2:
                    if c == 8:
                        p1_s0(m + 2, 0)
                    if c == 9:
                        p1_s0(m + 2, 1)
                    if c == 10:
                        p1_s12(m + 2, 0)
                        p1_s0(m + 2, 2)
                issue_chunk()

            def hook(bd, m=m, nxt2=nxt2):
                if not nxt2:
                    return
                if 0 <= bd - 1 < 5:
                    phase1_pe(m + 2, bd - 1)
                if bd + 1 < 5:
                    p1_s12(m + 2, bd + 1)
                if bd + 3 < 5:
                    p1_s0(m + 2, bd + 3)
            ffn_down(m, hook)
            flush_y()
            if nxt:
                x_reload(m + 1)
                if nxt2:
                    wout_win_interleaved(m + 1, m + 2)
                else:
                    wout_all(m + 1)
                    for i in range(4):
                        hn2T(i)
        flush_y()
        if debug:
            pass
        tk.final_wait("pool", s_y + s_y2)

        with nc.Block() as block:
            @block.sync
            def _(e):
                tk.replay("sp", e)

            @block.gpsimd
            def _(e):
                tk.replay("pool", e)

            @block.scalar
            def _(e):
                tk.replay("act", e)

            @block.vector
            def _(e):
                tk.replay("dve", e)

            @block.tensor
            def _(e):
                tk.replay("pe", e)
    return nc


def _prep_shared(mix_pre_g, a_ln_g, a_ln_b, a_sp_w, a_sp_b, b_conv_w, b_conv_b, b_ln_g, b_ln_b, grp_g,
                 mix_post_g, ffn_pre_g, ffn_post_g):
    f = np.float32
    vecP = np.zeros((128, VP_N), f)
    vecP[:, VP_CB:VP_CB + 4] = np.asarray(b_conv_b[0], f).reshape(4, 128).T
    vecP[:, VP_SPB:VP_SPB + 4] = np.asarray(a_sp_b[0], f).T
    cw = np.concatenate([np.asarray(b_conv_w[0], f), np.zeros((1, 512), f)], axis=0)
    ck = cw.reshape(4, 8, 4, 4, 32)
    vecP[:, VP_CW:VP_CW + 128] = ck.transpose(0, 4, 2, 1, 3).reshape(128, 128)
    vecP[:, VP_GPRE:VP_GPRE + 8] = np.asarray(mix_pre_g[0], f).reshape(8, 128).T
    vecP[:, VP_GPRE2:VP_GPRE2 + 8] = np.asarray(ffn_pre_g[0], f).reshape(8, 128).T
    vecP[:, VP_GRP:VP_GRP + 8] = np.asarray(grp_g[0], f).reshape(8, 128).T
    vb = np.concatenate([np.asarray(v[0], f).reshape(-1) for v in
                         (mix_post_g, ffn_post_g, a_ln_g, a_ln_b, b_ln_g, b_ln_b)])
    vecB = np.ascontiguousarray(np.broadcast_to(vb[None, :], (128, VB_N)))
    wsT = np.ascontiguousarray(np.asarray(a_sp_w[0], f).transpose(2, 0, 1)).reshape(128, 512)
    ident = np.eye(128, dtype=f)
    return vecP, vecB, wsT, ident


def _halo(X, t0):
    h = np.zeros((32, X.shape[1]), X.dtype)
    if t0 % SEQ != 0:
        h[0:16] = X[t0 - 16:t0]
    if (t0 + T) % SEQ != 0:
        h[16:32] = X[t0 + T:t0 + T + 16]
    return h


_NC_CACHE = {}
MASK = np.ascontiguousarray(np.tile(np.eye(32, dtype=np.float32), (4, 1)))


def kernel(x_prompt, x_sample, mix_pre_g, w_in, a_ln_g, a_ln_b, a_sp_w, a_sp_b, b_conv_w, b_conv_b,
           b_ln_g, b_ln_b, grp_g, w_out, mix_post_g, ffn_pre_g, w_gate_up, w_down, ffn_post_g):
    f = np.float32
    xp = np.asarray(x_prompt, f)
    xs = np.asarray(x_sample, f)
    X = np.concatenate([xp.reshape(-1, D), xs.reshape(-1, D)], axis=0)
    ntot = X.shape[0]
    assert ntot == NCORES * TOK_PER_CORE
    vecP, vecB, wsT, ident = _prep_shared(mix_pre_g, a_ln_g, a_ln_b, a_sp_w, a_sp_b, b_conv_w, b_conv_b,
                                          b_ln_g, b_ln_b, grp_g, mix_post_g, ffn_pre_g, ffn_post_g)
    NT = NT_FULL
    if NT not in _NC_CACHE:
        _NC_CACHE[NT] = build(NT)
    nc = _NC_CACHE[NT]
    w_in_ = np.ascontiguousarray(np.asarray(w_in, f)[0])
    w_out_ = np.ascontiguousarray(np.asarray(w_out, f)[0])
    w_gu_ = np.ascontiguousarray(np.asarray(w_gate_up, f)[0])
    w_dn_ = np.ascontiguousarray(np.asarray(w_down, f)[0])
    in_maps = []
    for c in range(NCORES):
        t0 = c * TOK_PER_CORE
        xh = np.concatenate([_halo(X, t0 + m * T) for m in range(NT)], axis=0)
        in_maps.append({
            "x": np.ascontiguousarray(X[t0:t0 + TOK_PER_CORE]), "xh": xh,
            "w_in": w_in_, "w_out": w_out_, "w_gu": w_gu_, "w_dn": w_dn_,
            "vecP": vecP, "vecB": vecB, "wsT": wsT, "ident": ident, "mask": MASK,
        })
    res = run_bass_kernel_spmd(nc, in_maps, core_ids=list(range(NCORES)))
    Y = np.concatenate([np.asarray(r["y"], f) for r in res.results], axis=0)
    npr = xp.shape[0] * xp.shape[1]
    y_prompt = Y[:npr].reshape(xp.shape)
    y_sample = Y[npr:].reshape(xs.shape)
    return (y_prompt, y_sample)
```
